# Optimizing a Trainium2 kernel written in Bass

```python
import math
import jax, jax.numpy as jnp
from jax import lax
import numpy as np

D_MODEL = 1024
BATCH = 4
SEQ = 4096
DEPTH = 1
DEC_BATCH = 32
DEC_SEQ = 1
PAST_LEN = 8192
PAGE_SIZE = 128

D_MIX = D_MODEL
D_A = D_MIX // 2
D_B = D_MIX - D_A
H_A = 4
DV_A = D_A // H_A
DK_A = DV_A // 2
GATE_RANK = 16
GATE_TEMP = 16.0
GLA_CHUNK = 64
H_B = 8
HD_B = D_B // H_B
DILATED_GROUPS = ((128, 1), (512, 4), (2048, 16))
WINDOW = 2048
ATTN_BLOCK = 128
N_BUCKETS = 32
MAX_EXACT = 16
MAX_DISTANCE = 2048
LN_EPS = 1e-5
RMS_EPS = 1e-6
DEEPNORM_ALPHA = (2.0 * DEPTH) ** 0.25
DEEPNORM_BETA = (8.0 * DEPTH) ** -0.25
SPLITS = (H_A * DK_A, H_A * DK_A, D_A, GATE_RANK, D_A, D_B, D_B, D_B, D_B)
D_IN = 256 + 256 + 512 + 16 + 512 + 512 + 512 + 512 + 512

kernel_name = "hymba_gla_dilated_swa_deepnorm_step"

F32 = jnp.float32


def layer_norm(x, g, b):
    xf = x.astype(F32)
    mu = jnp.mean(xf, axis=-1, keepdims=True)
    var = jnp.mean(jnp.square(xf - mu), axis=-1, keepdims=True)
    return ((xf - mu) * lax.rsqrt(var + LN_EPS) * g.astype(F32) + b.astype(F32)).astype(x.dtype)


def t5_bucket(dist):
    d = jnp.maximum(dist.astype(F32), 1.0)
    log_b = MAX_EXACT + jnp.log(d / MAX_EXACT) / math.log(MAX_DISTANCE / MAX_EXACT) * (N_BUCKETS - MAX_EXACT)
    log_b = jnp.minimum(log_b.astype(jnp.int32), N_BUCKETS - 1)
    return jnp.where(dist < MAX_EXACT, dist, log_b)


def gla_chunked(q, k, v, log_a, s0, chunk):
    B, T = q.shape[:2]
    n = T // chunk

    def to_chunks(a):
        return a.astype(F32).reshape(B, n, chunk, H_A, a.shape[-1]).transpose(1, 0, 3, 2, 4)

    causal = jnp.tril(jnp.ones((chunk, chunk), dtype=bool))

    def step(state, inp):
        qc, kc, vc, gc = inp
        b = jnp.cumsum(gc, axis=2)
        diff = b[:, :, :, None, :] - b[:, :, None, :, :]
        decay = jnp.exp(jnp.where(causal[None, None, :, :, None], diff, -jnp.inf))
        att = jnp.einsum('bhtd,bhsd,bhtsd->bhts', qc, kc, decay)
        o = (jnp.einsum('bhts,bhsv->bhtv', att, vc)
             + jnp.einsum('bhtd,bhdv->bhtv', qc * jnp.exp(b), state))
        b_last = b[:, :, -1:, :]
        new_state = (jnp.exp(b_last[:, :, 0, :])[..., None] * state
                     + jnp.einsum('bhsd,bhsv->bhdv', kc * jnp.exp(b_last - b), vc))
        return new_state, o

    s_final, o = lax.scan(step, s0.astype(F32), (to_chunks(q), to_chunks(k), to_chunks(v), to_chunks(log_a)))
    return o.transpose(1, 0, 3, 2, 4).reshape(B, T, H_A, DV_A), s_final


def dilated_attention(q, k_all, v_all, q_idx, rel_bias):
    L = k_all.shape[1]
    scale = HD_B ** -0.5
    qf = q.astype(F32)
    ms, ss, outs = [], [], []
    for window, dil in DILATED_GROUPS:
        dist = jnp.arange(window // dil + 1, dtype=jnp.int32) * dil
        idx = q_idx[:, None] - dist[None, :]
        valid = idx >= 0
        idx_c = jnp.clip(idx, 0, L - 1)
        kg = jnp.take(k_all, idx_c, axis=1).astype(F32)
        vg = jnp.take(v_all, idx_c, axis=1).astype(F32)
        bias = rel_bias[t5_bucket(dist)].astype(F32).T
        logits = jnp.einsum('bqhd,bqjhd->bhqj', qf, kg) * scale + bias[None, :, None, :]
        logits = jnp.where(valid[None, None], logits, -jnp.inf)
        m = jnp.max(logits, axis=-1, keepdims=True)
        p = jnp.exp(logits - m)
        s = jnp.sum(p, axis=-1, keepdims=True)
        outs.append(jnp.einsum('bhqj,bqjhd->bhqd', p, vg) / s)
        ms.append(m)
        ss.append(s)
    m_all = jnp.maximum(jnp.maximum(ms[0], ms[1]), ms[2])
    wts = [s * jnp.exp(m - m_all) for m, s in zip(ms, ss)]
    out = (wts[0] * outs[0] + wts[1] * outs[1] + wts[2] * outs[2]) / (wts[0] + wts[1] + wts[2])
    return out.transpose(0, 2, 1, 3)


def prompt_dilated(q, k, v, rel_bias):
    B, S = q.shape[:2]
    nb = S // ATTN_BLOCK
    qb = q.reshape(B, nb, ATTN_BLOCK, H_B, HD_B).transpose(1, 0, 2, 3, 4)

    def body(args):
        blk, q_blk = args
        q_idx = blk * ATTN_BLOCK + jnp.arange(ATTN_BLOCK, dtype=jnp.int32)
        return dilated_attention(q_blk, k, v, q_idx, rel_bias)

    out = lax.map(body, (jnp.arange(nb, dtype=jnp.int32), qb))
    return out.transpose(1, 0, 2, 3, 4).reshape(B, S, H_B, HD_B)


def mixer_layer(x, s0, k_past, v_past, w_in, w_alpha2, b_alpha, gla_norm_g, w_out, ln_g, ln_b, rel_bias):
    B, T, _ = x.shape
    h = jnp.einsum('btd,de->bte', x, w_in)
    split_points = [int(c) for c in np.cumsum(SPLITS)[:-1]]
    qa, ka, va, za, ga, qb, kb, vb, gb = jnp.split(h, split_points, axis=-1)

    log_a = jax.nn.log_sigmoid((jnp.einsum('btr,rk->btk', za, w_alpha2) + b_alpha).astype(F32)) / GATE_TEMP
    qa = qa.reshape(B, T, H_A, DK_A) * (DK_A ** -0.5)
    ka = ka.reshape(B, T, H_A, DK_A)
    va = va.reshape(B, T, H_A, DV_A)
    log_a = log_a.reshape(B, T, H_A, DK_A)
    chunk = GLA_CHUNK if T % GLA_CHUNK == 0 else T
    o_a, s_new = gla_chunked(qa, ka, va, log_a, s0, chunk)
    o_a = o_a * lax.rsqrt(jnp.mean(jnp.square(o_a), axis=-1, keepdims=True) + RMS_EPS) * gla_norm_g.astype(F32)
    o_a = o_a.reshape(B, T, D_A)

    qb = qb.reshape(B, T, H_B, HD_B)
    kb = kb.reshape(B, T, H_B, HD_B)
    vb = vb.reshape(B, T, H_B, HD_B)
    if k_past is None:
        o_b = prompt_dilated(qb, kb, vb, rel_bias)
        keep = min(WINDOW, T)
        k_rows, v_rows = kb[:, T - keep:], vb[:, T - keep:]
    else:
        w_past = k_past.shape[1]
        k_all = jnp.concatenate([k_past, kb], axis=1)
        v_all = jnp.concatenate([v_past, vb], axis=1)
        q_idx = w_past + jnp.arange(T, dtype=jnp.int32)
        o_b = dilated_attention(qb, k_all, v_all, q_idx, rel_bias)
        k_rows, v_rows = kb, vb
    o_b = o_b.reshape(B, T, D_B)

    mix = jnp.concatenate([o_a * jax.nn.silu(ga.astype(F32)), o_b * jax.nn.silu(gb.astype(F32))], axis=-1)
    y = jnp.einsum('bte,ed->btd', mix.astype(x.dtype), w_out)
    x_new = layer_norm(DEEPNORM_ALPHA * x + y, ln_g, ln_b)
    return x_new, s_new, k_rows, v_rows


def setup_inputs(seed: int = 0) -> dict:
    key = jax.random.key(seed)
    ks = jax.random.split(key, 14)
    w_past = min(WINDOW, PAST_LEN)
    x_prompt = jax.random.normal(ks[0], (BATCH, SEQ, D_MODEL), F32)
    x_sample = jax.random.normal(ks[1], (DEC_BATCH, DEC_SEQ, D_MODEL), F32)
    state_gla = 0.5 * jax.random.normal(ks[2], (DEPTH, DEC_BATCH, H_A, DK_A, DV_A), F32)
    cache_k_win = jax.random.normal(ks[3], (DEPTH, DEC_BATCH, w_past, H_B, HD_B), F32)
    cache_v_win = jax.random.normal(ks[4], (DEPTH, DEC_BATCH, w_past, H_B, HD_B), F32)
    col_scale = np.ones((D_IN,), np.float32)
    offs = np.cumsum((0,) + SPLITS)
    col_scale[offs[2]:offs[3]] = DEEPNORM_BETA
    col_scale[offs[7]:offs[8]] = DEEPNORM_BETA
    w_in = jax.random.normal(ks[5], (DEPTH, D_MODEL, D_IN), F32) * (D_MODEL ** -0.5) * jnp.asarray(col_scale)
    w_alpha2 = jax.random.normal(ks[6], (DEPTH, GATE_RANK, H_A * DK_A), F32) * (GATE_RANK ** -0.5)
    b_alpha = 0.1 * jax.random.normal(ks[7], (DEPTH, H_A * DK_A), F32)
    gla_norm_g = 1.0 + 0.02 * jax.random.normal(ks[8], (DEPTH, DV_A), F32)
    w_out = jax.random.normal(ks[9], (DEPTH, D_MIX, D_MODEL), F32) * (D_MIX ** -0.5) * DEEPNORM_BETA
    ln_g = 1.0 + 0.02 * jax.random.normal(ks[10], (DEPTH, D_MODEL), F32)
    ln_b = 0.02 * jax.random.normal(ks[11], (DEPTH, D_MODEL), F32)
    rel_bias = 0.5 * jax.random.normal(ks[12], (N_BUCKETS, H_B), F32)
    return {"x_prompt": x_prompt, "x_sample": x_sample, "state_gla": state_gla,
            "cache_k_win": cache_k_win, "cache_v_win": cache_v_win,
            "w_in": w_in, "w_alpha2": w_alpha2, "b_alpha": b_alpha, "gla_norm_g": gla_norm_g,
            "w_out": w_out, "ln_g": ln_g, "ln_b": ln_b, "rel_bias": rel_bias}


def reference(x_prompt, x_sample, state_gla, cache_k_win, cache_v_win, w_in, w_alpha2, b_alpha,
              gla_norm_g, w_out, ln_g, ln_b, rel_bias):
    yp, ys = x_prompt, x_sample
    sp_l, ss_l, kp_l, vp_l, ks_l, vs_l = [], [], [], [], [], []
    for l in range(DEPTH):
        params = (w_in[l], w_alpha2[l], b_alpha[l], gla_norm_g[l], w_out[l], ln_g[l], ln_b[l], rel_bias)
        s0p = jnp.zeros((yp.shape[0], H_A, DK_A, DV_A), F32)
        yp, sp, kp, vp = mixer_layer(yp, s0p, None, None, *params)
        ys, ss, kn, vn = mixer_layer(ys, state_gla[l], cache_k_win[l], cache_v_win[l], *params)
        sp_l.append(sp); ss_l.append(ss); kp_l.append(kp); vp_l.append(vp); ks_l.append(kn); vs_l.append(vn)
    state_gla_prompt = jnp.stack(sp_l)
    state_gla_sample = jnp.stack(ss_l)
    k_win_prompt = jnp.stack(kp_l)
    v_win_prompt = jnp.stack(vp_l)
    k_win_sample_new = jnp.stack(ks_l)
    v_win_sample_new = jnp.stack(vs_l)
    return (yp, ys, state_gla_prompt, state_gla_sample, k_win_prompt, v_win_prompt, k_win_sample_new, v_win_sample_new)
```

```python
from contextlib import ExitStack
import math
import numpy as np
import concourse.bass as bass
import concourse.mybir as mybir
from concourse.bass_utils import run_bass_kernel_spmd

F32 = mybir.dt.float32
BF16 = mybir.dt.bfloat16
AF = mybir.ActivationFunctionType
ALU = mybir.AluOpType
AX = mybir.AxisListType

NCORES = 8
D = 1024
DIN = 3600
NLOC = 4096
NOWN = 2048
NB = 32
NOFF = 5
ALPHA = 2.0 ** 0.25
LN_EPS = 1e-5
RMS_EPS = 1e-6
NEG = -30000.0


class _Stop(Exception):
    pass


class Buf:
    __slots__ = ("name", "w", "rs", "excl")

    def __init__(self, name="", excl=False):
        self.name = name
        self.w = None
        self.rs = {}
        self.excl = excl


class Sched:
    def __init__(self, nc, nds=48):
        self.nc = nc
        self.engs = {"pe": nc.tensor, "act": nc.scalar, "dve": nc.vector,
                     "pool": nc.gpsimd, "sp": nc.sync}
        self.csem = {e: nc.alloc_semaphore("c_" + e) for e in ("pe", "act", "dve", "pool")}
        self.ccnt = {e: 0 for e in self.csem}
        self.NDS = nds
        self.dsem = [nc.alloc_semaphore("d%d" % i) for i in range(nds)]
        self.dcnt = [0] * nds
        self.dnext = 0
        self.seen = {e: {} for e in self.engs}
        self.snaps = {}
        self.rec = None

    def _need(self, e, ev, acc):
        if ev is None:
            return
        key, sem, val = ev
        if e == "pe" and key == "pe":
            return
        if self.seen[e].get(key, 0) >= val:
            return
        for i, (k2, s2, v2) in enumerate(acc):
            if k2 == key:
                if val > v2:
                    acc[i] = (key, sem, val)
                return
        acc.append((key, sem, val))

    def _collect(self, e, reads, writes):
        acc = []
        for b in reads:
            self._need(e, b.w, acc)
            if b.excl:
                for k, (sem, val) in b.rs.items():
                    if k != e:
                        self._need(e, (k, sem, val), acc)
        for b in writes:
            self._need(e, b.w, acc)
            for k, (sem, val) in b.rs.items():
                self._need(e, (k, sem, val), acc)
        acc.sort(key=lambda t: -t[2])
        keep = []
        implied = {}
        for (key, sem, val) in acc:
            if implied.get(key, 0) >= val:
                continue
            keep.append((key, sem, val))
            snap = self.snaps.get((key, val))
            if snap:
                for k2, v2 in snap.items():
                    if implied.get(k2, 0) < v2:
                        implied[k2] = v2
        return keep

    def _mark(self, e, w):
        key, sem, val = w
        se = self.seen[e]
        if se.get(key, 0) < val:
            se[key] = val
        snap = self.snaps.get((key, val))
        if snap:
            for k2, v2 in snap.items():
                if k2 != e and se.get(k2, 0) < v2:
                    se[k2] = v2

    def _wait(self, e, ev):
        acc = []
        self._need(e, ev, acc)
        for w in acc:
            self.engs[e].wait_ge(w[1], w[2])
            self._mark(e, w)

    def _commit(self, e, ev, reads, writes):
        key, sem, val = ev
        snap = dict(self.seen[e])
        snap.pop(e, None) if e in ("act", "dve", "pool") else None
        self.snaps[(key, val)] = snap
        for b in reads:
            b.rs[key] = (sem, val)
        for b in writes:
            b.w = ev
            b.rs = {}

    def _emit(self, e, fn, reads, writes):
        ws = self._collect(e, reads, writes)
        for w in ws[:-1]:
            self.engs[e].wait_ge(w[1], w[2])
        ins = fn()
        if ws:
            ins._wait_ge(ws[-1][1], ws[-1][2])
        for w in ws:
            self._mark(e, w)
        return ins

    def record(self, f, *args):
        self.rec = []
        f(*args)
        q, self.rec = self.rec, None
        return q

    def replay_rr(self, queues):
        qs = [list(q) for q in queues if q]
        idx = [0] * len(qs)
        live = True
        while live:
            live = False
            for i, q in enumerate(qs):
                if idx[i] < len(q):
                    kind, a, b, r, w = q[idx[i]]
                    idx[i] += 1
                    live = True
                    if kind == "op":
                        self.op(a, b, r, w)
                    elif kind == "dma":
                        self.dma(a, b[0], b[1], r, w)
                    else:
                        self.mm(b, r, w)

    def pipeline(self, recs, nst):
        parts = []
        for q in recs:
            n = len(q)
            cuts = [(n * k) // nst for k in range(nst + 1)]
            parts.append([q[cuts[k]:cuts[k + 1]] for k in range(nst)])
        for step in range(len(recs) + nst - 1):
            qs = []
            for k in range(nst - 1, -1, -1):
                if 0 <= step - k < len(recs):
                    qs.append(parts[step - k][k])
            self.replay_rr(qs)

    def op(self, e, fn, reads=(), writes=()):
        if self.rec is not None:
            self.rec.append(("op", e, fn, list(reads), list(writes)))
            return
        ins = self._emit(e, fn, reads, writes)
        self.ccnt[e] += 1
        ins.then_inc(self.csem[e], 1)
        self._commit(e, (e, self.csem[e], self.ccnt[e]), reads, writes)

    def mm(self, fns, reads=(), writes=()):
        if self.rec is not None:
            self.rec.append(("mm", "pe", list(fns), list(reads), list(writes)))
            return
        ins = self._emit("pe", fns[0], reads, writes)
        for f in fns[1:]:
            ins = f()
        self.ccnt["pe"] += 1
        ins.then_inc(self.csem["pe"], 1)
        self._commit("pe", ("pe", self.csem["pe"], self.ccnt["pe"]), reads, writes)

    def dma(self, q, out, in_, reads=(), writes=()):
        if self.rec is not None:
            self.rec.append(("dma", q, (out, in_), list(reads), list(writes)))
            return
        i = self.dnext
        self.dnext = (self.dnext + 1) % self.NDS
        if self.dcnt[i] > 0:
            self._wait(q, (("d", i), self.dsem[i], self.dcnt[i]))
        for w in self._collect(q, reads, writes):
            self.engs[q].wait_ge(w[1], w[2])
            self._mark(q, w)
        self.dcnt[i] += 16
        self.engs[q].dma_start(out=out, in_=in_).then_inc(self.dsem[i], 16)
        self._commit(q, (("d", i), self.dsem[i], self.dcnt[i]), reads, writes)

    def barrier(self):
        for e in self.engs:
            for f in self.csem:
                if self.ccnt[f] > 0:
                    self._wait(e, (f, self.csem[f], self.ccnt[f]))
            for i in range(self.NDS):
                if self.dcnt[i] > 0:
                    self._wait(e, (("d", i), self.dsem[i], self.dcnt[i]))

    def finish(self):
        for i in range(self.NDS):
            if self.dcnt[i] > 0:
                self._wait("sp", (("d", i), self.dsem[i], self.dcnt[i]))


def build():
    nc = bass.Bass("TRN2", target_bir_lowering=False)
    V, A, G, PE = nc.vector, nc.scalar, nc.gpsimd, nc.tensor
    S = Sched(nc)

    def din(n, s):
        return nc.dram_tensor(n, s, F32, kind="ExternalInput").ap()

    def dout(n, s):
        return nc.dram_tensor(n, s, F32, kind="ExternalOutput").ap()

    xloc = din("xloc", [NLOC, D]); xs = din("xs", [4, D]); st = din("st", [4, 4, 64, 128])
    ck = din("ck", [4, 2048, 512]); cv = din("cv", [4, 2048, 512])
    w_in = din("w_in", [D, DIN]); w2 = din("w2", [16, 256]); ba = din("ba", [1, 256])
    gng = din("gng", [1, 128]); w_out = din("w_out", [D, D]); lng = din("lng", [1, D]); lnb = din("lnb", [1, D])
    biasT = din("biasT", [8, 128, NOFF * 128]); logc = din("logc", [128, NOFF * 128])
    bdec = din("bdec", [3, 128, 8]); b0 = din("b0", [1, 8]); cm = din("cm", [128, 1])
    biasF = din("biasF", [8, 128, 256])
    vscr = nc.dram_tensor("vscr", [NLOC, 256], BF16).ap()
    tri_d = din("tri", [128, 128]); ident_d = din("ident", [128, 128]); bd_d = din("bd", [8, 512])
    oh4_d = din("oh4", [128, 16])

    y_o = dout("y_o", [NOWN, D]); sp_o = dout("sp_o", [4, 64, 128])
    kw_o = dout("kw_o", [NOWN, 512]); vw_o = dout("vw_o", [NOWN, 512])
    ys_o = dout("ys_o", [4, D]); ss_o = dout("ss_o", [4, 4, 64, 128])
    kn_o = dout("kn_o", [4, 512]); vn_o = dout("vn_o", [4, 512])
    scr_q = nc.dram_tensor("scr_q", [4, 512], F32).ap()

    bkA = nc.alloc_psum_tensor("bkA", [128, 2048], F32).ap()
    bk = [bkA[:, i * 512:(i + 1) * 512] for i in range(4)]
    bk += [nc.alloc_psum_tensor("bk%d" % i, [128, 512], F32).ap() for i in (4, 5)]
    bkb = [Buf("bk%d" % i, True) for i in range(6)]

    top = ExitStack()

    def sbt(es, name, shape, dt=F32):
        return es.enter_context(nc.sbuf_tensor("s_" + name, shape, dt)).ap(), Buf(name)

    bxT = [Buf("xT%d" % i) for i in range(NB)]
    xTd = nc.dram_tensor("xTd", [8, 128, 8 * 512], BF16).ap()
    bxTd = Buf("xTd")
    mixT, _ = sbt(top, "mixT", [128, 8, NOWN], BF16)
    bmix = [[Buf("mix%d_%d" % (c, t)) for t in range(16)] for c in range(8)]
    xsT, bxsT = sbt(top, "xsT", [128, 8, 4], BF16)
    mixTs, bmixs = sbt(top, "mixTs", [128, 8, 4], BF16)
    identf, bidf = sbt(top, "identf", [128, 128])
    ident, bid = sbt(top, "ident", [128, 128], BF16)
    tri, btri = sbt(top, "tri", [128, 128])
    trin, btrin = sbt(top, "trin", [128, 128])
    mask4, bmask4 = sbt(top, "mask4", [128, 512], BF16)
    gngb, bgng = sbt(top, "gngb", [128, 512])
    cmt, bcm = sbt(top, "cmt", [128, 1])
    oh4, boh4 = sbt(top, "oh4", [128, 16])
    ones, bones = sbt(top, "ones", [128, 8])
    hsB, bhsB = sbt(top, "hsB", [4, 4, 512])
    GTss, bGTss = sbt(top, "GTss", [128, 4, 4])

    S.dma("sp", identf, ident_d, writes=[bidf])
    S.dma("sp", tri, tri_d, writes=[btri])
    S.dma("sp", cmt, cm, writes=[bcm])
    S.dma("sp", oh4, oh4_d, writes=[boh4])
    for h in range(4):
        S.dma("sp", gngb[:, h * 128:(h + 1) * 128], gng.partition_broadcast(128), writes=[bgng])
    S.op("pool", lambda: G.tensor_copy(out=ident, in_=identf), reads=[bidf], writes=[bid])
    S.op("pool", lambda: G.tensor_scalar(out=trin, in0=tri, scalar1=-1.0 / 16.0, scalar2=None, op0=ALU.mult),
         reads=[btri], writes=[btrin])
    for h in range(4):
        S.op("pool", lambda h=h: G.tensor_copy(out=mask4[:, h * 128:(h + 1) * 128], in_=tri), reads=[btri], writes=[bmask4])
    S.op("pool", lambda: G.memset(ones, 1.0), writes=[bones])

    try:
        esx = ExitStack()
        xT, _ = sbt(esx, "xT", [128, 8, NLOC], BF16)
        pb = [esx.enter_context(nc.psum_tensor("pb%d" % i, [128, 1024], BF16)).ap() for i in range(2)]
        pbb = [Buf("pb%d" % i, True) for i in range(2)]
        def xTr(sb):
            return bxT[sb * 4:(sb + 1) * 4]

        with ExitStack() as es:
            WA, bWA = sbt(es, "WA", [128, 8, 1552], BF16)
            w2f, bw2f = sbt(es, "w2f", [32, 256])
            W2a, bW2a = sbt(es, "W2a", [32, 256], BF16)
            S.op("pool", lambda: G.memset(w2f, 0.0), writes=[bw2f])
            S.dma("sp", w2f[0:16, :], w2, writes=[bw2f])
            S.dma("sp", w2f[16:17, :], ba, writes=[bw2f])
            S.op("pool", lambda: G.tensor_copy(out=W2a, in_=w2f), reads=[bw2f], writes=[bW2a])
            zaug = [sbt(es, "zaug%d" % i, [32, 512], BF16) for i in range(2)]
            for z, bz in zaug:
                S.op("pool", lambda z=z: G.memset(z, 1.0), writes=[bz])
            qTr2 = [sbt(es, "qTr%d" % i, [128, 2, 512]) for i in range(2)]
            Sst = [sbt(es, "Sst%d" % p, [128, 128]) for p in range(2)]
            Sbf = [sbt(es, "Sbf%d" % p, [128, 128], BF16) for p in range(2)]
            stmp, bstmp = sbt(es, "stmp", [128, 128])
            for p in range(2):
                S.op("pool", lambda p=p: G.memset(Sst[p][0], 0.0), writes=[Sst[p][1]])
                S.op("pool", lambda p=p: G.memset(Sbf[p][0], 0.0), writes=[Sbf[p][1]])
            DB = 3
            vA = [sbt(es, "vA%d" % i, [128, 512], BF16) for i in range(DB)]
            eL = [sbt(es, "eL%d" % i, [128, 256]) for i in range(DB)]
            ebm = [sbt(es, "ebm%d" % i, [128, 256]) for i in range(DB)]
            kt = [sbt(es, "kt%d" % i, [128, 256], BF16) for i in range(DB)]
            ebT = [sbt(es, "ebT%d" % i, [128, 2, 128]) for i in range(DB)]
            ela = [sbt(es, "ela%d" % i, [128, 2]) for i in range(DB)]
            qtT = [sbt(es, "qtT%d" % i, [128, 2, 128], BF16) for i in range(DB)]
            ktT = [sbt(es, "ktT%d" % i, [128, 2, 128], BF16) for i in range(DB)]
            ge = [sbt(es, "ge%d" % i, [128, 512]) for i in range(DB)]
            GS = [sbt(es, "GS%d" % i, [128, 512]) for i in range(DB)]
            Am = [sbt(es, "Am%d" % i, [128, 512], BF16) for i in range(DB)]
            sq = ge
            kAr = [sbt(es, "kAr%d" % i, [128, 256]) for i in range(DB)]
            ssq = [sbt(es, "ssq%d" % i, [128, 4]) for i in range(DB)]
            mixa = [sbt(es, "mixa%d" % i, [128, 512], BF16) for i in range(DB)]

            es_p1 = ExitStack()
            NXF, NXB = 2, 2
            xf = [sbt(es_p1, "xf%d" % i, [128, D]) for i in range(NXF)]
            xb = [sbt(es_p1, "xb%d" % i, [128, D], BF16) for i in range(NXB)]

            S.dma("sp", xf[0][0][0:4, :], xs, writes=[xf[0][1]])
            S.op("pool", lambda: G.tensor_copy(out=xb[0][0][0:4, :], in_=xf[0][0][0:4, :]), reads=[xf[0][1]], writes=[xb[0][1]])
            S.mm([lambda c=c: PE.transpose(pb[0][:, c * 4:(c + 1) * 4], xb[0][0][0:4, c * 128:(c + 1) * 128], ident[0:4, 0:4])
                  for c in range(8)], reads=[xb[0][1], bid], writes=[pbb[0]])
            S.op("dve", lambda: V.tensor_copy(out=xsT, in_=pb[0][:, 0:32].rearrange("p (c t) -> p c t", c=8)),
                 reads=[pbb[0]], writes=[bxsT])
            bstg = [Buf("wstg%d" % c) for c in range(8)]
            for c in range(8):
                stg_c = xT[:, c, 0:3104].bitcast(F32)
                S.dma("sp", stg_c, w_in[c * 128:(c + 1) * 128, 0:1552], writes=[bstg[c]])
            for c in range(8):
                stg_c = xT[:, c, 0:3104].bitcast(F32)
                e = ("act", "dve", "pool", "act", "dve", "act", "dve", "pool")[c]
                if e == "pool":
                    S.op(e, lambda c=c, stg_c=stg_c: G.tensor_copy(out=WA[:, c, :], in_=stg_c), reads=[bstg[c]] + bxT[0:25], writes=[bWA])
                elif e == "act":
                    S.op(e, lambda c=c, stg_c=stg_c: A.copy(out=WA[:, c, :], in_=stg_c), reads=[bstg[c]] + bxT[0:25], writes=[bWA])
                else:
                    S.op(e, lambda c=c, stg_c=stg_c: V.tensor_copy(out=WA[:, c, :], in_=stg_c), reads=[bstg[c]] + bxT[0:25], writes=[bWA])

            def p1_block(blk):
                f, bf_ = xf[blk % NXF]
                b_, bb_ = xb[blk % NXB]
                S.dma("sp", f, xloc[blk * 128:(blk + 1) * 128, :], writes=[bf_])
                if blk % 2 == 0:
                    S.op("act", lambda: A.copy(out=b_, in_=f), reads=[bf_], writes=[bb_])
                else:
                    S.op("dve", lambda: V.tensor_copy(out=b_, in_=f), reads=[bf_], writes=[bb_])
                pt, bpt = pb[blk % 2], pbb[blk % 2]
                S.mm([lambda c=c: PE.transpose(pt[:, c * 128:(c + 1) * 128], b_[:, c * 128:(c + 1) * 128], ident)
                      for c in range(8)], reads=[bb_, bid], writes=[bpt])
                src = pt.rearrange("p (c t) -> p c t", c=8)
                dst = xT[:, :, blk * 128:(blk + 1) * 128]
                if blk % 2 == 0:
                    S.op("dve", lambda: V.tensor_copy(out=dst, in_=src), reads=[bpt], writes=[bxT[blk]])
                else:
                    S.op("act", lambda: A.copy(out=dst, in_=src), reads=[bpt], writes=[bxT[blk]])
                if blk % 4 == 3:
                    sb_ = blk // 4
                    S.dma("sp", xTd[sb_].rearrange("p (c t) -> p c t", c=8), xT[:, :, sb_ * 512:(sb_ + 1) * 512],
                          reads=bxT[sb_ * 4:sb_ * 4 + 4], writes=[bxTd])

            accn = [0]

            def acc():
                i = accn[0] % 2
                accn[0] += 1
                return bk[i], bkb[i]

            def silu_gate(P, gps, bgps, ge_t, bge, GS_t, bGS):
                S.op("act", lambda: A.activation(out=ge_t[0:P, :], in_=gps[0:P, :], func=AF.Exp, scale=-1.0),
                     reads=[bgps], writes=[bge])
                S.op("act", lambda: A.activation(out=ge_t[0:P, :], in_=ge_t[0:P, :], func=AF.Ln, bias=1.0, scale=1.0),
                     reads=[bge], writes=[bge])
                S.op("act", lambda: A.activation(out=ge_t[0:P, :], in_=ge_t[0:P, :], func=AF.Exp, scale=-1.0),
                     reads=[bge], writes=[bge])
                S.op("dve", lambda: V.tensor_tensor(out=GS_t[0:P, :], in0=gps[0:P, :], in1=ge_t[0:P, :], op=ALU.mult),
                     reads=[bgps, bge], writes=[bGS])
                S.op("pool", lambda: G.tensor_tensor(out=GS_t[0:P, :], in0=GS_t[0:P, :], in1=gngb[0:P, :], op=ALU.mult),
                     reads=[bGS, bgng], writes=[bGS])

            def rms_mix(P, ops, bops, sq_t, bsq, ssq_t, bssq, GS_t, bGS, mixa_t, bmixa):
                S.op("act", lambda: A.copy(out=sq_t[0:P, :], in_=ops[0:P, :]), reads=[bops], writes=[bsq])
                S.op("dve", lambda: V.tensor_tensor(out=sq_t[0:P, :], in0=sq_t[0:P, :], in1=sq_t[0:P, :], op=ALU.mult),
                     reads=[bsq], writes=[bsq])
                S.op("dve", lambda: V.tensor_reduce(out=ssq_t[0:P, :], in_=sq_t[0:P, :].rearrange("p (h v) -> p h v", h=4),
                                                    axis=AX.X, op=ALU.add), reads=[bsq], writes=[bssq])
                S.op("act", lambda: A.activation(out=ssq_t[0:P, :], in_=ssq_t[0:P, :], func=AF.Ln, bias=RMS_EPS, scale=1.0 / 128.0),
                     reads=[bssq], writes=[bssq])
                S.op("act", lambda: A.activation(out=ssq_t[0:P, :], in_=ssq_t[0:P, :], func=AF.Exp, scale=-0.5),
                     reads=[bssq], writes=[bssq])
                for h in range(4):
                    S.op("dve", lambda h=h: V.scalar_tensor_tensor(
                        out=mixa_t[0:P, h * 128:(h + 1) * 128], in0=ops[0:P, h * 128:(h + 1) * 128],
                        scalar=ssq_t[0:P, h:h + 1], in1=GS_t[0:P, h * 128:(h + 1) * 128], op0=ALU.mult, op1=ALU.mult),
                        reads=[bops, bssq, bGS], writes=[bmixa])


            def stage_s(sb):
                tok = slice(sb * 512, (sb + 1) * 512)
                ps, bps = acc()
                S.mm([lambda c=c: PE.matmul(ps[0:16, :], lhsT=WA[:, c, 1024:1040], rhs=xT[:, c, tok],
                                            start=(c == 0), stop=(c == 7)) for c in range(8)],
                     reads=[bWA] + xTr(sb), writes=[bps])
                za, bza = zaug[sb % 2]
                S.op("act", lambda: A.copy(out=za[0:16, :], in_=ps[0:16, :]), reads=[bps], writes=[bza])
                if sb >= 4:
                    for (dst, bdst, col0) in ((qTr2[sb % 2][0], qTr2[sb % 2][1], 0),):
                        for dc in range(2):
                            ps2, bps2 = acc()
                            S.mm([lambda c=c, ps2=ps2, dc=dc, col0=col0: PE.matmul(
                                ps2, lhsT=WA[:, c, col0 + dc * 128:col0 + (dc + 1) * 128], rhs=xT[:, c, tok],
                                start=(c == 0), stop=(c == 7)) for c in range(8)],
                                reads=[bWA] + xTr(sb), writes=[bps2])
                            if dc == 0:
                                S.op("act", lambda ps2=ps2, dst=dst, dc=dc: A.copy(out=dst[:, dc, :], in_=ps2), reads=[bps2], writes=[bdst])
                            else:
                                S.op("dve", lambda ps2=ps2, dst=dst, dc=dc: V.tensor_copy(out=dst[:, dc, :], in_=ps2), reads=[bps2], writes=[bdst])

            def stage_a(blk):
                own = blk >= 16
                i3 = blk % DB
                bt = slice(blk * 128, (blk + 1) * 128)
                g1, bg1 = acc()
                S.mm([lambda c=c: PE.matmul(g1, lhsT=xT[:, c, bt], rhs=WA[:, c, 256:768],
                                            start=(c == 0), stop=(c == 7)) for c in range(8)],
                     reads=[bWA, bxT[blk]], writes=[bg1])
                vAt, bvA = vA[i3]
                kAt, bkA = kAr[i3]
                S.op("act", lambda: A.copy(out=vAt[:, 0:256], in_=g1[:, 256:512]), reads=[bg1], writes=[bvA])
                S.op("dve", lambda: V.tensor_copy(out=kAt, in_=g1[:, 0:256]), reads=[bg1], writes=[bkA])
                g2, bg2 = acc()
                S.mm([lambda c=c: PE.matmul(g2[:, 0:256], lhsT=xT[:, c, bt], rhs=WA[:, c, 768:1024],
                                            start=(c == 0), stop=(c == 7)) for c in range(8)],
                     reads=[bWA, bxT[blk]], writes=[bg2])
                S.op("act", lambda: A.copy(out=vAt[:, 256:512], in_=g2[:, 0:256]), reads=[bg2], writes=[bvA])
                if own:
                    g3, bg3 = acc()
                    S.mm([lambda c=c: PE.matmul(g3, lhsT=xT[:, c, bt], rhs=WA[:, c, 1040:1552],
                                                start=(c == 0), stop=(c == 7)) for c in range(8)],
                         reads=[bWA, bxT[blk]], writes=[bg3])
                    silu_gate(128, g3, bg3, ge[i3][0], ge[i3][1], GS[i3][0], GS[i3][1])

            def stage_b1(blk):
                own = blk >= 16
                i3 = blk % DB
                sb, j = blk // 4, blk % 4
                za, bza = zaug[sb % 2]
                ub, bub = bk[2], bkb[2]
                S.mm([lambda: PE.matmul(ub[:, 0:256], lhsT=za[:, j * 128:(j + 1) * 128], rhs=W2a, start=True, stop=True)],
                     reads=[bza, bW2a], writes=[bub])
                eLt, beL = eL[i3]
                S.op("act", lambda: A.activation(out=eLt, in_=ub[:, 0:256], func=AF.Exp, scale=-1.0), reads=[bub], writes=[beL])
                S.op("act", lambda: A.activation(out=eLt, in_=eLt, func=AF.Ln, bias=1.0, scale=1.0), reads=[beL], writes=[beL])
                fns = [lambda: PE.matmul(ub[:, 256:512], lhsT=trin, rhs=eLt, start=True, stop=True)]
                if own:
                    fns += [lambda dc=dc: PE.matmul(ub[:, dc * 128:(dc + 1) * 128], lhsT=eLt[:, dc * 128:(dc + 1) * 128], rhs=trin,
                                                    start=True, stop=True) for dc in range(2)]
                else:
                    fns += [lambda dc=dc: PE.matmul(ub[:, dc:dc + 1], lhsT=eLt[:, dc * 128:(dc + 1) * 128], rhs=trin[:, 127:128],
                                                    start=True, stop=True) for dc in range(2)]
                S.mm(fns, reads=[btrin, beL], writes=[bub])
                ebmt, bebm = ebm[i3]
                S.op("act", lambda: A.activation(out=ebmt, in_=ub[:, 256:512], func=AF.Exp, scale=-1.0), reads=[bub], writes=[bebm])
                ktt, bkt = kt[i3]
                S.op("dve", lambda: V.tensor_tensor(out=ktt, in0=kAr[i3][0], in1=ebmt, op=ALU.mult),
                     reads=[kAr[i3][1], bebm], writes=[bkt])
                elat, bela = ela[i3]
                if own:
                    ebTt, bebT = ebT[i3]
                    bT3 = ub[:, 0:256].rearrange("p (c t) -> p c t", c=2)
                    S.op("act", lambda: A.activation(out=ebTt, in_=bT3, func=AF.Exp), reads=[bub], writes=[bebT])
                    S.op("dve", lambda: V.tensor_copy(out=elat, in_=ebTt[:, :, 127]), reads=[bebT], writes=[bela])
                    qtTt, bqtT = qtT[i3]
                    ktTt, bktT = ktT[i3]
                    qTr, bqTr = qTr2[sb % 2]
                    S.op("dve", lambda: V.scalar_tensor_tensor(out=qtTt, in0=qTr[:, :, j * 128:(j + 1) * 128], scalar=0.125,
                                                               in1=ebTt, op0=ALU.mult, op1=ALU.mult),
                         reads=[bqTr, bebT], writes=[bqtT])
                    ptk, bptk = pb[(blk + 1) % 2], pbb[(blk + 1) % 2]
                    S.mm([lambda p=p: PE.transpose(ptk[:, 768 + p * 128:768 + (p + 1) * 128], ktt[:, p * 128:(p + 1) * 128], ident)
                          for p in range(2)], reads=[bkt, bid], writes=[bptk])
                    S.op("act", lambda: A.copy(out=ktTt, in_=ptk[:, 768:1024].rearrange("p (c t) -> p c t", c=2)),
                         reads=[bptk], writes=[bktT])
                else:
                    S.op("act", lambda: A.activation(out=elat, in_=ub[:, 0:2], func=AF.Exp), reads=[bub], writes=[bela])

            def stage_b2(blk):
                own = blk >= 16
                i3 = blk % DB
                vAt, bvA = vA[i3]
                ktt, bkt = kt[i3]
                elat, bela = ela[i3]
                apsb = ((bk[5], bkb[5]), (bk[3], bkb[3]))
                if own:
                    qtTt, bqtT = qtT[i3]
                    ktTt, bktT = ktT[i3]
                    S.mm([lambda h=h: PE.matmul(apsb[h % 2][0][:, (h // 2) * 128:(h // 2 + 1) * 128],
                                                lhsT=ktTt[(h % 2) * 64:(h % 2) * 64 + 64, h // 2, :],
                                                rhs=qtTt[(h % 2) * 64:(h % 2) * 64 + 64, h // 2, :],
                                                start=True, stop=True) for h in range(4)],
                         reads=[bktT, bqtT], writes=[bkb[5], bkb[3]])
                    Amt, bAm = Am[i3]
                    Am4 = Amt.rearrange("p (c r t) -> p c r t", c=2, r=2)
                    for par in range(2):
                        S.op("dve", lambda par=par: V.tensor_tensor(
                            out=Am4[:, :, par, :], in0=apsb[par][0][:, 0:256].rearrange("p (c t) -> p c t", c=2),
                            in1=mask4[:, 0:256].rearrange("p (c t) -> p c t", c=2), op=ALU.mult),
                            reads=[apsb[par][1], bmask4], writes=[bAm])
                    ops, bops = bk[4], bkb[4]
                    fns = []
                    for h in range(4):
                        fns.append(lambda h=h: PE.matmul(ops[:, h * 128:(h + 1) * 128], lhsT=Amt[:, h * 128:(h + 1) * 128],
                                                         rhs=vAt[:, h * 128:(h + 1) * 128], start=True, stop=False))
                        fns.append(lambda h=h: PE.matmul(ops[:, h * 128:(h + 1) * 128],
                                                         lhsT=qtTt[(h % 2) * 64:(h % 2) * 64 + 64, h // 2, :],
                                                         rhs=Sbf[h // 2][0][(h % 2) * 64:(h % 2) * 64 + 64, :],
                                                         start=False, stop=True))
                    S.mm(fns, reads=[bAm, bvA, bqtT, Sbf[0][1], Sbf[1][1]], writes=[bops])
                    rms_mix(128, ops, bops, sq[i3][0], sq[i3][1], ssq[i3][0], ssq[i3][1], GS[i3][0], GS[i3][1],
                            mixa[i3][0], mixa[i3][1])
                    pt, bpt = pb[blk % 2], pbb[blk % 2]
                    mx = mixa[i3][0]
                    S.mm([lambda h=h: PE.transpose(pt[:, h * 128:(h + 1) * 128], mx[:, h * 128:(h + 1) * 128], ident)
                          for h in range(4)], reads=[mixa[i3][1], bid], writes=[bpt])
                    ob = blk - 16
                    S.op("act", lambda: A.copy(out=mixT[:, 0:4, ob * 128:(ob + 1) * 128],
                                               in_=pt[:, 0:512].rearrange("p (c t) -> p c t", c=4)),
                         reads=[bpt], writes=[bmix[c][ob] for c in range(4)])
                for p in range(2):
                    kvb, bkvb = apsb[p]
                    S.mm([lambda p=p, kvb=kvb: PE.matmul(kvb[:, 256:512], lhsT=ktt[:, p * 128:(p + 1) * 128],
                                                         rhs=vAt[:, p * 256:(p + 1) * 256], start=True, stop=True)],
                         reads=[bkt, bvA], writes=[bkvb])
                for p in range(2):
                    kvb, bkvb = apsb[p]
                    St, bSt = Sst[p]
                    for hh in range(2):
                        r = slice(hh * 64, hh * 64 + 64)
                        S.op("dve", lambda p=p, hh=hh, r=r, St=St, kvb=kvb: V.tensor_tensor(
                            out=stmp[r, :], in0=kvb[r, 256 + hh * 128:256 + (hh + 1) * 128], in1=St[r, :], op=ALU.add),
                            reads=[bkvb, bSt], writes=[bstmp])
                    S.op("dve", lambda p=p, St=St: V.tensor_scalar(out=St, in0=stmp, scalar1=elat[:, p:p + 1], scalar2=None,
                                                                   op0=ALU.mult), reads=[bstmp, bela], writes=[bSt])
                    S.op("act", lambda p=p, St=St: A.copy(out=Sbf[p][0], in_=St), reads=[bSt], writes=[Sbf[p][1]])

            LA = 4
            for blk in range(LA):
                S.replay_rr([S.record(p1_block, blk)])
            for step in range(NB + 2):
                qs = []
                if 0 <= step - 2 < NB:
                    qs.append(S.record(stage_b2, step - 2))
                if 0 <= step - 1 < NB:
                    qs.append(S.record(stage_b1, step - 1))
                if step < NB:
                    def sa(step=step):
                        if step % 4 == 0:
                            stage_s(step // 4)
                        stage_a(step)
                    qs.append(S.record(sa))
                if step + LA < NB:
                    qs.append(S.record(p1_block, step + LA))
                S.replay_rr(qs)
            S.barrier()
            es_p1.close()
            for p in range(2):
                S.dma("sp", sp_o[2 * p:2 * p + 2].rearrange("h d v -> (h d) v"), Sst[p][0], reads=[Sst[p][1]])

            hsA, bhsA = sbt(es, "hsA", [4, 1552])
            for gi, (c0, c1) in enumerate(((0, 512), (512, 1024), (1024, 1536), (1536, 1552))):
                ps, bps = acc()
                S.mm([lambda c=c, ps=ps, c0=c0, c1=c1: PE.matmul(ps[0:4, 0:c1 - c0], lhsT=xsT[:, c, :], rhs=WA[:, c, c0:c1],
                                                                 start=(c == 0), stop=(c == 7)) for c in range(8)],
                     reads=[bWA, bxsT], writes=[bps])
                S.op("act", lambda ps=ps, c0=c0, c1=c1: A.copy(out=hsA[:, c0:c1], in_=ps[0:4, 0:c1 - c0]), reads=[bps], writes=[bhsA])
            ps, bps = acc()
            S.mm([lambda c=c, ps=ps: PE.matmul(ps[0:16, 0:4], lhsT=WA[:, c, 1024:1040], rhs=xsT[:, c, :],
                                               start=(c == 0), stop=(c == 7)) for c in range(8)], reads=[bWA, bxsT], writes=[bps])
            za, bza = zaug[0]
            S.op("act", lambda: A.copy(out=za[0:16, 0:4], in_=ps[0:16, 0:4]), reads=[bps], writes=[bza])
            ups, bups = bk[3], bkb[3]
            S.mm([lambda dc=dc: PE.matmul(ups[:, dc * 4:(dc + 1) * 4], lhsT=W2a[:, dc * 128:(dc + 1) * 128], rhs=za[:, 0:4],
                                          start=True, stop=True) for dc in range(2)], reads=[bza, bW2a], writes=[bups])
            aTs, baTs = sbt(es, "aTs", [128, 8])
            S.op("act", lambda: A.activation(out=aTs, in_=ups[:, 0:8], func=AF.Exp, scale=-1.0), reads=[bups], writes=[baTs])
            S.op("act", lambda: A.activation(out=aTs, in_=aTs, func=AF.Ln, bias=1.0, scale=1.0), reads=[baTs], writes=[baTs])
            S.op("act", lambda: A.activation(out=aTs, in_=aTs, func=AF.Exp, scale=-1.0 / 16.0), reads=[baTs], writes=[baTs])
            qsT, bqsT = sbt(es, "qsT", [128, 2, 4])
            ps, bps = acc()
            for dc in range(2):
                S.mm([lambda c=c, dc=dc, ps=ps: PE.matmul(ps[:, dc * 4:(dc + 1) * 4], lhsT=WA[:, c, dc * 128:(dc + 1) * 128],
                                                          rhs=xsT[:, c, :], start=(c == 0), stop=(c == 7)) for c in range(8)],
                     reads=[bWA, bxsT], writes=[bps])
            S.op("dve", lambda: V.tensor_scalar(out=qsT, in0=ps[:, 0:8].rearrange("p (c t) -> p c t", c=2), scalar1=0.125,
                                                scalar2=None, op0=ALU.mult), reads=[bps], writes=[bqsT])
            Sn = [sbt(es, "Sn%d" % b, [128, 2, 128]) for b in range(4)]
            kmask, bkmask = sbt(es, "kmask", [4, 256])
            qm = [sbt(es, "qm%d" % b, [128, 2, 4]) for b in range(4)]
            for b in range(4):
                Snt, bSn = Sn[b]
                for p in range(2):
                    S.dma("sp", Snt[:, p, :], st[b, 2 * p:2 * p + 2].rearrange("h d v -> (h d) v"), writes=[bSn])
                S.op("dve", lambda b=b: V.tensor_scalar(out=kmask, in0=hsA[:, 256:512], scalar1=identf[0:4, b:b + 1],
                                                        scalar2=None, op0=ALU.mult), reads=[bhsA, bidf], writes=[bkmask])
                kv, bkv = bk[2], bkb[2]
                S.mm([lambda p=p: PE.matmul(kv[:, p * 256:(p + 1) * 256], lhsT=kmask[:, p * 128:(p + 1) * 128],
                                            rhs=hsA[:, 512 + p * 256:512 + (p + 1) * 256], start=True, stop=True) for p in range(2)],
                     reads=[bkmask, bhsA], writes=[bkv])
                for p in range(2):
                    for hh in range(2):
                        r = slice(hh * 64, hh * 64 + 64)
                        S.op("dve", lambda b=b, p=p, hh=hh, r=r, Snt=Snt: V.scalar_tensor_tensor(
                            out=Snt[r, p, :], in0=Snt[r, p, :], scalar=aTs[r, p * 4 + b:p * 4 + b + 1],
                            in1=kv[r, p * 256 + hh * 128:p * 256 + (hh + 1) * 128], op0=ALU.mult, op1=ALU.add),
                            reads=[bSn, baTs, bkv], writes=[bSn])
                    S.dma("sp", ss_o[b, 2 * p:2 * p + 2].rearrange("h d v -> (h d) v"), Snt[:, p, :], reads=[bSn])
                qmt, bqm = qm[b]
                for dc in range(2):
                    S.op("dve", lambda b=b, dc=dc, qmt=qmt: V.tensor_tensor(out=qmt[:, dc, :], in0=qsT[:, dc, :],
                                                                            in1=oh4[:, 4 * b:4 * b + 4], op=ALU.mult),
                         reads=[bqsT, boh4], writes=[bqm])
            ospb = ((bk[4], bkb[4]), (bk[5], bkb[5]))
            fns = []
            for h in range(4):
                r = slice((h % 2) * 64, (h % 2) * 64 + 64)
                for b in range(4):
                    fns.append(lambda h=h, b=b, r=r: PE.matmul(ospb[h % 2][0][0:4, (h // 2) * 128:(h // 2 + 1) * 128],
                                                               lhsT=qm[b][0][r, h // 2, :],
                                                               rhs=Sn[b][0][r, h // 2, :], start=(b == 0), stop=(b == 3)))
            S.mm(fns, reads=[qm[b][1] for b in range(4)] + [Sn[b][1] for b in range(4)], writes=[bkb[4], bkb[5]])
            osp, bosp = ge[1][0][0:4, :], ge[1][1]
            osp4 = osp.rearrange("p (c r v) -> p c r v", c=2, r=2)
            for par in range(2):
                S.op("act", lambda par=par: A.copy(out=osp4[:, :, par, :], in_=ospb[par][0][0:4, 0:256].rearrange("p (c v) -> p c v", c=2)),
                     reads=[ospb[par][1]], writes=[bosp])
            gsb, bgsb = GS[1][0][0:4, :], GS[1][1]
            S.op("act", lambda: A.copy(out=gsb, in_=hsA[:, 1040:1552]), reads=[bhsA], writes=[bgsb])
            silu_gate(4, gsb, bgsb, ge[0][0], ge[0][1], GS[0][0], GS[0][1])
            rms_mix(4, osp, bosp, sq[0][0], sq[0][1], ssq[0][0], ssq[0][1], GS[0][0], GS[0][1], mixa[0][0], mixa[0][1])
            pt, bpt = pb[0], pbb[0]
            mx = mixa[0][0]
            S.mm([lambda h=h: PE.transpose(pt[:, h * 4:(h + 1) * 4], mx[0:4, h * 128:(h + 1) * 128], ident[0:4, 0:4])
                  for h in range(4)], reads=[mixa[0][1], bid], writes=[bpt])
            S.op("act", lambda: A.copy(out=mixTs[:, 0:4, :], in_=pt[:, 0:16].rearrange("p (c t) -> p c t", c=4)),
                 reads=[bpt], writes=[bmixs])
            S.barrier()
        esx.close()
        esy = ExitStack()
        bk67 = esy.enter_context(nc.psum_tensor("bk67", [128, 1024], F32)).ap()
        for i in (6, 7):
            bk.append(bk67[:, (i - 6) * 512:(i - 5) * 512])
            bkb.append(Buf("bk%d" % i, True))

        with ExitStack() as es:
            logct, blogc = sbt(es, "logct", [128, NOFF * 128])
            S.dma("sp", logct, logc, writes=[blogc])
            stg = [sbt(es, "stg%d" % i, [128, 8, 128]) for i in range(2)]
            bst, bbst = sbt(es, "bst", [128, NOFF * 128])
            NXS = 3
            xst = [sbt(es, "xst%d" % i, [128, 8, 512], BF16) for i in range(NXS)]
            PT = []
            for par in range(1):
                t = {}
                t["Wp"], t["bWp"] = sbt(es, "Wp%d" % par, [128, 8, 4, 128], BF16)
                t["E"], t["bE"] = sbt(es, "E%d" % par, [128, 2, NOFF * 128], BF16)
                t["KT"], _ = sbt(es, "KT%d" % par, [128, NLOC], BF16)
                t["bKT"] = [Buf("KT%d_%d" % (par, i)) for i in range(8)]
                t["QT"], _ = sbt(es, "QT%d" % par, [128, NOWN], BF16)
                t["bQT"] = [Buf("QT%d_%d" % (par, i)) for i in range(4)]
                t["GTs"], _ = sbt(es, "GTs%d" % par, [128, NOWN], BF16)
                t["bGT"] = [Buf("GT%d_%d" % (par, i)) for i in range(4)]
                t["Vaug"], _ = sbt(es, "Vaug%d" % par, [128, NB, 2, 128], BF16)
                t["bVa"] = [Buf("Va%d_%d" % (par, i)) for i in range(8)]
                PT.append(t)
            NSB = 6
            SBK = (0, 1, 2, 3, 6, 7)
            NPB = 3
            pex = [sbt(es, "pex%d" % i, [128, 2, 512], BF16) for i in range(NPB)]
            pmk = [sbt(es, "pmk%d" % i, [128, 2, 512], BF16) for i in range(NPB)]

            def pairview(gi):
                k0 = SBK[gi % NSB]
                base = bk67 if k0 == 6 else bkA[:, k0 * 512:(k0 + 2) * 512]
                return base.rearrange("p (h n) -> p h n", h=2)
            vout = [sbt(es, "vout%d" % i, [128, 4, 128]) for i in range(2)]
            dtmp = [sbt(es, "dtmp%d" % i, [128, 1024]) for i in range(2)]
            Vcp, bVcp = sbt(es, "Vcp", [128, 32, 256], BF16)
            bvscr = Buf("vscr")
            Oacc = [sbt(es, "Oacc%d" % i, [128, NOWN]) for i in range(2)]
            EF4, bEF4 = sbt(es, "EF4", [128, 2, 512], BF16)
            bfs, bbfs = sbt(es, "bfs", [128, 256])
            for par in range(1):
                t = PT[par]
                S.op("pool", lambda t=t: G.memset(t["Vaug"], 1.0), writes=t["bVa"])
                S.op("act", lambda t=t: A.mul(out=t["Vaug"][:, 0:16], in_=t["Vaug"][:, 0:16], mul=cmt[:, 0:1]),
                     reads=t["bVa"][0:4] + [bcm], writes=t["bVa"][0:4])
            grp_n = [0]
            xs_n = [0]
            ipn = [0]

            def ipbank():
                i = (6, 7, 4, 5)[ipn[0] % 4]
                ipn[0] += 1
                return bk[i], bkb[i]

            def xs_load(sb):
                xt_, bxt_ = xst[xs_n[0] % NXS]
                xs_n[0] += 1
                S.dma("sp", xt_, xTd[sb].rearrange("p (c t) -> p c t", c=8), reads=[bxTd], writes=[bxt_])
                return xt_, bxt_

            def setup(hp):
                t = PT[0]
                Wp, bWp, E, bE = t["Wp"], t["bWp"], t["E"], t["bE"]
                KT, bKT, QT, bQT, GTs, bGT, Vaug, bVa = t["KT"], t["bKT"], t["QT"], t["bQT"], t["GTs"], t["bGT"], t["Vaug"], t["bVa"]
                for g in range(4):
                    col0 = 1552 + 512 * g + 128 * hp
                    sg, bsg = stg[g % 2]
                    S.dma("sp", sg, w_in[:, col0:col0 + 128].rearrange("(c p) n -> p c n", p=128), writes=[bsg])
                    if g % 2 == 0:
                        S.op("pool", lambda g=g, sg=sg: G.tensor_copy(out=Wp[:, :, g, :], in_=sg), reads=[bsg], writes=[bWp])
                    else:
                        S.op("dve", lambda g=g, sg=sg: V.tensor_copy(out=Wp[:, :, g, :], in_=sg), reads=[bsg], writes=[bWp])
                tiles = [xs_load(0), xs_load(1)]
                for hh in range(2):
                    S.dma("sp", bst, biasT[2 * hp + hh], writes=[bbst])
                    S.op("dve", lambda: V.tensor_tensor(out=bst, in0=bst, in1=logct, op=ALU.add), reads=[bbst, blogc], writes=[bbst])
                    S.op("act", lambda hh=hh: A.activation(out=E[:, hh, :], in_=bst, func=AF.Exp), reads=[bbst], writes=[bE])
                def do_sb(sb, xt_, bxt_):
                    tok = slice(sb * 512, (sb + 1) * 512)
                    ps, bps = ipbank()
                    S.mm([lambda c=c, ps=ps: PE.matmul(ps, lhsT=Wp[:, c, 1, :], rhs=xt_[:, c, :], start=(c == 0), stop=(c == 7))
                          for c in range(8)], reads=[bWp, bxt_], writes=[bps])
                    S.op("act", lambda ps=ps: A.copy(out=KT[:, tok], in_=ps), reads=[bps], writes=[bKT[sb]])
                    if sb < 4:
                        ps, bps = ipbank()
                        fns = []
                        for j in range(4):
                            for c in range(8):
                                fns.append(lambda c=c, j=j, ps=ps: PE.matmul(
                                    ps[:, j * 128:(j + 1) * 128], lhsT=xt_[:, c, j * 128:(j + 1) * 128], rhs=Wp[:, c, 2, :],
                                    start=(c == 0), stop=(c == 7)))
                        S.mm(fns, reads=[bWp, bxt_], writes=[bps])
                        ps3 = ps.rearrange("p (j n) -> p j n", j=4)
                        S.op("act", lambda ps3=ps3: A.mul(out=Vaug[:, sb * 4:(sb + 1) * 4, 0, 0:64], in_=ps3[:, :, 0:64], mul=cmt[:, 0:1]),
                             reads=[bps, bcm], writes=[bVa[sb]])
                        S.op("dve", lambda ps3=ps3: V.tensor_scalar(out=Vaug[:, sb * 4:(sb + 1) * 4, 1, 64:128], in0=ps3[:, :, 64:128],
                                                                  scalar1=cmt[:, 0:1], scalar2=None, op0=ALU.mult),
                             reads=[bps, bcm], writes=[bVa[sb]])
                    else:
                        so = sb - 4
                        otok = slice(so * 512, (so + 1) * 512)
                        r0 = so * 512
                        vo, bvo = vout[0]
                        ko, bko = vout[1]
                        for g2 in range(2):
                            ps, bps = ipbank()
                            fns = []
                            for jj in range(2):
                                j = 2 * g2 + jj
                                for c in range(8):
                                    fns.append(lambda c=c, j=j, jj=jj, ps=ps: PE.matmul(
                                        ps[:, jj * 256:(jj + 1) * 256], lhsT=xt_[:, c, j * 128:(j + 1) * 128],
                                        rhs=Wp[:, c, 1:3, :].rearrange("p g n -> p (g n)"), start=(c == 0), stop=(c == 7)))
                            S.mm(fns, reads=[bWp, bxt_], writes=[bps])
                            ps4 = ps.rearrange("p (j g n) -> p j g n", j=2, g=2)
                            b0 = sb * 4 + 2 * g2
                            S.op("act", lambda ps4=ps4, b0=b0: A.copy(out=Vaug[:, b0:b0 + 2, 0, 0:64], in_=ps4[:, :, 1, 0:64]),
                                 reads=[bps], writes=[bVa[sb]])
                            S.op("dve", lambda ps4=ps4, b0=b0: V.tensor_copy(out=Vaug[:, b0:b0 + 2, 1, 64:128], in_=ps4[:, :, 1, 64:128]),
                                 reads=[bps], writes=[bVa[sb]])
                            S.op("dve", lambda ps4=ps4, g2=g2: V.tensor_copy(out=vo[:, 2 * g2:2 * g2 + 2, :], in_=ps4[:, :, 1, :]),
                                 reads=[bps], writes=[bvo])
                            S.op("act", lambda ps4=ps4, g2=g2: A.copy(out=ko[:, 2 * g2:2 * g2 + 2, :], in_=ps4[:, :, 0, :]),
                                 reads=[bps], writes=[bko])
                        S.dma("sp", vw_o[r0:r0 + 512, hp * 128:(hp + 1) * 128].rearrange("(j p) n -> p j n", p=128), vo, reads=[bvo])
                        S.dma("sp", kw_o[r0:r0 + 512, hp * 128:(hp + 1) * 128].rearrange("(j p) n -> p j n", p=128), ko, reads=[bko])
                        ps, bps = ipbank()
                        S.mm([lambda c=c, ps=ps: PE.matmul(ps, lhsT=Wp[:, c, 0, :], rhs=xt_[:, c, :], start=(c == 0), stop=(c == 7))
                              for c in range(8)], reads=[bWp, bxt_], writes=[bps])
                        S.op("dve", lambda ps=ps: V.tensor_scalar(out=QT[:, otok], in0=ps, scalar1=0.125, scalar2=None, op0=ALU.mult),
                             reads=[bps], writes=[bQT[so]])
                        ps, bps = ipbank()
                        S.mm([lambda c=c, ps=ps: PE.matmul(ps, lhsT=Wp[:, c, 3, :], rhs=xt_[:, c, :], start=(c == 0), stop=(c == 7))
                              for c in range(8)], reads=[bWp, bxt_], writes=[bps])
                        S.op("act", lambda ps=ps: A.activation(out=GTs[:, otok], in_=ps, func=AF.Silu), reads=[bps], writes=[bGT[so]])
                for sb in range(8):
                    if sb + 2 < 8:
                        tiles.append(xs_load(sb + 2))
                    do_sb(sb, tiles[sb][0], tiles[sb][1])
                for hh in range(2):
                    S.dma("sp", bfs, biasF[2 * hp + hh], writes=[bbfs])
                    S.op("act", lambda hh=hh: A.activation(out=EF4[:, hh, 0:256], in_=bfs, func=AF.Exp), reads=[bbfs], writes=[bEF4])
                    S.op("dve", lambda hh=hh: V.tensor_copy(out=EF4[:, hh, 256:512], in_=EF4[:, hh, 0:256]), reads=[bEF4], writes=[bEF4])
                S.dma("sp", vscr.rearrange("(b p) n -> p b n", p=128), Vaug.rearrange("p b h n -> p b (h n)"), reads=bVa, writes=[bvscr])
                S.dma("sp", Vcp.rearrange("i (f c) n -> i f c n", f=2), vscr.rearrange("(f i c) n -> i f c n", f=2, i=128, c=16),
                      reads=[bvscr], writes=[bVcp])
                ps, bps = ipbank()
                S.mm([lambda c=c, ps=ps: PE.matmul(ps[0:4, :], lhsT=xsT[:, c, :], rhs=Wp[:, c, :, :].rearrange("p g n -> p (g n)"),
                                                   start=(c == 0), stop=(c == 7)) for c in range(8)], reads=[bWp, bxsT], writes=[bps])
                S.op("act", lambda ps=ps: A.copy(out=hsB[:, hp, :], in_=ps[0:4, :]), reads=[bps], writes=[bhsB])
                ps, bps = ipbank()
                S.mm([lambda c=c, ps=ps: PE.matmul(ps[:, 0:4], lhsT=Wp[:, c, 3, :], rhs=xsT[:, c, :], start=(c == 0), stop=(c == 7))
                      for c in range(8)], reads=[bWp, bxsT], writes=[bps])
                S.op("act", lambda ps=ps: A.activation(out=GTss[:, hp, :], in_=ps[:, 0:4], func=AF.Silu), reads=[bps], writes=[bGTss])

            def mainloop(hp):
                t = PT[0]
                E, bE = t["E"], t["bE"]
                KT, bKT, QT, bQT, GTs, bGT, Vaug, bVa = t["KT"], t["bKT"], t["QT"], t["bQT"], t["GTs"], t["bGT"], t["Vaug"], t["bVa"]
                items = []
                for qg in range(4):
                    kbl = list(range(12 + 4 * qg, 20 + 4 * qg))
                    kbl.remove(16 + 4 * qg)
                    kbl = [16 + 4 * qg] + kbl
                    for idx, kb in enumerate(kbl):
                        qa = max(4 * qg, kb - 16)
                        qz = min(4 * qg + 3, kb - 12)
                        for hh in range(2):
                            items.append(dict(qg=qg, hh=hh, kb=kb, qa=qa, n=qz - qa + 1, oa=16 + qa - kb,
                                              first=(idx == 0), last=(idx == len(kbl) - 1), it=hh, gi=grp_n[0]))
                            grp_n[0] += 1

                def emit_qk(w):
                    hh, kb, qa, n, gi = w["hh"], w["kb"], w["qa"], w["n"], w["gi"]
                    r = slice(hh * 64, hh * 64 + 64)
                    sp_, bsp = bk[SBK[gi % NSB]], bkb[SBK[gi % NSB]]
                    S.mm([lambda: PE.matmul(sp_[:, 0:n * 128], lhsT=KT[r, kb * 128:(kb + 1) * 128],
                                            rhs=QT[r, qa * 128:(qa + n) * 128], start=True, stop=True)],
                         reads=[bKT[kb // 4]] + [bQT[q // 4] for q in range(qa, qa + n)], writes=[bsp])

                def emit_rest2(w0, w1):
                    qg, kb, qa, n, gi = w0["qg"], w0["kb"], w0["qa"], w0["n"], w0["gi"]
                    nw = n * 128
                    pv = pairview(gi)
                    bsp0, bsp1 = bkb[SBK[gi % NSB]], bkb[SBK[(gi + 1) % NSB]]
                    px, bpx = pex[(gi // 2) % NPB]
                    pm, bpm = pmk[(gi // 2) % NPB]
                    S.op("act", lambda: A.activation(out=px[:, :, 0:nw], in_=pv[:, :, 0:nw], func=AF.Exp),
                         reads=[bsp0, bsp1], writes=[bpx])
                    esl = E[:, :, w0["oa"] * 128:(w0["oa"] + n) * 128]
                    S.op("dve", lambda: V.tensor_tensor(out=pm[:, :, 0:nw], in0=px[:, :, 0:nw], in1=esl, op=ALU.mult),
                         reads=[bpx, bE], writes=[bpm])
                    c0 = (qa - 4 * qg) * 128
                    for hh in range(2):
                        ot, bot = bk[4 + hh], bkb[4 + hh]
                        S.mm([lambda hh=hh, ot=ot: PE.matmul(ot[:, c0:c0 + nw], lhsT=Vaug[:, kb, hh, :], rhs=pm[:, hh, 0:nw],
                                                             start=w0["first"], stop=w0["last"])],
                             reads=[bpm, bVa[kb // 4]], writes=[bot])
                    if w0["last"]:
                        S.op("act", lambda: A.copy(out=Oacc[0][0][:, qg * 512:(qg + 1) * 512], in_=bk[4]), reads=[bkb[4]], writes=[Oacc[0][1]])
                        S.op("dve", lambda: V.tensor_copy(out=Oacc[1][0][:, qg * 512:(qg + 1) * 512], in_=bk[5]), reads=[bkb[5]], writes=[Oacc[1][1]])

                for i in range(4):
                    emit_qk(items[i])
                for i in range(0, len(items), 2):
                    if i + 4 < len(items):
                        emit_qk(items[i + 4])
                        emit_qk(items[i + 5])
                    emit_rest2(items[i], items[i + 1])

                def far_qk(w):
                    hh, cg, gi = w["hh"], w["cg"], w["gi"]
                    r = slice(hh * 64, hh * 64 + 64)
                    sp_, bsp = bk[SBK[gi % NSB]], bkb[SBK[gi % NSB]]
                    fns = []
                    for k in range(2):
                        cc = 2 * cg + k
                        for f in range(2):
                            fns.append(lambda cc=cc, f=f, k=k: PE.matmul(
                                sp_[:, (2 * k + f) * 128:(2 * k + f + 1) * 128], lhsT=KT[r, 2048 * f + cc:2048 * (f + 1):16],
                                rhs=QT[r, cc:2048:16], start=True, stop=True))
                    S.mm(fns, reads=bKT + bQT, writes=[bsp])

                def far_rest2(w0, w1):
                    cg, gi = w0["cg"], w0["gi"]
                    pv = pairview(gi)
                    bsp0, bsp1 = bkb[SBK[gi % NSB]], bkb[SBK[(gi + 1) % NSB]]
                    px, bpx = pex[(gi // 2) % NPB]
                    pm, bpm = pmk[(gi // 2) % NPB]
                    S.op("act", lambda: A.activation(out=px, in_=pv, func=AF.Exp), reads=[bsp0, bsp1], writes=[bpx])
                    S.op("dve", lambda: V.tensor_tensor(out=pm, in0=px, in1=EF4, op=ALU.mult), reads=[bpx, bEF4], writes=[bpm])
                    for hh in range(2):
                        of, bof = bk[4 + hh], bkb[4 + hh]
                        fns = []
                        for k in range(2):
                            cc = 2 * cg + k
                            slot = (cg % 2) * 2 + k
                            for f in range(2):
                                fns.append(lambda cc=cc, f=f, k=k, slot=slot, hh=hh, of=of: PE.matmul(
                                    of[:, slot * 128:(slot + 1) * 128], lhsT=Vcp[:, f * 16 + cc, hh * 128:(hh + 1) * 128],
                                    rhs=pm[:, hh, (2 * k + f) * 128:(2 * k + f + 1) * 128], start=(f == 0), stop=(f == 1)))
                        S.mm(fns, reads=[bpm, bVcp], writes=[bof])
                        if cg % 2 == 1:
                            oa_, boa = Oacc[hh]
                            c0 = 2 * cg - 2
                            dst = oa_.rearrange("p (j c) -> p c j", c=16)[:, c0:c0 + 4, :]
                            S.op("dve", lambda dst=dst, of=of: V.tensor_tensor(out=dst, in0=of.rearrange("p (c j) -> p c j", c=4), in1=dst, op=ALU.add),
                                 reads=[bof, boa], writes=[boa])

                fitems = []
                for cg in range(8):
                    for hh in range(2):
                        fitems.append(dict(hh=hh, cg=cg, gi=grp_n[0]))
                        grp_n[0] += 1
                for i in range(4):
                    far_qk(fitems[i])
                for i in range(0, len(fitems), 2):
                    if i + 4 < len(fitems):
                        far_qk(fitems[i + 4])
                        far_qk(fitems[i + 5])
                    far_rest2(fitems[i], fitems[i + 1])

                def fin(half):
                    cs = slice(half * 1024, (half + 1) * 1024)
                    dt_, bdt = dtmp[half]
                    o0, bo0 = Oacc[0]
                    o1, bo1 = Oacc[1]
                    S.op("dve", lambda: V.tensor_copy(out=dt_[0:64, :], in_=o0[64:128, cs]), reads=[bo0], writes=[bdt])
                    S.op("dve", lambda: V.tensor_copy(out=dt_[64:128, :], in_=o1[0:64, cs]), reads=[bo1], writes=[bdt])
                    S.op("act", lambda: A.activation(out=dt_, in_=dt_, func=AF.Ln), reads=[bdt], writes=[bdt])
                    S.op("act", lambda: A.activation(out=dt_, in_=dt_, func=AF.Exp, scale=-1.0), reads=[bdt], writes=[bdt])
                    S.op("dve", lambda: V.tensor_tensor(out=dt_[0:64, :], in0=o0[0:64, cs], in1=dt_[0:64, :], op=ALU.mult),
                         reads=[bo0, bdt], writes=[bdt])
                    S.op("dve", lambda: V.tensor_tensor(out=dt_[64:128, :], in0=o1[64:128, cs], in1=dt_[64:128, :], op=ALU.mult),
                         reads=[bo1, bdt], writes=[bdt])
                    S.op("dve", lambda: V.tensor_tensor(out=mixT[:, 4 + hp, cs], in0=dt_, in1=GTs[:, cs], op=ALU.mult),
                         reads=[bdt] + bGT[2 * half:2 * half + 2], writes=[bmix[4 + hp][q] for q in range(8 * half, 8 * half + 8)])

                for half in range(2):
                    fin(half)

            for hp in range(4):
                S.replay_rr([S.record(setup, hp)])
                S.replay_rr([S.record(mainloop, hp)])
            S.barrier()

        with ExitStack() as es:
            S.rec = []
            qkv, bqkv = sbt(es, "qkv", [4, 3, 512])
            for t in range(3):
                S.op("dve", lambda t=t: V.tensor_copy(out=qkv[:, t, :].rearrange("p (h n) -> p h n", h=4), in_=hsB[:, :, t * 128:(t + 1) * 128]),
                     reads=[bhsB], writes=[bqkv])
            S.dma("sp", kn_o, qkv[:, 1, :], reads=[bqkv])
            S.dma("sp", vn_o, qkv[:, 2, :], reads=[bqkv])
            bscr = Buf("scr")
            S.dma("sp", scr_q, qkv[:, 0, :], reads=[bqkv], writes=[bscr])
            bdt_, bbd = sbt(es, "bdt", [8, 512])
            S.dma("sp", bdt_, bd_d, writes=[bbd])
            bdect, bbdec = sbt(es, "bdect", [128, 3, 8])
            for s in range(3):
                S.dma("sp", bdect[:, s, :], bdec[s], writes=[bbdec])
            b0t, bb0 = sbt(es, "b0t", [4, 8])
            S.dma("sp", b0t, b0.partition_broadcast(4), writes=[bb0])
            pr0, bpr0 = sbt(es, "pr0", [4, 512])
            l0, bl0 = sbt(es, "l0", [4, 8])
            S.op("dve", lambda: V.tensor_tensor(out=pr0, in0=qkv[:, 0, :], in1=qkv[:, 1, :], op=ALU.mult), reads=[bqkv], writes=[bpr0])
            S.op("dve", lambda: V.tensor_reduce(out=l0, in_=pr0.rearrange("p (h d) -> p h d", h=8), axis=AX.X, op=ALU.add),
                 reads=[bpr0], writes=[bl0])
            S.op("dve", lambda: V.scalar_tensor_tensor(out=l0, in0=l0, scalar=0.125, in1=b0t, op0=ALU.mult, op1=ALU.add),
                 reads=[bl0, bb0], writes=[bl0])
            S.op("act", lambda: A.activation(out=l0, in_=l0, func=AF.Exp), reads=[bl0], writes=[bl0])
            S.op("dve", lambda: V.tensor_scalar(out=l0, in0=l0, scalar1=3.0, scalar2=None, op0=ALU.mult), reads=[bl0], writes=[bl0])
            Kg = [sbt(es, "Kg%d" % i, [128, 512]) for i in range(3)]
            Vg = [sbt(es, "Vg%d" % i, [128, 512]) for i in range(3)]
            qbc, bqbc = sbt(es, "qbc", [128, 512])
            prod, bprod = sbt(es, "prod", [128, 512])
            lg, blg = sbt(es, "lg", [128, 3, 8])
            p0m, bp0m = sbt(es, "p0m", [4, 8])
            rd, brd = sbt(es, "rd", [8, 1])
            om, bom = sbt(es, "om", [8, 512])
            obp, bobp = bk[6], bkb[6]
            for b in range(4):
                S.dma("sp", qbc, scr_q[b:b + 1, :].partition_broadcast(128), reads=[bscr], writes=[bqbc])
                for si, sg_ in enumerate((1, 4, 16)):
                    r0 = 2048 - 128 * sg_
                    S.dma("sp", Kg[si][0], ck[b, r0:2048:sg_, :], writes=[Kg[si][1]])
                    S.dma("sp", Vg[si][0], cv[b, r0:2048:sg_, :], writes=[Vg[si][1]])
                    S.op("dve", lambda si=si: V.tensor_tensor(out=prod, in0=Kg[si][0], in1=qbc, op=ALU.mult),
                         reads=[Kg[si][1], bqbc], writes=[bprod])
                    S.op("dve", lambda si=si: V.tensor_reduce(out=lg[:, si, :], in_=prod.rearrange("p (h d) -> p h d", h=8), axis=AX.X, op=ALU.add),
                         reads=[bprod], writes=[blg])
                S.op("dve", lambda: V.scalar_tensor_tensor(out=lg, in0=lg, scalar=0.125, in1=bdect, op0=ALU.mult, op1=ALU.add),
                     reads=[blg, bbdec], writes=[blg])
                S.op("act", lambda: A.activation(out=lg, in_=lg, func=AF.Exp), reads=[blg], writes=[blg])
                S.op("dve", lambda b=b: V.tensor_scalar(out=p0m, in0=l0, scalar1=identf[0:4, b:b + 1], scalar2=None, op0=ALU.mult),
                     reads=[bl0, bidf], writes=[bp0m])
                ops_, bops_ = bk[7], bkb[7]
                fns = [lambda si=si: PE.matmul(ops_[0:8, :], lhsT=lg[:, si, :], rhs=Vg[si][0], start=(si == 0), stop=False) for si in range(3)]
                fns.append(lambda: PE.matmul(ops_[0:8, :], lhsT=p0m, rhs=qkv[:, 2, :], start=False, stop=True))
                S.mm(fns, reads=[blg, bp0m, bqkv] + [Vg[si][1] for si in range(3)], writes=[bops_])
                dps, bdps = bk[6][:, 32:64], bkb[6]
                fns = [lambda si=si: PE.matmul(dps[0:8, 0:1], lhsT=lg[:, si, :], rhs=ones[:, 0:1], start=(si == 0), stop=False) for si in range(3)]
                fns.append(lambda: PE.matmul(dps[0:8, 0:1], lhsT=p0m, rhs=ones[0:4, 0:1], start=False, stop=True))
                S.mm(fns, reads=[blg, bp0m, bones], writes=[bdps])
                S.op("dve", lambda: V.reciprocal(out=rd, in_=dps[0:8, 0:1]), reads=[bdps], writes=[brd])
                S.op("dve", lambda: V.scalar_tensor_tensor(out=om, in0=ops_[0:8, :], scalar=rd[:, 0:1], in1=bdt_, op0=ALU.mult, op1=ALU.mult),
                     reads=[bops_, brd, bbd], writes=[bom])
                S.mm([lambda ch=ch, b=b: PE.matmul(obp[:, ch * 4 + b:ch * 4 + b + 1], lhsT=om[:, ch * 128:(ch + 1) * 128], rhs=ones[0:8, 0:1],
                                                   start=True, stop=True) for ch in range(4)], reads=[bom, bones], writes=[bobp])
            S.op("dve", lambda: V.tensor_tensor(out=mixTs[:, 4:8, :], in0=obp[:, 0:16].rearrange("p (c t) -> p c t", c=4), in1=GTss, op=ALU.mult),
                 reads=[bobp, bGTss], writes=[bmixs])
            rec3b, S.rec = S.rec, None

            Wo, bWo = sbt(es, "Wo", [128, 8, D], BF16)
            stgO = [sbt(es, "stgO%d" % i, [128, D]) for i in range(2)]
            for c in range(8):
                sg, bsg = stgO[c % 2]
                S.dma("sp", sg, w_out[c * 128:(c + 1) * 128, :], writes=[bsg])
                if c % 2 == 0:
                    S.op("dve", lambda c=c, sg=sg: V.tensor_scalar(out=Wo[:, c, :], in0=sg, scalar1=1.0 / ALPHA, scalar2=None, op0=ALU.mult),
                         reads=[bsg], writes=[bWo])
                else:
                    S.op("act", lambda c=c, sg=sg: A.mul(out=Wo[:, c, :], in_=sg, mul=1.0 / ALPHA), reads=[bsg], writes=[bWo])
            lngt, blng = sbt(es, "lngt", [128, D])
            lnbt, blnb = sbt(es, "lnbt", [128, D])
            S.dma("sp", lngt, lng.partition_broadcast(128), writes=[blng])
            S.dma("sp", lnbt, lnb.partition_broadcast(128), writes=[blnb])
            NXR = 5
            xr = [sbt(es, "xr%d" % i, [128, D]) for i in range(NXR)]
            rr = [sbt(es, "rr%d" % i, [128, D]) for i in range(3)]
            sqrs = [sbt(es, "sqr%d" % i, [128, D]) for i in range(2)]
            stt = [sbt(es, "stt%d" % i, [128, 8]) for i in range(3)]
            xhl = [sbt(es, "xhl%d" % i, [128, 2, D], BF16) for i in range(2)]

            def x_load(P, it, x_src):
                xt, bxt = xr[it % NXR]
                S.dma("sp", xt[0:P, :], x_src, writes=[bxt])

            def out_block(P, it, lhs_fn, mix_bufs, x_src, y_dst):
                xt, bxt = xr[it % NXR]
                rt, brt = rr[it % 3]
                s8, bs8 = stt[it % 3]
                sqr, bsqr = sqrs[it % 2]
                xh, bxh = xhl[it % 2]
                S.op("pool", lambda: G.tensor_copy(out=xh[0:P, 0, :], in_=xt[0:P, :]), reads=[bxt], writes=[bxh])
                S.op("dve", lambda: V.tensor_tensor(out=xh[0:P, 1, :], in0=xt[0:P, :], in1=xh[0:P, 0, :], op=ALU.subtract),
                     reads=[bxt, bxh], writes=[bxh])
                yps = []
                for half in range(2):
                    yp, byp = bk[2 * (it % 3) + half], bkb[2 * (it % 3) + half]
                    yps.append((yp, byp))
                    fns = [lambda c=c, yp=yp, half=half: PE.matmul(yp[0:P, :], lhsT=lhs_fn(c), rhs=Wo[:, c, half * 512:(half + 1) * 512],
                                                                   start=(c == 0), stop=False) for c in range(8)]
                    fns.append(lambda yp=yp, half=half: PE.matmul(yp[0:P, :], lhsT=ident[0:P, 0:P], rhs=xh[0:P, 0, half * 512:(half + 1) * 512],
                                                                  start=False, stop=False))
                    fns.append(lambda yp=yp, half=half: PE.matmul(yp[0:P, :], lhsT=ident[0:P, 0:P], rhs=xh[0:P, 1, half * 512:(half + 1) * 512],
                                                                  start=False, stop=True))
                    S.mm(fns, reads=[bWo, bxh, bid] + mix_bufs, writes=[byp])
                for half in range(2):
                    yp, byp = yps[half]
                    S.op("act", lambda yp=yp, half=half: A.activation(out=sqr[0:P, 0:512], in_=yp[0:P, :], func=AF.Copy,
                                                                      accum_out=s8[0:P, half:half + 1]),
                         reads=[byp], writes=[bsqr, bs8])
                    S.op("act", lambda yp=yp, half=half: A.activation(out=sqr[0:P, 512:1024], in_=yp[0:P, :], func=AF.Square,
                                                                      accum_out=s8[0:P, 2 + half:3 + half]),
                         reads=[byp], writes=[bsqr, bs8])
                S.op("dve", lambda: V.tensor_reduce(out=s8[0:P, 4:6], in_=s8[0:P, 0:4].rearrange("p (a b) -> p a b", a=2), axis=AX.X, op=ALU.add),
                     reads=[bs8], writes=[bs8])
                S.op("dve", lambda: V.tensor_scalar(out=s8[0:P, 4:6], in0=s8[0:P, 4:6], scalar1=1.0 / D, scalar2=None, op0=ALU.mult),
                     reads=[bs8], writes=[bs8])
                S.op("dve", lambda: V.tensor_tensor(out=s8[0:P, 6:7], in0=s8[0:P, 4:5], in1=s8[0:P, 4:5], op=ALU.mult), reads=[bs8], writes=[bs8])
                S.op("dve", lambda: V.tensor_tensor(out=s8[0:P, 6:7], in0=s8[0:P, 5:6], in1=s8[0:P, 6:7], op=ALU.subtract), reads=[bs8], writes=[bs8])
                S.op("act", lambda: A.activation(out=s8[0:P, 7:8], in_=s8[0:P, 6:7], func=AF.Ln, bias=LN_EPS / (ALPHA * ALPHA), scale=1.0), reads=[bs8], writes=[bs8])
                S.op("act", lambda: A.activation(out=s8[0:P, 7:8], in_=s8[0:P, 7:8], func=AF.Exp, scale=-0.5), reads=[bs8], writes=[bs8])
                S.op("dve", lambda: V.scalar_tensor_tensor(out=s8[0:P, 6:7], in0=s8[0:P, 4:5], scalar=-1.0, in1=s8[0:P, 7:8],
                                                           op0=ALU.mult, op1=ALU.mult), reads=[bs8], writes=[bs8])
                for half in range(2):
                    yp, byp = yps[half]
                    S.op("act", lambda yp=yp, half=half: A.activation(
                        out=rt[0:P, half * 512:(half + 1) * 512], in_=yp[0:P, :], func=AF.Identity,
                        bias=s8[0:P, 6:7], scale=s8[0:P, 7:8]), reads=[byp, bs8], writes=[brt])
                S.op("dve", lambda: V.tensor_tensor(out=rt[0:P, :], in0=rt[0:P, :], in1=lngt[0:P, :], op=ALU.mult), reads=[brt, blng], writes=[brt])
                S.op("pool", lambda: G.tensor_tensor(out=rt[0:P, :], in0=rt[0:P, :], in1=lnbt[0:P, :], op=ALU.add), reads=[brt, blnb], writes=[brt])
                S.dma("sp", y_dst, rt[0:P, :], reads=[brt])

            def xsrc(tb):
                return xloc[NOWN + tb * 128:NOWN + (tb + 1) * 128, :]

            for tb in range(NXR - 1):
                x_load(128, tb, xsrc(tb))
            recs = []
            for tb in range(16):
                recs.append(S.record(out_block, 128, tb, (lambda c, tb=tb: mixT[:, c, tb * 128:(tb + 1) * 128]),
                                     [bmix[c][tb] for c in range(8)], None, y_o[tb * 128:(tb + 1) * 128, :]))
            parts = []
            for q in recs:
                n = len(q)
                parts.append([q[(n * k) // 3:(n * (k + 1)) // 3] for k in range(3)])
            for step in range(16 + 2):
                nb_ = step + NXR - 1
                if nb_ < 16:
                    x_load(128, nb_, xsrc(nb_))
                elif nb_ == 16:
                    x_load(4, 16, xs)
                n3 = len(rec3b)
                ch3 = rec3b[(n3 * step) // 18:(n3 * (step + 1)) // 18]
                S.replay_rr([parts[step - k][k] for k in (2, 1, 0) if 0 <= step - k < 16] + [ch3])
            out_block(4, 16, lambda c: mixTs[:, c, :], [bmixs], xs, ys_o)
            S.barrier()
    except _Stop:
        pass
    S.finish()
    pass
    return nc


def _t5_bucket(dist):
    dist = np.asarray(dist, np.int32)
    d = np.maximum(dist.astype(np.float32), np.float32(1.0))
    lb = np.float32(16) + np.log(d / np.float32(16)) / np.float32(math.log(2048 / 16)) * np.float32(16)
    lb = np.minimum(lb.astype(np.int32), 31)
    return np.where(dist < 16, dist, lb).astype(np.int64)


def _static_tables():
    i = np.arange(128)[:, None]
    j = np.arange(128)[None, :]
    dmat = np.stack([128 * o + j - i for o in range(NOFF)], 0)
    cnt = ((dmat >= 0) & (dmat <= 128)).astype(np.int64) \
        + ((dmat >= 0) & (dmat <= 512) & (dmat % 4 == 0)) + ((dmat >= 0) & (dmat <= 512) & (dmat % 16 == 0))
    valid = cnt > 0
    bucket = _t5_bucket(np.clip(dmat, 0, 2048))
    logc = np.where(valid, np.log(np.maximum(cnt, 1)), 0.0).astype(np.float32)
    return dmat, valid, bucket, logc


_NC = None


def kernel(x_prompt, x_sample, state_gla, cache_k_win, cache_v_win, w_in, w_alpha2, b_alpha,
           gla_norm_g, w_out, ln_g, ln_b, rel_bias):
    global _NC
    f = lambda a: np.ascontiguousarray(np.asarray(a), dtype=np.float32)
    x_prompt, x_sample, state_gla = f(x_prompt), f(x_sample), f(state_gla)
    cache_k_win, cache_v_win = f(cache_k_win), f(cache_v_win)
    w_in, w_alpha2, b_alpha, gla_norm_g = f(w_in)[0], f(w_alpha2)[0], f(b_alpha), f(gla_norm_g)
    w_out, ln_g, ln_b, rel_bias = f(w_out)[0], f(ln_g), f(ln_b), f(rel_bias)

    dmat, valid, bucket, logc = _static_tables()
    gat = rel_bias[bucket]
    gat = np.where(valid[..., None], gat, np.float32(NEG))
    biasT = np.ascontiguousarray(gat.transpose(3, 1, 0, 2).reshape(8, 128, NOFF * 128), dtype=np.float32)
    logc2 = np.ascontiguousarray(logc.transpose(1, 0, 2).reshape(128, NOFF * 128), dtype=np.float32)
    ii = np.arange(128)
    nA = 128 + ii[None, :] - ii[:, None]
    nB = ii[None, :] - ii[:, None]
    nf = np.concatenate([nA, nB], 1)
    vf = (nf >= 33) & (nf <= 128)
    gf = rel_bias[_t5_bucket(np.clip(nf, 0, 128) * 16)]
    biasF = np.ascontiguousarray(np.where(vf[..., None], gf, np.float32(NEG)).transpose(2, 0, 1), dtype=np.float32)
    bdec = np.stack([rel_bias[_t5_bucket((128 - ii) * s)] for s in (1, 4, 16)], 0).astype(np.float32)
    b0 = rel_bias[_t5_bucket(np.array([0]))].astype(np.float32)
    tri = (ii[:, None] <= ii[None, :]).astype(np.float32)
    ident = np.eye(128, dtype=np.float32)
    bd = np.zeros((8, 512), np.float32)
    for h in range(8):
        bd[h, h * 64:(h + 1) * 64] = 1.0
    oh4 = np.zeros((128, 16), np.float32)
    for b in range(4):
        oh4[:, 4 * b + b] = 1.0

    if _NC is None:
        _NC = build()
    in_maps = []
    for c in range(NCORES):
        b, half = c // 2, c % 2
        if half == 0:
            xloc = np.concatenate([np.zeros((NOWN, D), np.float32), x_prompt[b, :NOWN]], 0)
        else:
            xloc = x_prompt[b]
        sl = slice(4 * c, 4 * c + 4)
        in_maps.append({
            "xloc": np.ascontiguousarray(xloc), "xs": np.ascontiguousarray(x_sample[sl, 0]),
            "st": np.ascontiguousarray(state_gla[0, sl]),
            "ck": np.ascontiguousarray(cache_k_win[0, sl].reshape(4, 2048, 512)),
            "cv": np.ascontiguousarray(cache_v_win[0, sl].reshape(4, 2048, 512)),
            "w_in": w_in, "w2": w_alpha2, "ba": b_alpha, "gng": gla_norm_g, "w_out": w_out, "lng": ln_g, "lnb": ln_b,
            "biasT": biasT, "logc": logc2, "bdec": bdec, "b0": b0, "biasF": biasF,
            "cm": np.full((128, 1), float(half), np.float32),
            "tri": tri, "ident": ident, "bd": bd, "oh4": oh4,
        })
    res = run_bass_kernel_spmd(_NC, in_maps, core_ids=list(range(NCORES))).results
    y_p = np.stack([np.concatenate([res[2 * b]["y_o"], res[2 * b + 1]["y_o"]], 0) for b in range(4)], 0)
    y_s = np.concatenate([res[c]["ys_o"] for c in range(NCORES)], 0)[:, None, :]
    s_p = np.stack([res[2 * b + 1]["sp_o"] for b in range(4)], 0)[None]
    s_s = np.concatenate([res[c]["ss_o"] for c in range(NCORES)], 0)[None]
    k_p = np.stack([res[2 * b + 1]["kw_o"].reshape(NOWN, 8, 64) for b in range(4)], 0)[None]
    v_p = np.stack([res[2 * b + 1]["vw_o"].reshape(NOWN, 8, 64) for b in range(4)], 0)[None]
    k_n = np.concatenate([res[c]["kn_o"] for c in range(NCORES)], 0).reshape(32, 1, 8, 64)[None]
    v_n = np.concatenate([res[c]["vn_o"] for c in range(NCORES)], 0).reshape(32, 1, 8, 64)[None]
    return (y_p.astype(np.float32), y_s.astype(np.float32), s_p.astype(np.float32), s_s.astype(np.float32),
            k_p.astype(np.float32), v_p.astype(np.float32), k_n.astype(np.float32), v_n.astype(np.float32))
```

```python
from contextlib import ExitStack
import math
import numpy as np
import concourse.bass as bass
import concourse.mybir as mybir
from concourse.bass_utils import run_bass_kernel_spmd

F32 = mybir.dt.float32
BF16 = mybir.dt.bfloat16
AF = mybir.ActivationFunctionType
ALU = mybir.AluOpType
AX = mybir.AxisListType

NCORES = 8
D = 1024
DIN = 3600
NLOC = 4096
NOWN = 2048
NB = 32
NOFF = 5
ALPHA = 2.0 ** 0.25
LN_EPS = 1e-5
RMS_EPS = 1e-6
NEG = -30000.0


class _Stop(Exception):
    pass


class Buf:
    __slots__ = ("name", "w", "rs", "excl")

    def __init__(self, name="", excl=False):
        self.name = name
        self.w = None
        self.rs = {}
        self.excl = excl


class Sched:
    def __init__(self, nc, nds=48):
        self.nc = nc
        self.engs = {"pe": nc.tensor, "act": nc.scalar, "dve": nc.vector,
                     "pool": nc.gpsimd, "sp": nc.sync}
        self.csem = {e: nc.alloc_semaphore("c_" + e) for e in ("pe", "act", "dve", "pool")}
        self.ccnt = {e: 0 for e in self.csem}
        self.NDS = nds
        self.dsem = [nc.alloc_semaphore("d%d" % i) for i in range(nds)]
        self.dcnt = [0] * nds
        self.dnext = 0
        self.seen = {e: {} for e in self.engs}
        self.snaps = {}
        self.rec = None

    def _need(self, e, ev, acc):
        if ev is None:
            return
        key, sem, val = ev
        if e == "pe" and key == "pe":
            return
        if self.seen[e].get(key, 0) >= val:
            return
        for i, (k2, s2, v2) in enumerate(acc):
            if k2 == key:
                if val > v2:
                    acc[i] = (key, sem, val)
                return
        acc.append((key, sem, val))

    def _collect(self, e, reads, writes):
        acc = []
        for b in reads:
            self._need(e, b.w, acc)
            if b.excl:
                for k, (sem, val) in b.rs.items():
                    if k != e:
                        self._need(e, (k, sem, val), acc)
        for b in writes:
            self._need(e, b.w, acc)
            for k, (sem, val) in b.rs.items():
                self._need(e, (k, sem, val), acc)
        acc.sort(key=lambda t: -t[2])
        keep = []
        implied = {}
        for (key, sem, val) in acc:
            if implied.get(key, 0) >= val:
                continue
            keep.append((key, sem, val))
            snap = self.snaps.get((key, val))
            if snap:
                for k2, v2 in snap.items():
                    if implied.get(k2, 0) < v2:
                        implied[k2] = v2
        return keep

    def _mark(self, e, w):
        key, sem, val = w
        se = self.seen[e]
        if se.get(key, 0) < val:
            se[key] = val
        snap = self.snaps.get((key, val))
        if snap:
            for k2, v2 in snap.items():
                if k2 != e and se.get(k2, 0) < v2:
                    se[k2] = v2

    def _wait(self, e, ev):
        acc = []
        self._need(e, ev, acc)
        for w in acc:
            self.engs[e].wait_ge(w[1], w[2])
            self._mark(e, w)

    def _commit(self, e, ev, reads, writes):
        key, sem, val = ev
        snap = dict(self.seen[e])
        snap.pop(e, None) if e in ("act", "dve", "pool") else None
        self.snaps[(key, val)] = snap
        for b in reads:
            b.rs[key] = (sem, val)
        for b in writes:
            b.w = ev
            b.rs = {}

    def _emit(self, e, fn, reads, writes):
        ws = self._collect(e, reads, writes)
        for w in ws[:-1]:
            self.engs[e].wait_ge(w[1], w[2])
        ins = fn()
        if ws:
            ins._wait_ge(ws[-1][1], ws[-1][2])
        for w in ws:
            self._mark(e, w)
        return ins

    def record(self, f, *args):
        self.rec = []
        f(*args)
        q, self.rec = self.rec, None
        return q

    def replay_rr(self, queues):
        qs = [list(q) for q in queues if q]
        idx = [0] * len(qs)
        live = True
        while live:
            live = False
            for i, q in enumerate(qs):
                if idx[i] < len(q):
                    kind, a, b, r, w = q[idx[i]]
                    idx[i] += 1
                    live = True
                    if kind == "op":
                        self.op(a, b, r, w)
                    elif kind == "dma":
                        self.dma(a, b[0], b[1], r, w)
                    else:
                        self.mm(b, r, w)

    def pipeline(self, recs, nst):
        parts = []
        for q in recs:
            n = len(q)
            cuts = [(n * k) // nst for k in range(nst + 1)]
            parts.append([q[cuts[k]:cuts[k + 1]] for k in range(nst)])
        for step in range(len(recs) + nst - 1):
            qs = []
            for k in range(nst - 1, -1, -1):
                if 0 <= step - k < len(recs):
                    qs.append(parts[step - k][k])
            self.replay_rr(qs)

    def op(self, e, fn, reads=(), writes=()):
        if self.rec is not None:
            self.rec.append(("op", e, fn, list(reads), list(writes)))
            return
        ins = self._emit(e, fn, reads, writes)
        self.ccnt[e] += 1
        ins.then_inc(self.csem[e], 1)
        self._commit(e, (e, self.csem[e], self.ccnt[e]), reads, writes)

    def mm(self, fns, reads=(), writes=()):
        if self.rec is not None:
            self.rec.append(("mm", "pe", list(fns), list(reads), list(writes)))
            return
        ins = self._emit("pe", fns[0], reads, writes)
        for f in fns[1:]:
            ins = f()
        self.ccnt["pe"] += 1
        ins.then_inc(self.csem["pe"], 1)
        self._commit("pe", ("pe", self.csem["pe"], self.ccnt["pe"]), reads, writes)

    def dma(self, q, out, in_, reads=(), writes=()):
        if self.rec is not None:
            self.rec.append(("dma", q, (out, in_), list(reads), list(writes)))
            return
        i = self.dnext
        self.dnext = (self.dnext + 1) % self.NDS
        if self.dcnt[i] > 0:
            self._wait(q, (("d", i), self.dsem[i], self.dcnt[i]))
        for w in self._collect(q, reads, writes):
            self.engs[q].wait_ge(w[1], w[2])
            self._mark(q, w)
        self.dcnt[i] += 16
        self.engs[q].dma_start(out=out, in_=in_).then_inc(self.dsem[i], 16)
        self._commit(q, (("d", i), self.dsem[i], self.dcnt[i]), reads, writes)

    def barrier(self):
        for e in self.engs:
            for f in self.csem:
                if self.ccnt[f] > 0:
                    self._wait(e, (f, self.csem[f], self.ccnt[f]))
            for i in range(self.NDS):
                if self.dcnt[i] > 0:
                    self._wait(e, (("d", i), self.dsem[i], self.dcnt[i]))

    def finish(self):
        for i in range(self.NDS):
            if self.dcnt[i] > 0:
                self._wait("sp", (("d", i), self.dsem[i], self.dcnt[i]))


def build():
    nc = bass.Bass("TRN2", target_bir_lowering=False)
    V, A, G, PE = nc.vector, nc.scalar, nc.gpsimd, nc.tensor
    S = Sched(nc)

    def din(n, s):
        return nc.dram_tensor(n, s, F32, kind="ExternalInput").ap()

    def dout(n, s):
        return nc.dram_tensor(n, s, F32, kind="ExternalOutput").ap()

    xloc = din("xloc", [NLOC, D]); xs = din("xs", [4, D]); st = din("st", [4, 4, 64, 128])
    ck = din("ck", [4, 2048, 512]); cv = din("cv", [4, 2048, 512])
    w_in = din("w_in", [D, DIN]); w2 = din("w2", [16, 256]); ba = din("ba", [1, 256])
    gng = din("gng", [1, 128]); w_out = din("w_out", [D, D]); lng = din("lng", [1, D]); lnb = din("lnb", [1, D])
    biasT = din("biasT", [8, 128, NOFF * 128]); logc = din("logc", [128, NOFF * 128])
    bdec = din("bdec", [3, 128, 8]); b0 = din("b0", [1, 8]); cm = din("cm", [128, 1])
    biasF = din("biasF", [8, 128, 256])
    vscr = nc.dram_tensor("vscr", [NLOC, 256], BF16).ap()
    tri_d = din("tri", [128, 128]); ident_d = din("ident", [128, 128]); bd_d = din("bd", [8, 512])
    oh4_d = din("oh4", [128, 16])

    y_o = dout("y_o", [NOWN, D]); sp_o = dout("sp_o", [4, 64, 128])
    kw_o = dout("kw_o", [NOWN, 512]); vw_o = dout("vw_o", [NOWN, 512])
    ys_o = dout("ys_o", [4, D]); ss_o = dout("ss_o", [4, 4, 64, 128])
    kn_o = dout("kn_o", [4, 512]); vn_o = dout("vn_o", [4, 512])
    scr_q = nc.dram_tensor("scr_q", [4, 512], F32).ap()

    bkA = nc.alloc_psum_tensor("bkA", [128, 2048], F32).ap()
    bk = [bkA[:, i * 512:(i + 1) * 512] for i in range(4)]
    bk += [nc.alloc_psum_tensor("bk%d" % i, [128, 512], F32).ap() for i in (4, 5)]
    bkb = [Buf("bk%d" % i, True) for i in range(6)]

    top = ExitStack()

    def sbt(es, name, shape, dt=F32):
        return es.enter_context(nc.sbuf_tensor("s_" + name, shape, dt)).ap(), Buf(name)

    bxT = [Buf("xT%d" % i) for i in range(NB)]
    xTd = nc.dram_tensor("xTd", [8, 128, 8 * 512], BF16).ap()
    bxTd = Buf("xTd")
    mixT, _ = sbt(top, "mixT", [128, 8, NOWN], BF16)
    bmix = [[Buf("mix%d_%d" % (c, t)) for t in range(16)] for c in range(8)]
    xsT, bxsT = sbt(top, "xsT", [128, 8, 4], BF16)
    mixTs, bmixs = sbt(top, "mixTs", [128, 8, 4], BF16)
    identf, bidf = sbt(top, "identf", [128, 128])
    ident, bid = sbt(top, "ident", [128, 128], BF16)
    tri, btri = sbt(top, "tri", [128, 128])
    trin, btrin = sbt(top, "trin", [128, 128])
    mask4, bmask4 = sbt(top, "mask4", [128, 512], BF16)
    gngb, bgng = sbt(top, "gngb", [128, 512])
    cmt, bcm = sbt(top, "cmt", [128, 1])
    oh4, boh4 = sbt(top, "oh4", [128, 16])
    ones, bones = sbt(top, "ones", [128, 8])
    hsB, bhsB = sbt(top, "hsB", [4, 4, 512])
    GTss, bGTss = sbt(top, "GTss", [128, 4, 4])

    S.dma("sp", identf, ident_d, writes=[bidf])
    S.dma("sp", tri, tri_d, writes=[btri])
    S.dma("sp", cmt, cm, writes=[bcm])
    S.dma("sp", oh4, oh4_d, writes=[boh4])
    for h in range(4):
        S.dma("sp", gngb[:, h * 128:(h + 1) * 128], gng.partition_broadcast(128), writes=[bgng])
    S.op("pool", lambda: G.tensor_copy(out=ident, in_=identf), reads=[bidf], writes=[bid])
    S.op("pool", lambda: G.tensor_scalar(out=trin, in0=tri, scalar1=-1.0 / 16.0, scalar2=None, op0=ALU.mult),
         reads=[btri], writes=[btrin])
    for h in range(4):
        S.op("pool", lambda h=h: G.tensor_copy(out=mask4[:, h * 128:(h + 1) * 128], in_=tri), reads=[btri], writes=[bmask4])
    S.op("pool", lambda: G.memset(ones, 1.0), writes=[bones])

    try:
        esx = ExitStack()
        xT, _ = sbt(esx, "xT", [128, 8, NLOC], BF16)
        pb = [esx.enter_context(nc.psum_tensor("pb%d" % i, [128, 1024], BF16)).ap() for i in range(2)]
        pbb = [Buf("pb%d" % i, True) for i in range(2)]
        def xTr(sb):
            return bxT[sb * 4:(sb + 1) * 4]

        with ExitStack() as es:
            WA, bWA = sbt(es, "WA", [128, 8, 1552], BF16)
            w2f, bw2f = sbt(es, "w2f", [32, 256])
            W2a, bW2a = sbt(es, "W2a", [32, 256], BF16)
            S.op("pool", lambda: G.memset(w2f, 0.0), writes=[bw2f])
            S.dma("sp", w2f[0:16, :], w2, writes=[bw2f])
            S.dma("sp", w2f[16:17, :], ba, writes=[bw2f])
            S.op("pool", lambda: G.tensor_copy(out=W2a, in_=w2f), reads=[bw2f], writes=[bW2a])
            zaug = [sbt(es, "zaug%d" % i, [32, 512], BF16) for i in range(2)]
            for z, bz in zaug:
                S.op("pool", lambda z=z: G.memset(z, 1.0), writes=[bz])
            qTr2 = [sbt(es, "qTr%d" % i, [128, 2, 512]) for i in range(2)]
            Sst = [sbt(es, "Sst%d" % p, [128, 128]) for p in range(2)]
            Sbf = [sbt(es, "Sbf%d" % p, [128, 128], BF16) for p in range(2)]
            stmp, bstmp = sbt(es, "stmp", [128, 128])
            for p in range(2):
                S.op("pool", lambda p=p: G.memset(Sst[p][0], 0.0), writes=[Sst[p][1]])
                S.op("pool", lambda p=p: G.memset(Sbf[p][0], 0.0), writes=[Sbf[p][1]])
            DB = 3
            vA = [sbt(es, "vA%d" % i, [128, 512], BF16) for i in range(DB)]
            eL = [sbt(es, "eL%d" % i, [128, 256]) for i in range(DB)]
            ebm = [sbt(es, "ebm%d" % i, [128, 256]) for i in range(DB)]
            kt = [sbt(es, "kt%d" % i, [128, 256], BF16) for i in range(DB)]
            ebT = [sbt(es, "ebT%d" % i, [128, 2, 128]) for i in range(DB)]
            ela = [sbt(es, "ela%d" % i, [128, 2]) for i in range(DB)]
            qtT = [sbt(es, "qtT%d" % i, [128, 2, 128], BF16) for i in range(DB)]
            ktT = [sbt(es, "ktT%d" % i, [128, 2, 128], BF16) for i in range(DB)]
            ge = [sbt(es, "ge%d" % i, [128, 512]) for i in range(DB)]
            GS = [sbt(es, "GS%d" % i, [128, 512]) for i in range(DB)]
            Am = [sbt(es, "Am%d" % i, [128, 512], BF16) for i in range(DB)]
            sq = ge
            kAr = [sbt(es, "kAr%d" % i, [128, 256]) for i in range(DB)]
            ssq = [sbt(es, "ssq%d" % i, [128, 4]) for i in range(DB)]
            mixa = [sbt(es, "mixa%d" % i, [128, 512], BF16) for i in range(DB)]

            es_p1 = ExitStack()
            NXF, NXB = 2, 2
            xf = [sbt(es_p1, "xf%d" % i, [128, D]) for i in range(NXF)]
            xb = [sbt(es_p1, "xb%d" % i, [128, D], BF16) for i in range(NXB)]

            S.dma("sp", xf[0][0][0:4, :], xs, writes=[xf[0][1]])
            S.op("pool", lambda: G.tensor_copy(out=xb[0][0][0:4, :], in_=xf[0][0][0:4, :]), reads=[xf[0][1]], writes=[xb[0][1]])
            S.mm([lambda c=c: PE.transpose(pb[0][:, c * 4:(c + 1) * 4], xb[0][0][0:4, c * 128:(c + 1) * 128], ident[0:4, 0:4])
                  for c in range(8)], reads=[xb[0][1], bid], writes=[pbb[0]])
            S.op("dve", lambda: V.tensor_copy(out=xsT, in_=pb[0][:, 0:32].rearrange("p (c t) -> p c t", c=8)),
                 reads=[pbb[0]], writes=[bxsT])
            bstg = [Buf("wstg%d" % c) for c in range(8)]
            for c in range(8):
                stg_c = xT[:, c, 0:3104].bitcast(F32)
                S.dma("sp", stg_c, w_in[c * 128:(c + 1) * 128, 0:1552], writes=[bstg[c]])
            for c in range(8):
                stg_c = xT[:, c, 0:3104].bitcast(F32)
                e = ("act", "dve", "pool", "act", "dve", "act", "dve", "pool")[c]
                if e == "pool":
                    S.op(e, lambda c=c, stg_c=stg_c: G.tensor_copy(out=WA[:, c, :], in_=stg_c), reads=[bstg[c]] + bxT[0:25], writes=[bWA])
                elif e == "act":
                    S.op(e, lambda c=c, stg_c=stg_c: A.copy(out=WA[:, c, :], in_=stg_c), reads=[bstg[c]] + bxT[0:25], writes=[bWA])
                else:
                    S.op(e, lambda c=c, stg_c=stg_c: V.tensor_copy(out=WA[:, c, :], in_=stg_c), reads=[bstg[c]] + bxT[0:25], writes=[bWA])

            def p1_block(blk):
                f, bf_ = xf[blk % NXF]
                b_, bb_ = xb[blk % NXB]
                S.dma("sp", f, xloc[blk * 128:(blk + 1) * 128, :], writes=[bf_])
                if blk % 2 == 0:
                    S.op("act", lambda: A.copy(out=b_, in_=f), reads=[bf_], writes=[bb_])
                else:
                    S.op("dve", lambda: V.tensor_copy(out=b_, in_=f), reads=[bf_], writes=[bb_])
                pt, bpt = pb[blk % 2], pbb[blk % 2]
                S.mm([lambda c=c: PE.transpose(pt[:, c * 128:(c + 1) * 128], b_[:, c * 128:(c + 1) * 128], ident)
                      for c in range(8)], reads=[bb_, bid], writes=[bpt])
                src = pt.rearrange("p (c t) -> p c t", c=8)
                dst = xT[:, :, blk * 128:(blk + 1) * 128]
                if blk % 2 == 0:
                    S.op("dve", lambda: V.tensor_copy(out=dst, in_=src), reads=[bpt], writes=[bxT[blk]])
                else:
                    S.op("act", lambda: A.copy(out=dst, in_=src), reads=[bpt], writes=[bxT[blk]])
                if blk % 4 == 3:
                    sb_ = blk // 4
                    S.dma("sp", xTd[sb_].rearrange("p (c t) -> p c t", c=8), xT[:, :, sb_ * 512:(sb_ + 1) * 512],
                          reads=bxT[sb_ * 4:sb_ * 4 + 4], writes=[bxTd])

            accn = [0]

            def acc():
                i = accn[0] % 2
                accn[0] += 1
                return bk[i], bkb[i]

            def silu_gate(P, gps, bgps, ge_t, bge, GS_t, bGS):
                S.op("act", lambda: A.activation(out=ge_t[0:P, :], in_=gps[0:P, :], func=AF.Exp, scale=-1.0),
                     reads=[bgps], writes=[bge])
                S.op("act", lambda: A.activation(out=ge_t[0:P, :], in_=ge_t[0:P, :], func=AF.Ln, bias=1.0, scale=1.0),
                     reads=[bge], writes=[bge])
                S.op("act", lambda: A.activation(out=ge_t[0:P, :], in_=ge_t[0:P, :], func=AF.Exp, scale=-1.0),
                     reads=[bge], writes=[bge])
                S.op("dve", lambda: V.tensor_tensor(out=GS_t[0:P, :], in0=gps[0:P, :], in1=ge_t[0:P, :], op=ALU.mult),
                     reads=[bgps, bge], writes=[bGS])
                S.op("pool", lambda: G.tensor_tensor(out=GS_t[0:P, :], in0=GS_t[0:P, :], in1=gngb[0:P, :], op=ALU.mult),
                     reads=[bGS, bgng], writes=[bGS])

            def rms_mix(P, ops, bops, sq_t, bsq, ssq_t, bssq, GS_t, bGS, mixa_t, bmixa):
                S.op("act", lambda: A.copy(out=sq_t[0:P, :], in_=ops[0:P, :]), reads=[bops], writes=[bsq])
                S.op("dve", lambda: V.tensor_tensor(out=sq_t[0:P, :], in0=sq_t[0:P, :], in1=sq_t[0:P, :], op=ALU.mult),
                     reads=[bsq], writes=[bsq])
                S.op("dve", lambda: V.tensor_reduce(out=ssq_t[0:P, :], in_=sq_t[0:P, :].rearrange("p (h v) -> p h v", h=4),
                                                    axis=AX.X, op=ALU.add), reads=[bsq], writes=[bssq])
                S.op("act", lambda: A.activation(out=ssq_t[0:P, :], in_=ssq_t[0:P, :], func=AF.Ln, bias=RMS_EPS, scale=1.0 / 128.0),
                     reads=[bssq], writes=[bssq])
                S.op("act", lambda: A.activation(out=ssq_t[0:P, :], in_=ssq_t[0:P, :], func=AF.Exp, scale=-0.5),
                     reads=[bssq], writes=[bssq])
                for h in range(4):
                    S.op("dve", lambda h=h: V.scalar_tensor_tensor(
                        out=mixa_t[0:P, h * 128:(h + 1) * 128], in0=ops[0:P, h * 128:(h + 1) * 128],
                        scalar=ssq_t[0:P, h:h + 1], in1=GS_t[0:P, h * 128:(h + 1) * 128], op0=ALU.mult, op1=ALU.mult),
                        reads=[bops, bssq, bGS], writes=[bmixa])


            def stage_s(sb):
                tok = slice(sb * 512, (sb + 1) * 512)
                ps, bps = acc()
                S.mm([lambda c=c: PE.matmul(ps[0:16, :], lhsT=WA[:, c, 1024:1040], rhs=xT[:, c, tok],
                                            start=(c == 0), stop=(c == 7)) for c in range(8)],
                     reads=[bWA] + xTr(sb), writes=[bps])
                za, bza = zaug[sb % 2]
                S.op("act", lambda: A.copy(out=za[0:16, :], in_=ps[0:16, :]), reads=[bps], writes=[bza])
                if sb >= 4:
                    for (dst, bdst, col0) in ((qTr2[sb % 2][0], qTr2[sb % 2][1], 0),):
                        for dc in range(2):
                            ps2, bps2 = acc()
                            S.mm([lambda c=c, ps2=ps2, dc=dc, col0=col0: PE.matmul(
                                ps2, lhsT=WA[:, c, col0 + dc * 128:col0 + (dc + 1) * 128], rhs=xT[:, c, tok],
                                start=(c == 0), stop=(c == 7)) for c in range(8)],
                                reads=[bWA] + xTr(sb), writes=[bps2])
                            if dc == 0:
                                S.op("act", lambda ps2=ps2, dst=dst, dc=dc: A.copy(out=dst[:, dc, :], in_=ps2), reads=[bps2], writes=[bdst])
                            else:
                                S.op("dve", lambda ps2=ps2, dst=dst, dc=dc: V.tensor_copy(out=dst[:, dc, :], in_=ps2), reads=[bps2], writes=[bdst])

            def stage_a(blk):
                own = blk >= 16
                i3 = blk % DB
                bt = slice(blk * 128, (blk + 1) * 128)
                g1, bg1 = acc()
                S.mm([lambda c=c: PE.matmul(g1, lhsT=xT[:, c, bt], rhs=WA[:, c, 256:768],
                                            start=(c == 0), stop=(c == 7)) for c in range(8)],
                     reads=[bWA, bxT[blk]], writes=[bg1])
                vAt, bvA = vA[i3]
                kAt, bkA = kAr[i3]
                S.op("act", lambda: A.copy(out=vAt[:, 0:256], in_=g1[:, 256:512]), reads=[bg1], writes=[bvA])
                S.op("dve", lambda: V.tensor_copy(out=kAt, in_=g1[:, 0:256]), reads=[bg1], writes=[bkA])
                g2, bg2 = acc()
                S.mm([lambda c=c: PE.matmul(g2[:, 0:256], lhsT=xT[:, c, bt], rhs=WA[:, c, 768:1024],
                                            start=(c == 0), stop=(c == 7)) for c in range(8)],
                     reads=[bWA, bxT[blk]], writes=[bg2])
                S.op("act", lambda: A.copy(out=vAt[:, 256:512], in_=g2[:, 0:256]), reads=[bg2], writes=[bvA])
                if own:
                    g3, bg3 = acc()
                    S.mm([lambda c=c: PE.matmul(g3, lhsT=xT[:, c, bt], rhs=WA[:, c, 1040:1552],
                                                start=(c == 0), stop=(c == 7)) for c in range(8)],
                         reads=[bWA, bxT[blk]], writes=[bg3])
                    silu_gate(128, g3, bg3, ge[i3][0], ge[i3][1], GS[i3][0], GS[i3][1])

            def stage_b1(blk):
                own = blk >= 16
                i3 = blk % DB
                sb, j = blk // 4, blk % 4
                za, bza = zaug[sb % 2]
                ub, bub = bk[2], bkb[2]
                S.mm([lambda: PE.matmul(ub[:, 0:256], lhsT=za[:, j * 128:(j + 1) * 128], rhs=W2a, start=True, stop=True)],
                     reads=[bza, bW2a], writes=[bub])
                eLt, beL = eL[i3]
                S.op("act", lambda: A.activation(out=eLt, in_=ub[:, 0:256], func=AF.Exp, scale=-1.0), reads=[bub], writes=[beL])
                S.op("act", lambda: A.activation(out=eLt, in_=eLt, func=AF.Ln, bias=1.0, scale=1.0), reads=[beL], writes=[beL])
                fns = [lambda: PE.matmul(ub[:, 256:512], lhsT=trin, rhs=eLt, start=True, stop=True)]
                if own:
                    fns += [lambda dc=dc: PE.matmul(ub[:, dc * 128:(dc + 1) * 128], lhsT=eLt[:, dc * 128:(dc + 1) * 128], rhs=trin,
                                                    start=True, stop=True) for dc in range(2)]
                else:
                    fns += [lambda dc=dc: PE.matmul(ub[:, dc:dc + 1], lhsT=eLt[:, dc * 128:(dc + 1) * 128], rhs=trin[:, 127:128],
                                                    start=True, stop=True) for dc in range(2)]
                S.mm(fns, reads=[btrin, beL], writes=[bub])
                ebmt, bebm = ebm[i3]
                S.op("act", lambda: A.activation(out=ebmt, in_=ub[:, 256:512], func=AF.Exp, scale=-1.0), reads=[bub], writes=[bebm])
                ktt, bkt = kt[i3]
                S.op("dve", lambda: V.tensor_tensor(out=ktt, in0=kAr[i3][0], in1=ebmt, op=ALU.mult),
                     reads=[kAr[i3][1], bebm], writes=[bkt])
                elat, bela = ela[i3]
                if own:
                    ebTt, bebT = ebT[i3]
                    bT3 = ub[:, 0:256].rearrange("p (c t) -> p c t", c=2)
                    S.op("act", lambda: A.activation(out=ebTt, in_=bT3, func=AF.Exp), reads=[bub], writes=[bebT])
                    S.op("dve", lambda: V.tensor_copy(out=elat, in_=ebTt[:, :, 127]), reads=[bebT], writes=[bela])
                    qtTt, bqtT = qtT[i3]
                    ktTt, bktT = ktT[i3]
                    qTr, bqTr = qTr2[sb % 2]
                    S.op("dve", lambda: V.scalar_tensor_tensor(out=qtTt, in0=qTr[:, :, j * 128:(j + 1) * 128], scalar=0.125,
                                                               in1=ebTt, op0=ALU.mult, op1=ALU.mult),
                         reads=[bqTr, bebT], writes=[bqtT])
                    ptk, bptk = pb[(blk + 1) % 2], pbb[(blk + 1) % 2]
                    S.mm([lambda p=p: PE.transpose(ptk[:, 768 + p * 128:768 + (p + 1) * 128], ktt[:, p * 128:(p + 1) * 128], ident)
                          for p in range(2)], reads=[bkt, bid], writes=[bptk])
                    S.op("act", lambda: A.copy(out=ktTt, in_=ptk[:, 768:1024].rearrange("p (c t) -> p c t", c=2)),
                         reads=[bptk], writes=[bktT])
                else:
                    S.op("act", lambda: A.activation(out=elat, in_=ub[:, 0:2], func=AF.Exp), reads=[bub], writes=[bela])

            def stage_b2(blk):
                own = blk >= 16
                i3 = blk % DB
                vAt, bvA = vA[i3]
                ktt, bkt = kt[i3]
                elat, bela = ela[i3]
                apsb = ((bk[5], bkb[5]), (bk[3], bkb[3]))
                if own:
                    qtTt, bqtT = qtT[i3]
                    ktTt, bktT = ktT[i3]
                    S.mm([lambda h=h: PE.matmul(apsb[h % 2][0][:, (h // 2) * 128:(h // 2 + 1) * 128],
                                                lhsT=ktTt[(h % 2) * 64:(h % 2) * 64 + 64, h // 2, :],
                                                rhs=qtTt[(h % 2) * 64:(h % 2) * 64 + 64, h // 2, :],
                                                start=True, stop=True) for h in range(4)],
                         reads=[bktT, bqtT], writes=[bkb[5], bkb[3]])
                    Amt, bAm = Am[i3]
                    Am4 = Amt.rearrange("p (c r t) -> p c r t", c=2, r=2)
                    for par in range(2):
                        S.op("dve", lambda par=par: V.tensor_tensor(
                            out=Am4[:, :, par, :], in0=apsb[par][0][:, 0:256].rearrange("p (c t) -> p c t", c=2),
                            in1=mask4[:, 0:256].rearrange("p (c t) -> p c t", c=2), op=ALU.mult),
                            reads=[apsb[par][1], bmask4], writes=[bAm])
                    ops, bops = bk[4], bkb[4]
                    fns = []
                    for h in range(4):
                        fns.append(lambda h=h: PE.matmul(ops[:, h * 128:(h + 1) * 128], lhsT=Amt[:, h * 128:(h + 1) * 128],
                                                         rhs=vAt[:, h * 128:(h + 1) * 128], start=True, stop=False))
                        fns.append(lambda h=h: PE.matmul(ops[:, h * 128:(h + 1) * 128],
                                                         lhsT=qtTt[(h % 2) * 64:(h % 2) * 64 + 64, h // 2, :],
                                                         rhs=Sbf[h // 2][0][(h % 2) * 64:(h % 2) * 64 + 64, :],
                                                         start=False, stop=True))
                    S.mm(fns, reads=[bAm, bvA, bqtT, Sbf[0][1], Sbf[1][1]], writes=[bops])
                    rms_mix(128, ops, bops, sq[i3][0], sq[i3][1], ssq[i3][0], ssq[i3][1], GS[i3][0], GS[i3][1],
                            mixa[i3][0], mixa[i3][1])
                    pt, bpt = pb[blk % 2], pbb[blk % 2]
                    mx = mixa[i3][0]
                    S.mm([lambda h=h: PE.transpose(pt[:, h * 128:(h + 1) * 128], mx[:, h * 128:(h + 1) * 128], ident)
                          for h in range(4)], reads=[mixa[i3][1], bid], writes=[bpt])
                    ob = blk - 16
                    S.op("act", lambda: A.copy(out=mixT[:, 0:4, ob * 128:(ob + 1) * 128],
                                               in_=pt[:, 0:512].rearrange("p (c t) -> p c t", c=4)),
                         reads=[bpt], writes=[bmix[c][ob] for c in range(4)])
                for p in range(2):
                    kvb, bkvb = apsb[p]
                    S.mm([lambda p=p, kvb=kvb: PE.matmul(kvb[:, 256:512], lhsT=ktt[:, p * 128:(p + 1) * 128],
                                                         rhs=vAt[:, p * 256:(p + 1) * 256], start=True, stop=True)],
                         reads=[bkt, bvA], writes=[bkvb])
                for p in range(2):
                    kvb, bkvb = apsb[p]
                    St, bSt = Sst[p]
                    for hh in range(2):
                        r = slice(hh * 64, hh * 64 + 64)
                        S.op("dve", lambda p=p, hh=hh, r=r, St=St, kvb=kvb: V.tensor_tensor(
                            out=stmp[r, :], in0=kvb[r, 256 + hh * 128:256 + (hh + 1) * 128], in1=St[r, :], op=ALU.add),
                            reads=[bkvb, bSt], writes=[bstmp])
                    S.op("dve", lambda p=p, St=St: V.tensor_scalar(out=St, in0=stmp, scalar1=elat[:, p:p + 1], scalar2=None,
                                                                   op0=ALU.mult), reads=[bstmp, bela], writes=[bSt])
                    S.op("act", lambda p=p, St=St: A.copy(out=Sbf[p][0], in_=St), reads=[bSt], writes=[Sbf[p][1]])

            LA = 4
            for blk in range(LA):
                S.replay_rr([S.record(p1_block, blk)])
            for step in range(NB + 2):
                qs = []
                if 0 <= step - 2 < NB:
                    qs.append(S.record(stage_b2, step - 2))
                if 0 <= step - 1 < NB:
                    qs.append(S.record(stage_b1, step - 1))
                if step < NB:
                    def sa(step=step):
                        if step % 4 == 0:
                            stage_s(step // 4)
                        stage_a(step)
                    qs.append(S.record(sa))
                if step + LA < NB:
                    qs.append(S.record(p1_block, step + LA))
                S.replay_rr(qs)
            S.barrier()
            es_p1.close()
            for p in range(2):
                S.dma("sp", sp_o[2 * p:2 * p + 2].rearrange("h d v -> (h d) v"), Sst[p][0], reads=[Sst[p][1]])

            hsA, bhsA = sbt(es, "hsA", [4, 1552])
            for gi, (c0, c1) in enumerate(((0, 512), (512, 1024), (1024, 1536), (1536, 1552))):
                ps, bps = acc()
                S.mm([lambda c=c, ps=ps, c0=c0, c1=c1: PE.matmul(ps[0:4, 0:c1 - c0], lhsT=xsT[:, c, :], rhs=WA[:, c, c0:c1],
                                                                 start=(c == 0), stop=(c == 7)) for c in range(8)],
                     reads=[bWA, bxsT], writes=[bps])
                S.op("act", lambda ps=ps, c0=c0, c1=c1: A.copy(out=hsA[:, c0:c1], in_=ps[0:4, 0:c1 - c0]), reads=[bps], writes=[bhsA])
            ps, bps = acc()
            S.mm([lambda c=c, ps=ps: PE.matmul(ps[0:16, 0:4], lhsT=WA[:, c, 1024:1040], rhs=xsT[:, c, :],
                                               start=(c == 0), stop=(c == 7)) for c in range(8)], reads=[bWA, bxsT], writes=[bps])
            za, bza = zaug[0]
            S.op("act", lambda: A.copy(out=za[0:16, 0:4], in_=ps[0:16, 0:4]), reads=[bps], writes=[bza])
            ups, bups = bk[3], bkb[3]
            S.mm([lambda dc=dc: PE.matmul(ups[:, dc * 4:(dc + 1) * 4], lhsT=W2a[:, dc * 128:(dc + 1) * 128], rhs=za[:, 0:4],
                                          start=True, stop=True) for dc in range(2)], reads=[bza, bW2a], writes=[bups])
            aTs, baTs = sbt(es, "aTs", [128, 8])
            S.op("act", lambda: A.activation(out=aTs, in_=ups[:, 0:8], func=AF.Exp, scale=-1.0), reads=[bups], writes=[baTs])
            S.op("act", lambda: A.activation(out=aTs, in_=aTs, func=AF.Ln, bias=1.0, scale=1.0), reads=[baTs], writes=[baTs])
            S.op("act", lambda: A.activation(out=aTs, in_=aTs, func=AF.Exp, scale=-1.0 / 16.0), reads=[baTs], writes=[baTs])
            qsT, bqsT = sbt(es, "qsT", [128, 2, 4])
            ps, bps = acc()
            for dc in range(2):
                S.mm([lambda c=c, dc=dc, ps=ps: PE.matmul(ps[:, dc * 4:(dc + 1) * 4], lhsT=WA[:, c, dc * 128:(dc + 1) * 128],
                                                          rhs=xsT[:, c, :], start=(c == 0), stop=(c == 7)) for c in range(8)],
                     reads=[bWA, bxsT], writes=[bps])
            S.op("dve", lambda: V.tensor_scalar(out=qsT, in0=ps[:, 0:8].rearrange("p (c t) -> p c t", c=2), scalar1=0.125,
                                                scalar2=None, op0=ALU.mult), reads=[bps], writes=[bqsT])
            Sn = [sbt(es, "Sn%d" % b, [128, 2, 128]) for b in range(4)]
            kmask, bkmask = sbt(es, "kmask", [4, 256])
            qm = [sbt(es, "qm%d" % b, [128, 2, 4]) for b in range(4)]
            for b in range(4):
                Snt, bSn = Sn[b]
                for p in range(2):
                    S.dma("sp", Snt[:, p, :], st[b, 2 * p:2 * p + 2].rearrange("h d v -> (h d) v"), writes=[bSn])
                S.op("dve", lambda b=b: V.tensor_scalar(out=kmask, in0=hsA[:, 256:512], scalar1=identf[0:4, b:b + 1],
                                                        scalar2=None, op0=ALU.mult), reads=[bhsA, bidf], writes=[bkmask])
                kv, bkv = bk[2], bkb[2]
                S.mm([lambda p=p: PE.matmul(kv[:, p * 256:(p + 1) * 256], lhsT=kmask[:, p * 128:(p + 1) * 128],
                                            rhs=hsA[:, 512 + p * 256:512 + (p + 1) * 256], start=True, stop=True) for p in range(2)],
                     reads=[bkmask, bhsA], writes=[bkv])
                for p in range(2):
                    for hh in range(2):
                        r = slice(hh * 64, hh * 64 + 64)
                        S.op("dve", lambda b=b, p=p, hh=hh, r=r, Snt=Snt: V.scalar_tensor_tensor(
                            out=Snt[r, p, :], in0=Snt[r, p, :], scalar=aTs[r, p * 4 + b:p * 4 + b + 1],
                            in1=kv[r, p * 256 + hh * 128:p * 256 + (hh + 1) * 128], op0=ALU.mult, op1=ALU.add),
                            reads=[bSn, baTs, bkv], writes=[bSn])
                    S.dma("sp", ss_o[b, 2 * p:2 * p + 2].rearrange("h d v -> (h d) v"), Snt[:, p, :], reads=[bSn])
                qmt, bqm = qm[b]
                for dc in range(2):
                    S.op("dve", lambda b=b, dc=dc, qmt=qmt: V.tensor_tensor(out=qmt[:, dc, :], in0=qsT[:, dc, :],
                                                                            in1=oh4[:, 4 * b:4 * b + 4], op=ALU.mult),
                         reads=[bqsT, boh4], writes=[bqm])
            ospb = ((bk[4], bkb[4]), (bk[5], bkb[5]))
            fns = []
            for h in range(4):
                r = slice((h % 2) * 64, (h % 2) * 64 + 64)
                for b in range(4):
                    fns.append(lambda h=h, b=b, r=r: PE.matmul(ospb[h % 2][0][0:4, (h // 2) * 128:(h // 2 + 1) * 128],
                                                               lhsT=qm[b][0][r, h // 2, :],
                                                               rhs=Sn[b][0][r, h // 2, :], start=(b == 0), stop=(b == 3)))
            S.mm(fns, reads=[qm[b][1] for b in range(4)] + [Sn[b][1] for b in range(4)], writes=[bkb[4], bkb[5]])
            osp, bosp = ge[1][0][0:4, :], ge[1][1]
            osp4 = osp.rearrange("p (c r v) -> p c r v", c=2, r=2)
            for par in range(2):
                S.op("act", lambda par=par: A.copy(out=osp4[:, :, par, :], in_=ospb[par][0][0:4, 0:256].rearrange("p (c v) -> p c v", c=2)),
                     reads=[ospb[par][1]], writes=[bosp])
            gsb, bgsb = GS[1][0][0:4, :], GS[1][1]
            S.op("act", lambda: A.copy(out=gsb, in_=hsA[:, 1040:1552]), reads=[bhsA], writes=[bgsb])
            silu_gate(4, gsb, bgsb, ge[0][0], ge[0][1], GS[0][0], GS[0][1])
            rms_mix(4, osp, bosp, sq[0][0], sq[0][1], ssq[0][0], ssq[0][1], GS[0][0], GS[0][1], mixa[0][0], mixa[0][1])
            pt, bpt = pb[0], pbb[0]
            mx = mixa[0][0]
            S.mm([lambda h=h: PE.transpose(pt[:, h * 4:(h + 1) * 4], mx[0:4, h * 128:(h + 1) * 128], ident[0:4, 0:4])
                  for h in range(4)], reads=[mixa[0][1], bid], writes=[bpt])
            S.op("act", lambda: A.copy(out=mixTs[:, 0:4, :], in_=pt[:, 0:16].rearrange("p (c t) -> p c t", c=4)),
                 reads=[bpt], writes=[bmixs])
            S.barrier()
        esx.close()
        esy = ExitStack()
        bk67 = esy.enter_context(nc.psum_tensor("bk67", [128, 1024], F32)).ap()
        for i in (6, 7):
            bk.append(bk67[:, (i - 6) * 512:(i - 5) * 512])
            bkb.append(Buf("bk%d" % i, True))

        with ExitStack() as es:
            logct, blogc = sbt(es, "logct", [128, NOFF * 128])
            S.dma("sp", logct, logc, writes=[blogc])
            stg = [sbt(es, "stg%d" % i, [128, 8, 128]) for i in range(4)]
            bst, bbst = sbt(es, "bst", [128, NOFF * 128])
            NXS = 3
            xst = [sbt(es, "xst%d" % i, [128, 8, 512], BF16) for i in range(NXS)]
            PT = []
            for par in range(1):
                t = {}
                t["Wp"], t["bWp"] = sbt(es, "Wp%d" % par, [128, 8, 4, 128], BF16)
                t["E"], t["bE"] = sbt(es, "E%d" % par, [128, 2, NOFF * 128], BF16)
                t["KT"], _ = sbt(es, "KT%d" % par, [128, NLOC], BF16)
                t["bKT"] = [Buf("KT%d_%d" % (par, i)) for i in range(8)]
                t["QT"], _ = sbt(es, "QT%d" % par, [128, NOWN], BF16)
                t["bQT"] = [Buf("QT%d_%d" % (par, i)) for i in range(4)]
                t["GTs"], _ = sbt(es, "GTs%d" % par, [128, NOWN], BF16)
                t["bGT"] = [Buf("GT%d_%d" % (par, i)) for i in range(4)]
                t["Vaug"], _ = sbt(es, "Vaug%d" % par, [128, NB, 2, 128], BF16)
                t["bVa"] = [Buf("Va%d_%d" % (par, i)) for i in range(8)]
                PT.append(t)
            NSB = 6
            SBK = (0, 1, 2, 3, 6, 7)
            NPB = 3
            pex = [sbt(es, "pex%d" % i, [128, 2, 512], BF16) for i in range(NPB)]
            pmk = [sbt(es, "pmk%d" % i, [128, 2, 512], BF16) for i in range(NPB)]

            def pairview(gi):
                k0 = SBK[gi % NSB]
                base = bk67 if k0 == 6 else bkA[:, k0 * 512:(k0 + 2) * 512]
                return base.rearrange("p (h n) -> p h n", h=2)
            vout = [sbt(es, "vout%d" % i, [128, 4, 128]) for i in range(2)]
            dtmp = [sbt(es, "dtmp%d" % i, [128, 1024]) for i in range(2)]
            Vcp, bVcp = sbt(es, "Vcp", [128, 32, 256], BF16)
            bvscr = Buf("vscr")
            Oacc = [sbt(es, "Oacc%d" % i, [128, NOWN]) for i in range(2)]
            EF4, bEF4 = sbt(es, "EF4", [128, 2, 512], BF16)
            bfs, bbfs = sbt(es, "bfs", [128, 256])
            for par in range(1):
                t = PT[par]
                S.op("pool", lambda t=t: G.memset(t["Vaug"], 1.0), writes=t["bVa"])
                S.op("act", lambda t=t: A.mul(out=t["Vaug"][:, 0:16], in_=t["Vaug"][:, 0:16], mul=cmt[:, 0:1]),
                     reads=t["bVa"][0:4] + [bcm], writes=t["bVa"][0:4])
            grp_n = [0]
            xs_n = [0]
            ipn = [0]

            def ipbank():
                i = (6, 7, 4, 5)[ipn[0] % 4]
                ipn[0] += 1
                return bk[i], bkb[i]

            def xs_load(sb):
                xt_, bxt_ = xst[xs_n[0] % NXS]
                xs_n[0] += 1
                S.dma("sp", xt_, xTd[sb].rearrange("p (c t) -> p c t", c=8), reads=[bxTd], writes=[bxt_])
                return xt_, bxt_

            def setup(hp):
                t = PT[0]
                Wp, bWp, E, bE = t["Wp"], t["bWp"], t["E"], t["bE"]
                KT, bKT, QT, bQT, GTs, bGT, Vaug, bVa = t["KT"], t["bKT"], t["QT"], t["bQT"], t["GTs"], t["bGT"], t["Vaug"], t["bVa"]
                for g in range(4):
                    sg, bsg = stg[g]
                    if g % 2 == 0:
                        S.op("act", lambda g=g, sg=sg: A.copy(out=Wp[:, :, g, :], in_=sg), reads=[bsg], writes=[bWp])
                    else:
                        S.op("dve", lambda g=g, sg=sg: V.tensor_copy(out=Wp[:, :, g, :], in_=sg), reads=[bsg], writes=[bWp])
                tiles = [xs_load(0), xs_load(1)]
                for hh in range(2):
                    S.dma("sp", bst, biasT[2 * hp + hh], writes=[bbst])
                    S.op("dve", lambda: V.tensor_tensor(out=bst, in0=bst, in1=logct, op=ALU.add), reads=[bbst, blogc], writes=[bbst])
                    S.op("act", lambda hh=hh: A.activation(out=E[:, hh, :], in_=bst, func=AF.Exp), reads=[bbst], writes=[bE])
                def do_sb(sb, xt_, bxt_):
                    tok = slice(sb * 512, (sb + 1) * 512)
                    ps, bps = ipbank()
                    S.mm([lambda c=c, ps=ps: PE.matmul(ps, lhsT=Wp[:, c, 1, :], rhs=xt_[:, c, :], start=(c == 0), stop=(c == 7))
                          for c in range(8)], reads=[bWp, bxt_], writes=[bps])
                    S.op("act", lambda ps=ps: A.copy(out=KT[:, tok], in_=ps), reads=[bps], writes=[bKT[sb]])
                    if sb < 4:
                        ps, bps = ipbank()
                        fns = []
                        for j in range(4):
                            for c in range(8):
                                fns.append(lambda c=c, j=j, ps=ps: PE.matmul(
                                    ps[:, j * 128:(j + 1) * 128], lhsT=xt_[:, c, j * 128:(j + 1) * 128], rhs=Wp[:, c, 2, :],
                                    start=(c == 0), stop=(c == 7)))
                        S.mm(fns, reads=[bWp, bxt_], writes=[bps])
                        ps3 = ps.rearrange("p (j n) -> p j n", j=4)
                        S.op("act", lambda ps3=ps3: A.mul(out=Vaug[:, sb * 4:(sb + 1) * 4, 0, 0:64], in_=ps3[:, :, 0:64], mul=cmt[:, 0:1]),
                             reads=[bps, bcm], writes=[bVa[sb]])
                        S.op("dve", lambda ps3=ps3: V.tensor_scalar(out=Vaug[:, sb * 4:(sb + 1) * 4, 1, 64:128], in0=ps3[:, :, 64:128],
                                                                  scalar1=cmt[:, 0:1], scalar2=None, op0=ALU.mult),
                             reads=[bps, bcm], writes=[bVa[sb]])
                    else:
                        so = sb - 4
                        otok = slice(so * 512, (so + 1) * 512)
                        r0 = so * 512
                        vo, bvo = vout[0]
                        ko, bko = vout[1]
                        for g2 in range(2):
                            ps, bps = ipbank()
                            fns = []
                            for jj in range(2):
                                j = 2 * g2 + jj
                                for c in range(8):
                                    fns.append(lambda c=c, j=j, jj=jj, ps=ps: PE.matmul(
                                        ps[:, jj * 256:(jj + 1) * 256], lhsT=xt_[:, c, j * 128:(j + 1) * 128],
                                        rhs=Wp[:, c, 1:3, :].rearrange("p g n -> p (g n)"), start=(c == 0), stop=(c == 7)))
                            S.mm(fns, reads=[bWp, bxt_], writes=[bps])
                            ps4 = ps.rearrange("p (j g n) -> p j g n", j=2, g=2)
                            b0 = sb * 4 + 2 * g2
                            S.op("act", lambda ps4=ps4, b0=b0: A.copy(out=Vaug[:, b0:b0 + 2, 0, 0:64], in_=ps4[:, :, 1, 0:64]),
                                 reads=[bps], writes=[bVa[sb]])
                            S.op("dve", lambda ps4=ps4, b0=b0: V.tensor_copy(out=Vaug[:, b0:b0 + 2, 1, 64:128], in_=ps4[:, :, 1, 64:128]),
                                 reads=[bps], writes=[bVa[sb]])
                            S.op("dve", lambda ps4=ps4, g2=g2: V.tensor_copy(out=vo[:, 2 * g2:2 * g2 + 2, :], in_=ps4[:, :, 1, :]),
                                 reads=[bps], writes=[bvo])
                            S.op("act", lambda ps4=ps4, g2=g2: A.copy(out=ko[:, 2 * g2:2 * g2 + 2, :], in_=ps4[:, :, 0, :]),
                                 reads=[bps], writes=[bko])
                        S.dma("sp", vw_o[r0:r0 + 512, hp * 128:(hp + 1) * 128].rearrange("(j p) n -> p j n", p=128), vo, reads=[bvo])
                        S.dma("sp", kw_o[r0:r0 + 512, hp * 128:(hp + 1) * 128].rearrange("(j p) n -> p j n", p=128), ko, reads=[bko])
                        ps, bps = ipbank()
                        S.mm([lambda c=c, ps=ps: PE.matmul(ps, lhsT=Wp[:, c, 0, :], rhs=xt_[:, c, :], start=(c == 0), stop=(c == 7))
                              for c in range(8)], reads=[bWp, bxt_], writes=[bps])
                        S.op("dve", lambda ps=ps: V.tensor_scalar(out=QT[:, otok], in0=ps, scalar1=0.125, scalar2=None, op0=ALU.mult),
                             reads=[bps], writes=[bQT[so]])
                        ps, bps = ipbank()
                        S.mm([lambda c=c, ps=ps: PE.matmul(ps, lhsT=Wp[:, c, 3, :], rhs=xt_[:, c, :], start=(c == 0), stop=(c == 7))
                              for c in range(8)], reads=[bWp, bxt_], writes=[bps])
                        S.op("act", lambda ps=ps: A.activation(out=GTs[:, otok], in_=ps, func=AF.Silu), reads=[bps], writes=[bGT[so]])
                for sb in range(8):
                    if sb + 2 < 8:
                        tiles.append(xs_load(sb + 2))
                    do_sb(sb, tiles[sb][0], tiles[sb][1])
                for hh in range(2):
                    S.dma("sp", bfs, biasF[2 * hp + hh], writes=[bbfs])
                    S.op("act", lambda hh=hh: A.activation(out=EF4[:, hh, 0:256], in_=bfs, func=AF.Exp), reads=[bbfs], writes=[bEF4])
                    S.op("dve", lambda hh=hh: V.tensor_copy(out=EF4[:, hh, 256:512], in_=EF4[:, hh, 0:256]), reads=[bEF4], writes=[bEF4])
                S.dma("sp", vscr.rearrange("(b p) n -> p b n", p=128), Vaug.rearrange("p b h n -> p b (h n)"), reads=bVa, writes=[bvscr])
                S.dma("sp", Vcp.rearrange("i (f c) n -> i f c n", f=2), vscr.rearrange("(f i c) n -> i f c n", f=2, i=128, c=16),
                      reads=[bvscr], writes=[bVcp])
                ps, bps = ipbank()
                S.mm([lambda c=c, ps=ps: PE.matmul(ps[0:4, :], lhsT=xsT[:, c, :], rhs=Wp[:, c, :, :].rearrange("p g n -> p (g n)"),
                                                   start=(c == 0), stop=(c == 7)) for c in range(8)], reads=[bWp, bxsT], writes=[bps])
                S.op("act", lambda ps=ps: A.copy(out=hsB[:, hp, :], in_=ps[0:4, :]), reads=[bps], writes=[bhsB])
                ps, bps = ipbank()
                S.mm([lambda c=c, ps=ps: PE.matmul(ps[:, 0:4], lhsT=Wp[:, c, 3, :], rhs=xsT[:, c, :], start=(c == 0), stop=(c == 7))
                      for c in range(8)], reads=[bWp, bxsT], writes=[bps])
                S.op("act", lambda ps=ps: A.activation(out=GTss[:, hp, :], in_=ps[:, 0:4], func=AF.Silu), reads=[bps], writes=[bGTss])

            def mainloop(hp):
                t = PT[0]
                E, bE = t["E"], t["bE"]
                KT, bKT, QT, bQT, GTs, bGT, Vaug, bVa = t["KT"], t["bKT"], t["QT"], t["bQT"], t["GTs"], t["bGT"], t["Vaug"], t["bVa"]
                items = []
                for qg in range(4):
                    kbl = list(range(12 + 4 * qg, 20 + 4 * qg))
                    kbl.remove(16 + 4 * qg)
                    kbl = [16 + 4 * qg] + kbl
                    for idx, kb in enumerate(kbl):
                        qa = max(4 * qg, kb - 16)
                        qz = min(4 * qg + 3, kb - 12)
                        for hh in range(2):
                            items.append(dict(qg=qg, hh=hh, kb=kb, qa=qa, n=qz - qa + 1, oa=16 + qa - kb,
                                              first=(idx == 0), last=(idx == len(kbl) - 1), it=hh, gi=grp_n[0]))
                            grp_n[0] += 1

                def emit_qk(w):
                    hh, kb, qa, n, gi = w["hh"], w["kb"], w["qa"], w["n"], w["gi"]
                    r = slice(hh * 64, hh * 64 + 64)
                    sp_, bsp = bk[SBK[gi % NSB]], bkb[SBK[gi % NSB]]
                    S.mm([lambda: PE.matmul(sp_[:, 0:n * 128], lhsT=KT[r, kb * 128:(kb + 1) * 128],
                                            rhs=QT[r, qa * 128:(qa + n) * 128], start=True, stop=True)],
                         reads=[bKT[kb // 4]] + [bQT[q // 4] for q in range(qa, qa + n)], writes=[bsp])

                def emit_rest2(w0, w1):
                    qg, kb, qa, n, gi = w0["qg"], w0["kb"], w0["qa"], w0["n"], w0["gi"]
                    nw = n * 128
                    pv = pairview(gi)
                    bsp0, bsp1 = bkb[SBK[gi % NSB]], bkb[SBK[(gi + 1) % NSB]]
                    px, bpx = pex[(gi // 2) % NPB]
                    pm, bpm = pmk[(gi // 2) % NPB]
                    S.op("act", lambda: A.activation(out=px[:, :, 0:nw], in_=pv[:, :, 0:nw], func=AF.Exp),
                         reads=[bsp0, bsp1], writes=[bpx])
                    esl = E[:, :, w0["oa"] * 128:(w0["oa"] + n) * 128]
                    S.op("dve", lambda: V.tensor_tensor(out=pm[:, :, 0:nw], in0=px[:, :, 0:nw], in1=esl, op=ALU.mult),
                         reads=[bpx, bE], writes=[bpm])
                    c0 = (qa - 4 * qg) * 128
                    for hh in range(2):
                        ot, bot = bk[4 + hh], bkb[4 + hh]
                        S.mm([lambda hh=hh, ot=ot: PE.matmul(ot[:, c0:c0 + nw], lhsT=Vaug[:, kb, hh, :], rhs=pm[:, hh, 0:nw],
                                                             start=w0["first"], stop=w0["last"])],
                             reads=[bpm, bVa[kb // 4]], writes=[bot])
                    if w0["last"]:
                        S.op("act", lambda: A.copy(out=Oacc[0][0][:, qg * 512:(qg + 1) * 512], in_=bk[4]), reads=[bkb[4]], writes=[Oacc[0][1]])
                        S.op("dve", lambda: V.tensor_copy(out=Oacc[1][0][:, qg * 512:(qg + 1) * 512], in_=bk[5]), reads=[bkb[5]], writes=[Oacc[1][1]])

                for i in range(4):
                    emit_qk(items[i])
                for i in range(0, len(items), 2):
                    if i + 4 < len(items):
                        emit_qk(items[i + 4])
                        emit_qk(items[i + 5])
                    emit_rest2(items[i], items[i + 1])

                def far_qk(w):
                    hh, cg, gi = w["hh"], w["cg"], w["gi"]
                    r = slice(hh * 64, hh * 64 + 64)
                    sp_, bsp = bk[SBK[gi % NSB]], bkb[SBK[gi % NSB]]
                    fns = []
                    for k in range(2):
                        cc = 2 * cg + k
                        for f in range(2):
                            fns.append(lambda cc=cc, f=f, k=k: PE.matmul(
                                sp_[:, (2 * k + f) * 128:(2 * k + f + 1) * 128], lhsT=KT[r, 2048 * f + cc:2048 * (f + 1):16],
                                rhs=QT[r, cc:2048:16], start=True, stop=True))
                    S.mm(fns, reads=bKT + bQT, writes=[bsp])

                def far_rest2(w0, w1):
                    cg, gi = w0["cg"], w0["gi"]
                    pv = pairview(gi)
                    bsp0, bsp1 = bkb[SBK[gi % NSB]], bkb[SBK[(gi + 1) % NSB]]
                    px, bpx = pex[(gi // 2) % NPB]
                    pm, bpm = pmk[(gi // 2) % NPB]
                    S.op("act", lambda: A.activation(out=px, in_=pv, func=AF.Exp), reads=[bsp0, bsp1], writes=[bpx])
                    S.op("dve", lambda: V.tensor_tensor(out=pm, in0=px, in1=EF4, op=ALU.mult), reads=[bpx, bEF4], writes=[bpm])
                    for hh in range(2):
                        of, bof = bk[4 + hh], bkb[4 + hh]
                        fns = []
                        for k in range(2):
                            cc = 2 * cg + k
                            slot = (cg % 2) * 2 + k
                            for f in range(2):
                                fns.append(lambda cc=cc, f=f, k=k, slot=slot, hh=hh, of=of: PE.matmul(
                                    of[:, slot * 128:(slot + 1) * 128], lhsT=Vcp[:, f * 16 + cc, hh * 128:(hh + 1) * 128],
                                    rhs=pm[:, hh, (2 * k + f) * 128:(2 * k + f + 1) * 128], start=(f == 0), stop=(f == 1)))
                        S.mm(fns, reads=[bpm, bVcp], writes=[bof])
                        if cg % 2 == 1:
                            oa_, boa = Oacc[hh]
                            c0 = 2 * cg - 2
                            dst = oa_.rearrange("p (j c) -> p c j", c=16)[:, c0:c0 + 4, :]
                            S.op("dve", lambda dst=dst, of=of: V.tensor_tensor(out=dst, in0=of.rearrange("p (c j) -> p c j", c=4), in1=dst, op=ALU.add),
                                 reads=[bof, boa], writes=[boa])

                fitems = []
                for cg in range(8):
                    for hh in range(2):
                        fitems.append(dict(hh=hh, cg=cg, gi=grp_n[0]))
                        grp_n[0] += 1
                for i in range(4):
                    far_qk(fitems[i])
                for i in range(0, len(fitems), 2):
                    if i + 4 < len(fitems):
                        far_qk(fitems[i + 4])
                        far_qk(fitems[i + 5])
                    far_rest2(fitems[i], fitems[i + 1])

                def fin(half):
                    cs = slice(half * 1024, (half + 1) * 1024)
                    dt_, bdt = dtmp[half]
                    o0, bo0 = Oacc[0]
                    o1, bo1 = Oacc[1]
                    S.op("dve", lambda: V.tensor_copy(out=dt_[0:64, :], in_=o0[64:128, cs]), reads=[bo0], writes=[bdt])
                    S.op("dve", lambda: V.tensor_copy(out=dt_[64:128, :], in_=o1[0:64, cs]), reads=[bo1], writes=[bdt])
                    S.op("act", lambda: A.activation(out=dt_, in_=dt_, func=AF.Ln), reads=[bdt], writes=[bdt])
                    S.op("act", lambda: A.activation(out=dt_, in_=dt_, func=AF.Exp, scale=-1.0), reads=[bdt], writes=[bdt])
                    S.op("dve", lambda: V.tensor_tensor(out=dt_[0:64, :], in0=o0[0:64, cs], in1=dt_[0:64, :], op=ALU.mult),
                         reads=[bo0, bdt], writes=[bdt])
                    S.op("dve", lambda: V.tensor_tensor(out=dt_[64:128, :], in0=o1[64:128, cs], in1=dt_[64:128, :], op=ALU.mult),
                         reads=[bo1, bdt], writes=[bdt])
                    S.op("dve", lambda: V.tensor_tensor(out=mixT[:, 4 + hp, cs], in0=dt_, in1=GTs[:, cs], op=ALU.mult),
                         reads=[bdt] + bGT[2 * half:2 * half + 2], writes=[bmix[4 + hp][q] for q in range(8 * half, 8 * half + 8)])

                for half in range(2):
                    fin(half)

            def wdma(hp):
                for g in range(4):
                    col0 = 1552 + 512 * g + 128 * hp
                    S.dma("sp", stg[g][0], w_in[:, col0:col0 + 128].rearrange("(c p) n -> p c n", p=128), writes=[stg[g][1]])

            wdma(0)
            for hp in range(4):
                S.replay_rr([S.record(setup, hp)])
                if hp + 1 < 4:
                    wdma(hp + 1)
                S.replay_rr([S.record(mainloop, hp)])
            S.barrier()

        with ExitStack() as es:
            S.rec = []
            qkv, bqkv = sbt(es, "qkv", [4, 3, 512])
            for t in range(3):
                S.op("dve", lambda t=t: V.tensor_copy(out=qkv[:, t, :].rearrange("p (h n) -> p h n", h=4), in_=hsB[:, :, t * 128:(t + 1) * 128]),
                     reads=[bhsB], writes=[bqkv])
            S.dma("sp", kn_o, qkv[:, 1, :], reads=[bqkv])
            S.dma("sp", vn_o, qkv[:, 2, :], reads=[bqkv])
            bscr = Buf("scr")
            S.dma("sp", scr_q, qkv[:, 0, :], reads=[bqkv], writes=[bscr])
            bdt_, bbd = sbt(es, "bdt", [8, 512])
            S.dma("sp", bdt_, bd_d, writes=[bbd])
            bdect, bbdec = sbt(es, "bdect", [128, 3, 8])
            for s in range(3):
                S.dma("sp", bdect[:, s, :], bdec[s], writes=[bbdec])
            b0t, bb0 = sbt(es, "b0t", [4, 8])
            S.dma("sp", b0t, b0.partition_broadcast(4), writes=[bb0])
            pr0, bpr0 = sbt(es, "pr0", [4, 512])
            l0, bl0 = sbt(es, "l0", [4, 8])
            S.op("dve", lambda: V.tensor_tensor(out=pr0, in0=qkv[:, 0, :], in1=qkv[:, 1, :], op=ALU.mult), reads=[bqkv], writes=[bpr0])
            S.op("dve", lambda: V.tensor_reduce(out=l0, in_=pr0.rearrange("p (h d) -> p h d", h=8), axis=AX.X, op=ALU.add),
                 reads=[bpr0], writes=[bl0])
            S.op("dve", lambda: V.scalar_tensor_tensor(out=l0, in0=l0, scalar=0.125, in1=b0t, op0=ALU.mult, op1=ALU.add),
                 reads=[bl0, bb0], writes=[bl0])
            S.op("act", lambda: A.activation(out=l0, in_=l0, func=AF.Exp), reads=[bl0], writes=[bl0])
            S.op("dve", lambda: V.tensor_scalar(out=l0, in0=l0, scalar1=3.0, scalar2=None, op0=ALU.mult), reads=[bl0], writes=[bl0])
            Kg = [sbt(es, "Kg%d" % i, [128, 512]) for i in range(3)]
            Vg = [sbt(es, "Vg%d" % i, [128, 512]) for i in range(3)]
            qbc, bqbc = sbt(es, "qbc", [128, 512])
            prod, bprod = sbt(es, "prod", [128, 512])
            lg, blg = sbt(es, "lg", [128, 3, 8])
            p0m, bp0m = sbt(es, "p0m", [4, 8])
            rd, brd = sbt(es, "rd", [8, 1])
            om, bom = sbt(es, "om", [8, 512])
            obp, bobp = bk[6], bkb[6]
            for b in range(4):
                S.dma("sp", qbc, scr_q[b:b + 1, :].partition_broadcast(128), reads=[bscr], writes=[bqbc])
                for si, sg_ in enumerate((1, 4, 16)):
                    r0 = 2048 - 128 * sg_
                    S.dma("sp", Kg[si][0], ck[b, r0:2048:sg_, :], writes=[Kg[si][1]])
                    S.dma("sp", Vg[si][0], cv[b, r0:2048:sg_, :], writes=[Vg[si][1]])
                    S.op("dve", lambda si=si: V.tensor_tensor(out=prod, in0=Kg[si][0], in1=qbc, op=ALU.mult),
                         reads=[Kg[si][1], bqbc], writes=[bprod])
                    S.op("dve", lambda si=si: V.tensor_reduce(out=lg[:, si, :], in_=prod.rearrange("p (h d) -> p h d", h=8), axis=AX.X, op=ALU.add),
                         reads=[bprod], writes=[blg])
                S.op("dve", lambda: V.scalar_tensor_tensor(out=lg, in0=lg, scalar=0.125, in1=bdect, op0=ALU.mult, op1=ALU.add),
                     reads=[blg, bbdec], writes=[blg])
                S.op("act", lambda: A.activation(out=lg, in_=lg, func=AF.Exp), reads=[blg], writes=[blg])
                S.op("dve", lambda b=b: V.tensor_scalar(out=p0m, in0=l0, scalar1=identf[0:4, b:b + 1], scalar2=None, op0=ALU.mult),
                     reads=[bl0, bidf], writes=[bp0m])
                ops_, bops_ = bk[7], bkb[7]
                fns = [lambda si=si: PE.matmul(ops_[0:8, :], lhsT=lg[:, si, :], rhs=Vg[si][0], start=(si == 0), stop=False) for si in range(3)]
                fns.append(lambda: PE.matmul(ops_[0:8, :], lhsT=p0m, rhs=qkv[:, 2, :], start=False, stop=True))
                S.mm(fns, reads=[blg, bp0m, bqkv] + [Vg[si][1] for si in range(3)], writes=[bops_])
                dps, bdps = bk[6][:, 32:64], bkb[6]
                fns = [lambda si=si: PE.matmul(dps[0:8, 0:1], lhsT=lg[:, si, :], rhs=ones[:, 0:1], start=(si == 0), stop=False) for si in range(3)]
                fns.append(lambda: PE.matmul(dps[0:8, 0:1], lhsT=p0m, rhs=ones[0:4, 0:1], start=False, stop=True))
                S.mm(fns, reads=[blg, bp0m, bones], writes=[bdps])
                S.op("dve", lambda: V.reciprocal(out=rd, in_=dps[0:8, 0:1]), reads=[bdps], writes=[brd])
                S.op("dve", lambda: V.scalar_tensor_tensor(out=om, in0=ops_[0:8, :], scalar=rd[:, 0:1], in1=bdt_, op0=ALU.mult, op1=ALU.mult),
                     reads=[bops_, brd, bbd], writes=[bom])
                S.mm([lambda ch=ch, b=b: PE.matmul(obp[:, ch * 4 + b:ch * 4 + b + 1], lhsT=om[:, ch * 128:(ch + 1) * 128], rhs=ones[0:8, 0:1],
                                                   start=True, stop=True) for ch in range(4)], reads=[bom, bones], writes=[bobp])
            S.op("dve", lambda: V.tensor_tensor(out=mixTs[:, 4:8, :], in0=obp[:, 0:16].rearrange("p (c t) -> p c t", c=4), in1=GTss, op=ALU.mult),
                 reads=[bobp, bGTss], writes=[bmixs])
            rec3b, S.rec = S.rec, None

            Wo, bWo = sbt(es, "Wo", [128, 8, D], BF16)
            stgO = [sbt(es, "stgO%d" % i, [128, D]) for i in range(2)]
            for c in range(8):
                sg, bsg = stgO[c % 2]
                S.dma("sp", sg, w_out[c * 128:(c + 1) * 128, :], writes=[bsg])
                if c % 2 == 0:
                    S.op("dve", lambda c=c, sg=sg: V.tensor_copy(out=Wo[:, c, :], in_=sg), reads=[bsg], writes=[bWo])
                else:
                    S.op("act", lambda c=c, sg=sg: A.copy(out=Wo[:, c, :], in_=sg), reads=[bsg], writes=[bWo])
            lngt, blng = sbt(es, "lngt", [128, D])
            lnbt, blnb = sbt(es, "lnbt", [128, D])
            S.dma("sp", lngt, lng.partition_broadcast(128), writes=[blng])
            S.dma("sp", lnbt, lnb.partition_broadcast(128), writes=[blnb])
            NXR = 5
            xr = [sbt(es, "xr%d" % i, [128, D]) for i in range(NXR)]
            rr = [sbt(es, "rr%d" % i, [128, D]) for i in range(3)]
            sqrs = [sbt(es, "sqr%d" % i, [128, D]) for i in range(2)]
            stt = [sbt(es, "stt%d" % i, [128, 8]) for i in range(3)]
            aI, baI = sbt(es, "aI", [128, 128])
            S.op("pool", lambda: G.tensor_scalar(out=aI, in0=identf, scalar1=ALPHA, scalar2=None, op0=ALU.mult), reads=[bidf], writes=[baI])

            def x_load(P, it, x_src):
                xt, bxt = xr[it % NXR]
                S.dma("sp", xt[0:P, :], x_src, writes=[bxt])

            def out_block(P, it, lhs_fn, mix_bufs, x_src, y_dst):
                xt, bxt = xr[it % NXR]
                rt, brt = rr[it % 3]
                s8, bs8 = stt[it % 3]
                sqr, bsqr = sqrs[it % 2]
                yps = []
                for half in range(2):
                    yp, byp = bk[2 * (it % 3) + half], bkb[2 * (it % 3) + half]
                    yps.append((yp, byp))
                    fns = [lambda c=c, yp=yp, half=half: PE.matmul(yp[0:P, :], lhsT=lhs_fn(c), rhs=Wo[:, c, half * 512:(half + 1) * 512],
                                                                   start=(c == 0), stop=False) for c in range(8)]
                    fns.append(lambda yp=yp, half=half: PE.matmul(yp[0:P, :], lhsT=aI[0:P, 0:P], rhs=xt[0:P, half * 512:(half + 1) * 512],
                                                                  start=False, stop=True))
                    S.mm(fns, reads=[bWo, bxt, baI] + mix_bufs, writes=[byp])
                for half in range(2):
                    yp, byp = yps[half]
                    S.op("act", lambda yp=yp, half=half: A.activation(out=sqr[0:P, 0:512], in_=yp[0:P, :], func=AF.Copy,
                                                                      accum_out=s8[0:P, half:half + 1]),
                         reads=[byp], writes=[bsqr, bs8])
                    S.op("act", lambda yp=yp, half=half: A.activation(out=sqr[0:P, 512:1024], in_=yp[0:P, :], func=AF.Square,
                                                                      accum_out=s8[0:P, 2 + half:3 + half]),
                         reads=[byp], writes=[bsqr, bs8])
                S.op("dve", lambda: V.tensor_reduce(out=s8[0:P, 4:6], in_=s8[0:P, 0:4].rearrange("p (a b) -> p a b", a=2), axis=AX.X, op=ALU.add),
                     reads=[bs8], writes=[bs8])
                S.op("dve", lambda: V.tensor_scalar(out=s8[0:P, 4:6], in0=s8[0:P, 4:6], scalar1=1.0 / D, scalar2=None, op0=ALU.mult),
                     reads=[bs8], writes=[bs8])
                S.op("dve", lambda: V.tensor_tensor(out=s8[0:P, 6:7], in0=s8[0:P, 4:5], in1=s8[0:P, 4:5], op=ALU.mult), reads=[bs8], writes=[bs8])
                S.op("dve", lambda: V.tensor_tensor(out=s8[0:P, 6:7], in0=s8[0:P, 5:6], in1=s8[0:P, 6:7], op=ALU.subtract), reads=[bs8], writes=[bs8])
                S.op("act", lambda: A.activation(out=s8[0:P, 7:8], in_=s8[0:P, 6:7], func=AF.Ln, bias=LN_EPS, scale=1.0), reads=[bs8], writes=[bs8])
                S.op("act", lambda: A.activation(out=s8[0:P, 7:8], in_=s8[0:P, 7:8], func=AF.Exp, scale=-0.5), reads=[bs8], writes=[bs8])
                S.op("dve", lambda: V.scalar_tensor_tensor(out=s8[0:P, 6:7], in0=s8[0:P, 4:5], scalar=-1.0, in1=s8[0:P, 7:8],
                                                           op0=ALU.mult, op1=ALU.mult), reads=[bs8], writes=[bs8])
                for half in range(2):
                    yp, byp = yps[half]
                    S.op("act", lambda yp=yp, half=half: A.activation(
                        out=rt[0:P, half * 512:(half + 1) * 512], in_=yp[0:P, :], func=AF.Identity,
                        bias=s8[0:P, 6:7], scale=s8[0:P, 7:8]), reads=[byp, bs8], writes=[brt])
                S.op("dve", lambda: V.tensor_tensor(out=rt[0:P, :], in0=rt[0:P, :], in1=lngt[0:P, :], op=ALU.mult), reads=[brt, blng], writes=[brt])
                S.op("pool", lambda: G.tensor_tensor(out=rt[0:P, :], in0=rt[0:P, :], in1=lnbt[0:P, :], op=ALU.add), reads=[brt, blnb], writes=[brt])
                S.dma("sp", y_dst, rt[0:P, :], reads=[brt])

            def xsrc(tb):
                return xloc[NOWN + tb * 128:NOWN + (tb + 1) * 128, :]

            for tb in range(NXR - 1):
                x_load(128, tb, xsrc(tb))
            recs = []
            for tb in range(16):
                recs.append(S.record(out_block, 128, tb, (lambda c, tb=tb: mixT[:, c, tb * 128:(tb + 1) * 128]),
                                     [bmix[c][tb] for c in range(8)], None, y_o[tb * 128:(tb + 1) * 128, :]))
            parts = []
            for q in recs:
                n = len(q)
                parts.append([q[(n * k) // 3:(n * (k + 1)) // 3] for k in range(3)])
            for step in range(16 + 2):
                nb_ = step + NXR - 1
                if nb_ < 16:
                    x_load(128, nb_, xsrc(nb_))
                elif nb_ == 16:
                    x_load(4, 16, xs)
                n3 = len(rec3b)
                ch3 = rec3b[(n3 * step) // 18:(n3 * (step + 1)) // 18]
                S.replay_rr([parts[step - k][k] for k in (2, 1, 0) if 0 <= step - k < 16] + [ch3])
            out_block(4, 16, lambda c: mixTs[:, c, :], [bmixs], xs, ys_o)
            S.barrier()
    except _Stop:
        pass
    S.finish()
    pass
    return nc


def _t5_bucket(dist):
    dist = np.asarray(dist, np.int32)
    d = np.maximum(dist.astype(np.float32), np.float32(1.0))
    lb = np.float32(16) + np.log(d / np.float32(16)) / np.float32(math.log(2048 / 16)) * np.float32(16)
    lb = np.minimum(lb.astype(np.int32), 31)
    return np.where(dist < 16, dist, lb).astype(np.int64)


def _static_tables():
    i = np.arange(128)[:, None]
    j = np.arange(128)[None, :]
    dmat = np.stack([128 * o + j - i for o in range(NOFF)], 0)
    cnt = ((dmat >= 0) & (dmat <= 128)).astype(np.int64) \
        + ((dmat >= 0) & (dmat <= 512) & (dmat % 4 == 0)) + ((dmat >= 0) & (dmat <= 512) & (dmat % 16 == 0))
    valid = cnt > 0
    bucket = _t5_bucket(np.clip(dmat, 0, 2048))
    logc = np.where(valid, np.log(np.maximum(cnt, 1)), 0.0).astype(np.float32)
    return dmat, valid, bucket, logc


_NC = None


def kernel(x_prompt, x_sample, state_gla, cache_k_win, cache_v_win, w_in, w_alpha2, b_alpha,
           gla_norm_g, w_out, ln_g, ln_b, rel_bias):
    global _NC
    f = lambda a: np.ascontiguousarray(np.asarray(a), dtype=np.float32)
    x_prompt, x_sample, state_gla = f(x_prompt), f(x_sample), f(state_gla)
    cache_k_win, cache_v_win = f(cache_k_win), f(cache_v_win)
    w_in, w_alpha2, b_alpha, gla_norm_g = f(w_in)[0], f(w_alpha2)[0], f(b_alpha), f(gla_norm_g)
    w_out, ln_g, ln_b, rel_bias = f(w_out)[0], f(ln_g), f(ln_b), f(rel_bias)

    dmat, valid, bucket, logc = _static_tables()
    gat = rel_bias[bucket]
    gat = np.where(valid[..., None], gat, np.float32(NEG))
    biasT = np.ascontiguousarray(gat.transpose(3, 1, 0, 2).reshape(8, 128, NOFF * 128), dtype=np.float32)
    logc2 = np.ascontiguousarray(logc.transpose(1, 0, 2).reshape(128, NOFF * 128), dtype=np.float32)
    ii = np.arange(128)
    nA = 128 + ii[None, :] - ii[:, None]
    nB = ii[None, :] - ii[:, None]
    nf = np.concatenate([nA, nB], 1)
    vf = (nf >= 33) & (nf <= 128)
    gf = rel_bias[_t5_bucket(np.clip(nf, 0, 128) * 16)]
    biasF = np.ascontiguousarray(np.where(vf[..., None], gf, np.float32(NEG)).transpose(2, 0, 1), dtype=np.float32)
    bdec = np.stack([rel_bias[_t5_bucket((128 - ii) * s)] for s in (1, 4, 16)], 0).astype(np.float32)
    b0 = rel_bias[_t5_bucket(np.array([0]))].astype(np.float32)
    tri = (ii[:, None] <= ii[None, :]).astype(np.float32)
    ident = np.eye(128, dtype=np.float32)
    bd = np.zeros((8, 512), np.float32)
    for h in range(8):
        bd[h, h * 64:(h + 1) * 64] = 1.0
    oh4 = np.zeros((128, 16), np.float32)
    for b in range(4):
        oh4[:, 4 * b + b] = 1.0

    if _NC is None:
        _NC = build()
    in_maps = []
    for c in range(NCORES):
        b, half = c // 2, c % 2
        if half == 0:
            xloc = np.concatenate([np.zeros((NOWN, D), np.float32), x_prompt[b, :NOWN]], 0)
        else:
            xloc = x_prompt[b]
        sl = slice(4 * c, 4 * c + 4)
        in_maps.append({
            "xloc": np.ascontiguousarray(xloc), "xs": np.ascontiguousarray(x_sample[sl, 0]),
            "st": np.ascontiguousarray(state_gla[0, sl]),
            "ck": np.ascontiguousarray(cache_k_win[0, sl].reshape(4, 2048, 512)),
            "cv": np.ascontiguousarray(cache_v_win[0, sl].reshape(4, 2048, 512)),
            "w_in": w_in, "w2": w_alpha2, "ba": b_alpha, "gng": gla_norm_g, "w_out": w_out, "lng": ln_g, "lnb": ln_b,
            "biasT": biasT, "logc": logc2, "bdec": bdec, "b0": b0, "biasF": biasF,
            "cm": np.full((128, 1), float(half), np.float32),
            "tri": tri, "ident": ident, "bd": bd, "oh4": oh4,
        })
    res = run_bass_kernel_spmd(_NC, in_maps, core_ids=list(range(NCORES))).results
    y_p = np.stack([np.concatenate([res[2 * b]["y_o"], res[2 * b + 1]["y_o"]], 0) for b in range(4)], 0)
    y_s = np.concatenate([res[c]["ys_o"] for c in range(NCORES)], 0)[:, None, :]
    s_p = np.stack([res[2 * b + 1]["sp_o"] for b in range(4)], 0)[None]
    s_s = np.concatenate([res[c]["ss_o"] for c in range(NCORES)], 0)[None]
    k_p = np.stack([res[2 * b + 1]["kw_o"].reshape(NOWN, 8, 64) for b in range(4)], 0)[None]
    v_p = np.stack([res[2 * b + 1]["vw_o"].reshape(NOWN, 8, 64) for b in range(4)], 0)[None]
    k_n = np.concatenate([res[c]["kn_o"] for c in range(NCORES)], 0).reshape(32, 1, 8, 64)[None]
    v_n = np.concatenate([res[c]["vn_o"] for c in range(NCORES)], 0).reshape(32, 1, 8, 64)[None]
    return (y_p.astype(np.float32), y_s.astype(np.float32), s_p.astype(np.float32), s_s.astype(np.float32),
            k_p.astype(np.float32), v_p.astype(np.float32), k_n.astype(np.float32), v_n.astype(np.float32))
```

```python
from contextlib import ExitStack
import math
import numpy as np
import concourse.bass as bass
import concourse.mybir as mybir
from concourse.bass_utils import run_bass_kernel_spmd

F32 = mybir.dt.float32
BF16 = mybir.dt.bfloat16
AF = mybir.ActivationFunctionType
ALU = mybir.AluOpType
AX = mybir.AxisListType

NCORES = 8
D = 1024
DIN = 3600
NLOC = 4096
NOWN = 2048
NB = 32
NOFF = 5
ALPHA = 2.0 ** 0.25
LN_EPS = 1e-5
RMS_EPS = 1e-6
NEG = -30000.0


class _Stop(Exception):
    pass


class Buf:
    __slots__ = ("name", "w", "rs", "excl")

    def __init__(self, name="", excl=False):
        self.name = name
        self.w = None
        self.rs = {}
        self.excl = excl


class Sched:
    def __init__(self, nc, nds=48):
        self.nc = nc
        self.engs = {"pe": nc.tensor, "act": nc.scalar, "dve": nc.vector,
                     "pool": nc.gpsimd, "sp": nc.sync}
        self.csem = {e: nc.alloc_semaphore("c_" + e) for e in ("pe", "act", "dve", "pool")}
        self.ccnt = {e: 0 for e in self.csem}
        self.NDS = nds
        self.dsem = [nc.alloc_semaphore("d%d" % i) for i in range(nds)]
        self.dcnt = [0] * nds
        self.dnext = 0
        self.seen = {e: {} for e in self.engs}
        self.snaps = {}
        self.rec = None

    def _need(self, e, ev, acc):
        if ev is None:
            return
        key, sem, val = ev
        if e == "pe" and key == "pe":
            return
        if self.seen[e].get(key, 0) >= val:
            return
        for i, (k2, s2, v2) in enumerate(acc):
            if k2 == key:
                if val > v2:
                    acc[i] = (key, sem, val)
                return
        acc.append((key, sem, val))

    def _collect(self, e, reads, writes):
        acc = []
        for b in reads:
            self._need(e, b.w, acc)
            if b.excl:
                for k, (sem, val) in b.rs.items():
                    if k != e:
                        self._need(e, (k, sem, val), acc)
        for b in writes:
            self._need(e, b.w, acc)
            for k, (sem, val) in b.rs.items():
                self._need(e, (k, sem, val), acc)
        acc.sort(key=lambda t: -t[2])
        keep = []
        implied = {}
        for (key, sem, val) in acc:
            if implied.get(key, 0) >= val:
                continue
            keep.append((key, sem, val))
            snap = self.snaps.get((key, val))
            if snap:
                for k2, v2 in snap.items():
                    if implied.get(k2, 0) < v2:
                        implied[k2] = v2
        return keep

    def _mark(self, e, w):
        key, sem, val = w
        se = self.seen[e]
        if se.get(key, 0) < val:
            se[key] = val
        snap = self.snaps.get((key, val))
        if snap:
            for k2, v2 in snap.items():
                if k2 != e and se.get(k2, 0) < v2:
                    se[k2] = v2

    def _wait(self, e, ev):
        acc = []
        self._need(e, ev, acc)
        for w in acc:
            self.engs[e].wait_ge(w[1], w[2])
            self._mark(e, w)

    def _commit(self, e, ev, reads, writes):
        key, sem, val = ev
        snap = dict(self.seen[e])
        snap.pop(e, None) if e in ("act", "dve", "pool") else None
        self.snaps[(key, val)] = snap
        for b in reads:
            b.rs[key] = (sem, val)
        for b in writes:
            b.w = ev
            b.rs = {}

    def _emit(self, e, fn, reads, writes):
        ws = self._collect(e, reads, writes)
        for w in ws[:-1]:
            self.engs[e].wait_ge(w[1], w[2])
        ins = fn()
        if ws:
            ins._wait_ge(ws[-1][1], ws[-1][2])
        for w in ws:
            self._mark(e, w)
        return ins

    def record(self, f, *args):
        self.rec = []
        f(*args)
        q, self.rec = self.rec, None
        return q

    def replay_rr(self, queues):
        qs = [list(q) for q in queues if q]
        idx = [0] * len(qs)
        live = True
        while live:
            live = False
            for i, q in enumerate(qs):
                if idx[i] < len(q):
                    kind, a, b, r, w = q[idx[i]]
                    idx[i] += 1
                    live = True
                    if kind == "op":
                        self.op(a, b, r, w)
                    elif kind == "dma":
                        self.dma(a, b[0], b[1], r, w)
                    else:
                        self.mm(b, r, w)

    def pipeline(self, recs, nst):
        parts = []
        for q in recs:
            n = len(q)
            cuts = [(n * k) // nst for k in range(nst + 1)]
            parts.append([q[cuts[k]:cuts[k + 1]] for k in range(nst)])
        for step in range(len(recs) + nst - 1):
            qs = []
            for k in range(nst - 1, -1, -1):
                if 0 <= step - k < len(recs):
                    qs.append(parts[step - k][k])
            self.replay_rr(qs)

    def op(self, e, fn, reads=(), writes=()):
        if self.rec is not None:
            self.rec.append(("op", e, fn, list(reads), list(writes)))
            return
        ins = self._emit(e, fn, reads, writes)
        self.ccnt[e] += 1
        ins.then_inc(self.csem[e], 1)
        self._commit(e, (e, self.csem[e], self.ccnt[e]), reads, writes)

    def mm(self, fns, reads=(), writes=()):
        if self.rec is not None:
            self.rec.append(("mm", "pe", list(fns), list(reads), list(writes)))
            return
        ins = self._emit("pe", fns[0], reads, writes)
        for f in fns[1:]:
            ins = f()
        self.ccnt["pe"] += 1
        ins.then_inc(self.csem["pe"], 1)
        self._commit("pe", ("pe", self.csem["pe"], self.ccnt["pe"]), reads, writes)

    def dma(self, q, out, in_, reads=(), writes=()):
        if self.rec is not None:
            self.rec.append(("dma", q, (out, in_), list(reads), list(writes)))
            return
        i = self.dnext
        self.dnext = (self.dnext + 1) % self.NDS
        if self.dcnt[i] > 0:
            self._wait(q, (("d", i), self.dsem[i], self.dcnt[i]))
        for w in self._collect(q, reads, writes):
            self.engs[q].wait_ge(w[1], w[2])
            self._mark(q, w)
        self.dcnt[i] += 16
        self.engs[q].dma_start(out=out, in_=in_).then_inc(self.dsem[i], 16)
        self._commit(q, (("d", i), self.dsem[i], self.dcnt[i]), reads, writes)

    def barrier(self):
        for e in self.engs:
            for f in self.csem:
                if self.ccnt[f] > 0:
                    self._wait(e, (f, self.csem[f], self.ccnt[f]))
            for i in range(self.NDS):
                if self.dcnt[i] > 0:
                    self._wait(e, (("d", i), self.dsem[i], self.dcnt[i]))

    def finish(self):
        for i in range(self.NDS):
            if self.dcnt[i] > 0:
                self._wait("sp", (("d", i), self.dsem[i], self.dcnt[i]))


def build():
    nc = bass.Bass("TRN2", target_bir_lowering=False)
    V, A, G, PE = nc.vector, nc.scalar, nc.gpsimd, nc.tensor
    S = Sched(nc)

    def din(n, s):
        return nc.dram_tensor(n, s, F32, kind="ExternalInput").ap()

    def dout(n, s):
        return nc.dram_tensor(n, s, F32, kind="ExternalOutput").ap()

    xloc = din("xloc", [NLOC, D]); xs = din("xs", [4, D]); st = din("st", [4, 4, 64, 128])
    ck = din("ck", [4, 2048, 512]); cv = din("cv", [4, 2048, 512])
    w_in = din("w_in", [D, DIN]); w2 = din("w2", [16, 256]); ba = din("ba", [1, 256])
    gng = din("gng", [1, 128]); w_out = din("w_out", [D, D]); lng = din("lng", [1, D]); lnb = din("lnb", [1, D])
    biasT = din("biasT", [8, 128, NOFF * 128]); logc = din("logc", [128, NOFF * 128])
    bdec = din("bdec", [3, 128, 8]); b0 = din("b0", [1, 8]); cm = din("cm", [128, 1])
    biasF = din("biasF", [8, 128, 256])
    vscr = nc.dram_tensor("vscr", [NLOC, 256], BF16).ap()
    tri_d = din("tri", [128, 128]); ident_d = din("ident", [128, 128]); bd_d = din("bd", [8, 512])
    oh4_d = din("oh4", [128, 16])

    y_o = dout("y_o", [NOWN, D]); sp_o = dout("sp_o", [4, 64, 128])
    kw_o = dout("kw_o", [NOWN, 512]); vw_o = dout("vw_o", [NOWN, 512])
    ys_o = dout("ys_o", [4, D]); ss_o = dout("ss_o", [4, 4, 64, 128])
    kn_o = dout("kn_o", [4, 512]); vn_o = dout("vn_o", [4, 512])
    scr_q = nc.dram_tensor("scr_q", [4, 512], F32).ap()

    bkA = nc.alloc_psum_tensor("bkA", [128, 2048], F32).ap()
    bk = [bkA[:, i * 512:(i + 1) * 512] for i in range(4)]
    bk += [nc.alloc_psum_tensor("bk%d" % i, [128, 512], F32).ap() for i in (4, 5)]
    bkb = [Buf("bk%d" % i, True) for i in range(6)]

    top = ExitStack()

    def sbt(es, name, shape, dt=F32):
        return es.enter_context(nc.sbuf_tensor("s_" + name, shape, dt)).ap(), Buf(name)

    bxT = [Buf("xT%d" % i) for i in range(NB)]
    xTd = nc.dram_tensor("xTd", [8, 128, 8 * 512], BF16).ap()
    bxTd = Buf("xTd")
    mixT, _ = sbt(top, "mixT", [128, 8, NOWN], BF16)
    bmix = [[Buf("mix%d_%d" % (c, t)) for t in range(16)] for c in range(8)]
    xsT, bxsT = sbt(top, "xsT", [128, 8, 4], BF16)
    mixTs, bmixs = sbt(top, "mixTs", [128, 8, 4], BF16)
    identf, bidf = sbt(top, "identf", [128, 128])
    ident, bid = sbt(top, "ident", [128, 128], BF16)
    tri, btri = sbt(top, "tri", [128, 128])
    trin, btrin = sbt(top, "trin", [128, 128])
    mask4, bmask4 = sbt(top, "mask4", [128, 512], BF16)
    gngb, bgng = sbt(top, "gngb", [128, 512])
    cmt, bcm = sbt(top, "cmt", [128, 1])
    oh4, boh4 = sbt(top, "oh4", [128, 16])
    ones, bones = sbt(top, "ones", [128, 8])
    hsB, bhsB = sbt(top, "hsB", [4, 4, 512])
    GTss, bGTss = sbt(top, "GTss", [128, 4, 4])

    S.dma("sp", identf, ident_d, writes=[bidf])
    S.dma("sp", tri, tri_d, writes=[btri])
    S.dma("sp", cmt, cm, writes=[bcm])
    S.dma("sp", oh4, oh4_d, writes=[boh4])
    for h in range(4):
        S.dma("sp", gngb[:, h * 128:(h + 1) * 128], gng.partition_broadcast(128), writes=[bgng])
    S.op("pool", lambda: G.tensor_copy(out=ident, in_=identf), reads=[bidf], writes=[bid])
    S.op("pool", lambda: G.tensor_scalar(out=trin, in0=tri, scalar1=-1.0 / 16.0, scalar2=None, op0=ALU.mult),
         reads=[btri], writes=[btrin])
    for h in range(4):
        S.op("pool", lambda h=h: G.tensor_copy(out=mask4[:, h * 128:(h + 1) * 128], in_=tri), reads=[btri], writes=[bmask4])
    S.op("pool", lambda: G.memset(ones, 1.0), writes=[bones])

    try:
        esx = ExitStack()
        xT, _ = sbt(esx, "xT", [128, 8, NLOC], BF16)
        pb = [esx.enter_context(nc.psum_tensor("pb%d" % i, [128, 1024], BF16)).ap() for i in range(2)]
        pbb = [Buf("pb%d" % i, True) for i in range(2)]
        def xTr(sb):
            return bxT[sb * 4:(sb + 1) * 4]

        with ExitStack() as es:
            WA, bWA = sbt(es, "WA", [128, 8, 1552], BF16)
            w2f, bw2f = sbt(es, "w2f", [32, 256])
            W2a, bW2a = sbt(es, "W2a", [32, 256], BF16)
            S.op("pool", lambda: G.memset(w2f, 0.0), writes=[bw2f])
            S.dma("sp", w2f[0:16, :], w2, writes=[bw2f])
            S.dma("sp", w2f[16:17, :], ba, writes=[bw2f])
            S.op("pool", lambda: G.tensor_copy(out=W2a, in_=w2f), reads=[bw2f], writes=[bW2a])
            zaug = [sbt(es, "zaug%d" % i, [32, 512], BF16) for i in range(2)]
            for z, bz in zaug:
                S.op("pool", lambda z=z: G.memset(z, 1.0), writes=[bz])
            qTr2 = [sbt(es, "qTr%d" % i, [128, 2, 512]) for i in range(2)]
            Sst = [sbt(es, "Sst%d" % p, [128, 128]) for p in range(2)]
            Sbf = [sbt(es, "Sbf%d" % p, [128, 128], BF16) for p in range(2)]
            stmp, bstmp = sbt(es, "stmp", [128, 128])
            for p in range(2):
                S.op("pool", lambda p=p: G.memset(Sst[p][0], 0.0), writes=[Sst[p][1]])
                S.op("pool", lambda p=p: G.memset(Sbf[p][0], 0.0), writes=[Sbf[p][1]])
            DB = 3
            vA = [sbt(es, "vA%d" % i, [128, 512], BF16) for i in range(DB)]
            eL = [sbt(es, "eL%d" % i, [128, 256]) for i in range(DB)]
            ebm = [sbt(es, "ebm%d" % i, [128, 256]) for i in range(DB)]
            kt = [sbt(es, "kt%d" % i, [128, 256], BF16) for i in range(DB)]
            ebT = [sbt(es, "ebT%d" % i, [128, 2, 128]) for i in range(DB)]
            ela = [sbt(es, "ela%d" % i, [128, 2]) for i in range(DB)]
            qtT = [sbt(es, "qtT%d" % i, [128, 2, 128], BF16) for i in range(DB)]
            ktT = [sbt(es, "ktT%d" % i, [128, 2, 128], BF16) for i in range(DB)]
            ge = [sbt(es, "ge%d" % i, [128, 512]) for i in range(DB)]
            GS = [sbt(es, "GS%d" % i, [128, 512]) for i in range(DB)]
            Am = [sbt(es, "Am%d" % i, [128, 512], BF16) for i in range(DB)]
            sq = ge
            kAr = [sbt(es, "kAr%d" % i, [128, 256]) for i in range(DB)]
            ssq = [sbt(es, "ssq%d" % i, [128, 4]) for i in range(DB)]
            mixa = [sbt(es, "mixa%d" % i, [128, 512], BF16) for i in range(DB)]

            es_p1 = ExitStack()
            NXF, NXB = 2, 2
            xf = [sbt(es_p1, "xf%d" % i, [128, D]) for i in range(NXF)]
            xb = [sbt(es_p1, "xb%d" % i, [128, D], BF16) for i in range(NXB)]

            S.dma("sp", xf[0][0][0:4, :], xs, writes=[xf[0][1]])
            S.op("pool", lambda: G.tensor_copy(out=xb[0][0][0:4, :], in_=xf[0][0][0:4, :]), reads=[xf[0][1]], writes=[xb[0][1]])
            S.mm([lambda c=c: PE.transpose(pb[0][:, c * 4:(c + 1) * 4], xb[0][0][0:4, c * 128:(c + 1) * 128], ident[0:4, 0:4])
                  for c in range(8)], reads=[xb[0][1], bid], writes=[pbb[0]])
            S.op("dve", lambda: V.tensor_copy(out=xsT, in_=pb[0][:, 0:32].rearrange("p (c t) -> p c t", c=8)),
                 reads=[pbb[0]], writes=[bxsT])
            bstg = [Buf("wstg%d" % c) for c in range(8)]
            for c in range(8):
                stg_c = xT[:, c, 0:3104].bitcast(F32)
                S.dma("sp", stg_c, w_in[c * 128:(c + 1) * 128, 0:1552], writes=[bstg[c]])
            for c in range(8):
                stg_c = xT[:, c, 0:3104].bitcast(F32)
                e = ("act", "dve", "pool", "act", "dve", "act", "dve", "pool")[c]
                if e == "pool":
                    S.op(e, lambda c=c, stg_c=stg_c: G.tensor_copy(out=WA[:, c, :], in_=stg_c), reads=[bstg[c]] + bxT[0:25], writes=[bWA])
                elif e == "act":
                    S.op(e, lambda c=c, stg_c=stg_c: A.copy(out=WA[:, c, :], in_=stg_c), reads=[bstg[c]] + bxT[0:25], writes=[bWA])
                else:
                    S.op(e, lambda c=c, stg_c=stg_c: V.tensor_copy(out=WA[:, c, :], in_=stg_c), reads=[bstg[c]] + bxT[0:25], writes=[bWA])

            def p1_block(blk):
                f, bf_ = xf[blk % NXF]
                b_, bb_ = xb[blk % NXB]
                S.dma("sp", f, xloc[blk * 128:(blk + 1) * 128, :], writes=[bf_])
                if blk % 2 == 0:
                    S.op("act", lambda: A.copy(out=b_, in_=f), reads=[bf_], writes=[bb_])
                else:
                    S.op("dve", lambda: V.tensor_copy(out=b_, in_=f), reads=[bf_], writes=[bb_])
                pt, bpt = pb[blk % 2], pbb[blk % 2]
                S.mm([lambda c=c: PE.transpose(pt[:, c * 128:(c + 1) * 128], b_[:, c * 128:(c + 1) * 128], ident)
                      for c in range(8)], reads=[bb_, bid], writes=[bpt])
                src = pt.rearrange("p (c t) -> p c t", c=8)
                dst = xT[:, :, blk * 128:(blk + 1) * 128]
                if blk % 2 == 0:
                    S.op("dve", lambda: V.tensor_copy(out=dst, in_=src), reads=[bpt], writes=[bxT[blk]])
                else:
                    S.op("act", lambda: A.copy(out=dst, in_=src), reads=[bpt], writes=[bxT[blk]])
                if blk % 4 == 3:
                    sb_ = blk // 4
                    S.dma("sp", xTd[sb_].rearrange("p (c t) -> p c t", c=8), xT[:, :, sb_ * 512:(sb_ + 1) * 512],
                          reads=bxT[sb_ * 4:sb_ * 4 + 4], writes=[bxTd])

            accn = [0]

            def acc():
                i = accn[0] % 2
                accn[0] += 1
                return bk[i], bkb[i]

            def silu_gate(P, gps, bgps, ge_t, bge, GS_t, bGS):
                S.op("act", lambda: A.activation(out=ge_t[0:P, :], in_=gps[0:P, :], func=AF.Exp, scale=-1.0),
                     reads=[bgps], writes=[bge])
                S.op("act", lambda: A.activation(out=ge_t[0:P, :], in_=ge_t[0:P, :], func=AF.Ln, bias=1.0, scale=1.0),
                     reads=[bge], writes=[bge])
                S.op("act", lambda: A.activation(out=ge_t[0:P, :], in_=ge_t[0:P, :], func=AF.Exp, scale=-1.0),
                     reads=[bge], writes=[bge])
                S.op("dve", lambda: V.tensor_tensor(out=GS_t[0:P, :], in0=gps[0:P, :], in1=ge_t[0:P, :], op=ALU.mult),
                     reads=[bgps, bge], writes=[bGS])
                S.op("pool", lambda: G.tensor_tensor(out=GS_t[0:P, :], in0=GS_t[0:P, :], in1=gngb[0:P, :], op=ALU.mult),
                     reads=[bGS, bgng], writes=[bGS])

            def rms_mix(P, ops, bops, sq_t, bsq, ssq_t, bssq, GS_t, bGS, mixa_t, bmixa):
                S.op("act", lambda: A.copy(out=sq_t[0:P, :], in_=ops[0:P, :]), reads=[bops], writes=[bsq])
                S.op("dve", lambda: V.tensor_tensor(out=sq_t[0:P, :], in0=sq_t[0:P, :], in1=sq_t[0:P, :], op=ALU.mult),
                     reads=[bsq], writes=[bsq])
                S.op("dve", lambda: V.tensor_reduce(out=ssq_t[0:P, :], in_=sq_t[0:P, :].rearrange("p (h v) -> p h v", h=4),
                                                    axis=AX.X, op=ALU.add), reads=[bsq], writes=[bssq])
                S.op("act", lambda: A.activation(out=ssq_t[0:P, :], in_=ssq_t[0:P, :], func=AF.Ln, bias=RMS_EPS, scale=1.0 / 128.0),
                     reads=[bssq], writes=[bssq])
                S.op("act", lambda: A.activation(out=ssq_t[0:P, :], in_=ssq_t[0:P, :], func=AF.Exp, scale=-0.5),
                     reads=[bssq], writes=[bssq])
                for h in range(4):
                    S.op("dve", lambda h=h: V.scalar_tensor_tensor(
                        out=mixa_t[0:P, h * 128:(h + 1) * 128], in0=ops[0:P, h * 128:(h + 1) * 128],
                        scalar=ssq_t[0:P, h:h + 1], in1=GS_t[0:P, h * 128:(h + 1) * 128], op0=ALU.mult, op1=ALU.mult),
                        reads=[bops, bssq, bGS], writes=[bmixa])


            def stage_s(sb):
                tok = slice(sb * 512, (sb + 1) * 512)
                ps, bps = acc()
                S.mm([lambda c=c: PE.matmul(ps[0:16, :], lhsT=WA[:, c, 1024:1040], rhs=xT[:, c, tok],
                                            start=(c == 0), stop=(c == 7)) for c in range(8)],
                     reads=[bWA] + xTr(sb), writes=[bps])
                za, bza = zaug[sb % 2]
                S.op("act", lambda: A.copy(out=za[0:16, :], in_=ps[0:16, :]), reads=[bps], writes=[bza])
                if sb >= 4:
                    for (dst, bdst, col0) in ((qTr2[sb % 2][0], qTr2[sb % 2][1], 0),):
                        for dc in range(2):
                            ps2, bps2 = acc()
                            S.mm([lambda c=c, ps2=ps2, dc=dc, col0=col0: PE.matmul(
                                ps2, lhsT=WA[:, c, col0 + dc * 128:col0 + (dc + 1) * 128], rhs=xT[:, c, tok],
                                start=(c == 0), stop=(c == 7)) for c in range(8)],
                                reads=[bWA] + xTr(sb), writes=[bps2])
                            if dc == 0:
                                S.op("act", lambda ps2=ps2, dst=dst, dc=dc: A.copy(out=dst[:, dc, :], in_=ps2), reads=[bps2], writes=[bdst])
                            else:
                                S.op("dve", lambda ps2=ps2, dst=dst, dc=dc: V.tensor_copy(out=dst[:, dc, :], in_=ps2), reads=[bps2], writes=[bdst])

            def stage_a(blk):
                own = blk >= 16
                i3 = blk % DB
                bt = slice(blk * 128, (blk + 1) * 128)
                g1, bg1 = acc()
                S.mm([lambda c=c: PE.matmul(g1, lhsT=xT[:, c, bt], rhs=WA[:, c, 256:768],
                                            start=(c == 0), stop=(c == 7)) for c in range(8)],
                     reads=[bWA, bxT[blk]], writes=[bg1])
                vAt, bvA = vA[i3]
                kAt, bkA = kAr[i3]
                S.op("act", lambda: A.copy(out=vAt[:, 0:256], in_=g1[:, 256:512]), reads=[bg1], writes=[bvA])
                S.op("dve", lambda: V.tensor_copy(out=kAt, in_=g1[:, 0:256]), reads=[bg1], writes=[bkA])
                g2, bg2 = acc()
                S.mm([lambda c=c: PE.matmul(g2[:, 0:256], lhsT=xT[:, c, bt], rhs=WA[:, c, 768:1024],
                                            start=(c == 0), stop=(c == 7)) for c in range(8)],
                     reads=[bWA, bxT[blk]], writes=[bg2])
                S.op("act", lambda: A.copy(out=vAt[:, 256:512], in_=g2[:, 0:256]), reads=[bg2], writes=[bvA])
                if own:
                    g3, bg3 = acc()
                    S.mm([lambda c=c: PE.matmul(g3, lhsT=xT[:, c, bt], rhs=WA[:, c, 1040:1552],
                                                start=(c == 0), stop=(c == 7)) for c in range(8)],
                         reads=[bWA, bxT[blk]], writes=[bg3])
                    silu_gate(128, g3, bg3, ge[i3][0], ge[i3][1], GS[i3][0], GS[i3][1])

            def stage_b1(blk):
                own = blk >= 16
                i3 = blk % DB
                sb, j = blk // 4, blk % 4
                za, bza = zaug[sb % 2]
                ub, bub = bk[2], bkb[2]
                S.mm([lambda: PE.matmul(ub[:, 0:256], lhsT=za[:, j * 128:(j + 1) * 128], rhs=W2a, start=True, stop=True)],
                     reads=[bza, bW2a], writes=[bub])
                eLt, beL = eL[i3]
                S.op("act", lambda: A.activation(out=eLt, in_=ub[:, 0:256], func=AF.Exp, scale=-1.0), reads=[bub], writes=[beL])
                S.op("act", lambda: A.activation(out=eLt, in_=eLt, func=AF.Ln, bias=1.0, scale=1.0), reads=[beL], writes=[beL])
                fns = [lambda: PE.matmul(ub[:, 256:512], lhsT=trin, rhs=eLt, start=True, stop=True)]
                if own:
                    fns += [lambda dc=dc: PE.matmul(ub[:, dc * 128:(dc + 1) * 128], lhsT=eLt[:, dc * 128:(dc + 1) * 128], rhs=trin,
                                                    start=True, stop=True) for dc in range(2)]
                else:
                    fns += [lambda dc=dc: PE.matmul(ub[:, dc:dc + 1], lhsT=eLt[:, dc * 128:(dc + 1) * 128], rhs=trin[:, 127:128],
                                                    start=True, stop=True) for dc in range(2)]
                S.mm(fns, reads=[btrin, beL], writes=[bub])
                ebmt, bebm = ebm[i3]
                S.op("act", lambda: A.activation(out=ebmt, in_=ub[:, 256:512], func=AF.Exp, scale=-1.0), reads=[bub], writes=[bebm])
                ktt, bkt = kt[i3]
                S.op("dve", lambda: V.tensor_tensor(out=ktt, in0=kAr[i3][0], in1=ebmt, op=ALU.mult),
                     reads=[kAr[i3][1], bebm], writes=[bkt])
                elat, bela = ela[i3]
                if own:
                    ebTt, bebT = ebT[i3]
                    bT3 = ub[:, 0:256].rearrange("p (c t) -> p c t", c=2)
                    S.op("act", lambda: A.activation(out=ebTt, in_=bT3, func=AF.Exp), reads=[bub], writes=[bebT])
                    S.op("dve", lambda: V.tensor_copy(out=elat, in_=ebTt[:, :, 127]), reads=[bebT], writes=[bela])
                    qtTt, bqtT = qtT[i3]
                    ktTt, bktT = ktT[i3]
                    qTr, bqTr = qTr2[sb % 2]
                    S.op("dve", lambda: V.scalar_tensor_tensor(out=qtTt, in0=qTr[:, :, j * 128:(j + 1) * 128], scalar=0.125,
                                                               in1=ebTt, op0=ALU.mult, op1=ALU.mult),
                         reads=[bqTr, bebT], writes=[bqtT])
                    ptk, bptk = pb[(blk + 1) % 2], pbb[(blk + 1) % 2]
                    S.mm([lambda p=p: PE.transpose(ptk[:, 768 + p * 128:768 + (p + 1) * 128], ktt[:, p * 128:(p + 1) * 128], ident)
                          for p in range(2)], reads=[bkt, bid], writes=[bptk])
                    S.op("act", lambda: A.copy(out=ktTt, in_=ptk[:, 768:1024].rearrange("p (c t) -> p c t", c=2)),
                         reads=[bptk], writes=[bktT])
                else:
                    S.op("act", lambda: A.activation(out=elat, in_=ub[:, 0:2], func=AF.Exp), reads=[bub], writes=[bela])

            def stage_b2(blk):
                own = blk >= 16
                i3 = blk % DB
                vAt, bvA = vA[i3]
                ktt, bkt = kt[i3]
                elat, bela = ela[i3]
                apsb = ((bk[5], bkb[5]), (bk[3], bkb[3]))
                if own:
                    qtTt, bqtT = qtT[i3]
                    ktTt, bktT = ktT[i3]
                    S.mm([lambda h=h: PE.matmul(apsb[h % 2][0][:, (h // 2) * 128:(h // 2 + 1) * 128],
                                                lhsT=ktTt[(h % 2) * 64:(h % 2) * 64 + 64, h // 2, :],
                                                rhs=qtTt[(h % 2) * 64:(h % 2) * 64 + 64, h // 2, :],
                                                start=True, stop=True) for h in range(4)],
                         reads=[bktT, bqtT], writes=[bkb[5], bkb[3]])
                    Amt, bAm = Am[i3]
                    Am4 = Amt.rearrange("p (c r t) -> p c r t", c=2, r=2)
                    for par in range(2):
                        S.op("dve", lambda par=par: V.tensor_tensor(
                            out=Am4[:, :, par, :], in0=apsb[par][0][:, 0:256].rearrange("p (c t) -> p c t", c=2),
                            in1=mask4[:, 0:256].rearrange("p (c t) -> p c t", c=2), op=ALU.mult),
                            reads=[apsb[par][1], bmask4], writes=[bAm])
                    ops, bops = bk[4], bkb[4]
                    fns = []
                    for h in range(4):
                        fns.append(lambda h=h: PE.matmul(ops[:, h * 128:(h + 1) * 128], lhsT=Amt[:, h * 128:(h + 1) * 128],
                                                         rhs=vAt[:, h * 128:(h + 1) * 128], start=True, stop=False))
                        fns.append(lambda h=h: PE.matmul(ops[:, h * 128:(h + 1) * 128],
                                                         lhsT=qtTt[(h % 2) * 64:(h % 2) * 64 + 64, h // 2, :],
                                                         rhs=Sbf[h // 2][0][(h % 2) * 64:(h % 2) * 64 + 64, :],
                                                         start=False, stop=True))
                    S.mm(fns, reads=[bAm, bvA, bqtT, Sbf[0][1], Sbf[1][1]], writes=[bops])
                    rms_mix(128, ops, bops, sq[i3][0], sq[i3][1], ssq[i3][0], ssq[i3][1], GS[i3][0], GS[i3][1],
                            mixa[i3][0], mixa[i3][1])
                    pt, bpt = pb[blk % 2], pbb[blk % 2]
                    mx = mixa[i3][0]
                    S.mm([lambda h=h: PE.transpose(pt[:, h * 128:(h + 1) * 128], mx[:, h * 128:(h + 1) * 128], ident)
                          for h in range(4)], reads=[mixa[i3][1], bid], writes=[bpt])
                    ob = blk - 16
                    S.op("act", lambda: A.copy(out=mixT[:, 0:4, ob * 128:(ob + 1) * 128],
                                               in_=pt[:, 0:512].rearrange("p (c t) -> p c t", c=4)),
                         reads=[bpt], writes=[bmix[c][ob] for c in range(4)])
                for p in range(2):
                    kvb, bkvb = apsb[p]
                    S.mm([lambda p=p, kvb=kvb: PE.matmul(kvb[:, 256:512], lhsT=ktt[:, p * 128:(p + 1) * 128],
                                                         rhs=vAt[:, p * 256:(p + 1) * 256], start=True, stop=True)],
                         reads=[bkt, bvA], writes=[bkvb])
                for p in range(2):
                    kvb, bkvb = apsb[p]
                    St, bSt = Sst[p]
                    for hh in range(2):
                        r = slice(hh * 64, hh * 64 + 64)
                        S.op("dve", lambda p=p, hh=hh, r=r, St=St, kvb=kvb: V.tensor_tensor(
                            out=stmp[r, :], in0=kvb[r, 256 + hh * 128:256 + (hh + 1) * 128], in1=St[r, :], op=ALU.add),
                            reads=[bkvb, bSt], writes=[bstmp])
                    S.op("dve", lambda p=p, St=St: V.tensor_scalar(out=St, in0=stmp, scalar1=elat[:, p:p + 1], scalar2=None,
                                                                   op0=ALU.mult), reads=[bstmp, bela], writes=[bSt])
                    S.op("act", lambda p=p, St=St: A.copy(out=Sbf[p][0], in_=St), reads=[bSt], writes=[Sbf[p][1]])

            LA = 4
            for blk in range(LA):
                S.replay_rr([S.record(p1_block, blk)])
            for step in range(NB + 2):
                qs = []
                if 0 <= step - 2 < NB:
                    qs.append(S.record(stage_b2, step - 2))
                if 0 <= step - 1 < NB:
                    qs.append(S.record(stage_b1, step - 1))
                if step < NB:
                    def sa(step=step):
                        if step % 4 == 0:
                            stage_s(step // 4)
                        stage_a(step)
                    qs.append(S.record(sa))
                if step + LA < NB:
                    qs.append(S.record(p1_block, step + LA))
                S.replay_rr(qs)
            S.barrier()
            es_p1.close()
            for p in range(2):
                S.dma("sp", sp_o[2 * p:2 * p + 2].rearrange("h d v -> (h d) v"), Sst[p][0], reads=[Sst[p][1]])

            hsA, bhsA = sbt(es, "hsA", [4, 1552])
            for gi, (c0, c1) in enumerate(((0, 512), (512, 1024), (1024, 1536), (1536, 1552))):
                ps, bps = acc()
                S.mm([lambda c=c, ps=ps, c0=c0, c1=c1: PE.matmul(ps[0:4, 0:c1 - c0], lhsT=xsT[:, c, :], rhs=WA[:, c, c0:c1],
                                                                 start=(c == 0), stop=(c == 7)) for c in range(8)],
                     reads=[bWA, bxsT], writes=[bps])
                S.op("act", lambda ps=ps, c0=c0, c1=c1: A.copy(out=hsA[:, c0:c1], in_=ps[0:4, 0:c1 - c0]), reads=[bps], writes=[bhsA])
            ps, bps = acc()
            S.mm([lambda c=c, ps=ps: PE.matmul(ps[0:16, 0:4], lhsT=WA[:, c, 1024:1040], rhs=xsT[:, c, :],
                                               start=(c == 0), stop=(c == 7)) for c in range(8)], reads=[bWA, bxsT], writes=[bps])
            za, bza = zaug[0]
            S.op("act", lambda: A.copy(out=za[0:16, 0:4], in_=ps[0:16, 0:4]), reads=[bps], writes=[bza])
            ups, bups = bk[3], bkb[3]
            S.mm([lambda dc=dc: PE.matmul(ups[:, dc * 4:(dc + 1) * 4], lhsT=W2a[:, dc * 128:(dc + 1) * 128], rhs=za[:, 0:4],
                                          start=True, stop=True) for dc in range(2)], reads=[bza, bW2a], writes=[bups])
            aTs, baTs = sbt(es, "aTs", [128, 8])
            S.op("act", lambda: A.activation(out=aTs, in_=ups[:, 0:8], func=AF.Exp, scale=-1.0), reads=[bups], writes=[baTs])
            S.op("act", lambda: A.activation(out=aTs, in_=aTs, func=AF.Ln, bias=1.0, scale=1.0), reads=[baTs], writes=[baTs])
            S.op("act", lambda: A.activation(out=aTs, in_=aTs, func=AF.Exp, scale=-1.0 / 16.0), reads=[baTs], writes=[baTs])
            qsT, bqsT = sbt(es, "qsT", [128, 2, 4])
            ps, bps = acc()
            for dc in range(2):
                S.mm([lambda c=c, dc=dc, ps=ps: PE.matmul(ps[:, dc * 4:(dc + 1) * 4], lhsT=WA[:, c, dc * 128:(dc + 1) * 128],
                                                          rhs=xsT[:, c, :], start=(c == 0), stop=(c == 7)) for c in range(8)],
                     reads=[bWA, bxsT], writes=[bps])
            S.op("dve", lambda: V.tensor_scalar(out=qsT, in0=ps[:, 0:8].rearrange("p (c t) -> p c t", c=2), scalar1=0.125,
                                                scalar2=None, op0=ALU.mult), reads=[bps], writes=[bqsT])
            Sn = [sbt(es, "Sn%d" % b, [128, 2, 128]) for b in range(4)]
            kmask, bkmask = sbt(es, "kmask", [4, 256])
            qm = [sbt(es, "qm%d" % b, [128, 2, 4]) for b in range(4)]
            for b in range(4):
                Snt, bSn = Sn[b]
                for p in range(2):
                    S.dma("sp", Snt[:, p, :], st[b, 2 * p:2 * p + 2].rearrange("h d v -> (h d) v"), writes=[bSn])
                S.op("dve", lambda b=b: V.tensor_scalar(out=kmask, in0=hsA[:, 256:512], scalar1=identf[0:4, b:b + 1],
                                                        scalar2=None, op0=ALU.mult), reads=[bhsA, bidf], writes=[bkmask])
                kv, bkv = bk[2], bkb[2]
                S.mm([lambda p=p: PE.matmul(kv[:, p * 256:(p + 1) * 256], lhsT=kmask[:, p * 128:(p + 1) * 128],
                                            rhs=hsA[:, 512 + p * 256:512 + (p + 1) * 256], start=True, stop=True) for p in range(2)],
                     reads=[bkmask, bhsA], writes=[bkv])
                for p in range(2):
                    for hh in range(2):
                        r = slice(hh * 64, hh * 64 + 64)
                        S.op("dve", lambda b=b, p=p, hh=hh, r=r, Snt=Snt: V.scalar_tensor_tensor(
                            out=Snt[r, p, :], in0=Snt[r, p, :], scalar=aTs[r, p * 4 + b:p * 4 + b + 1],
                            in1=kv[r, p * 256 + hh * 128:p * 256 + (hh + 1) * 128], op0=ALU.mult, op1=ALU.add),
                            reads=[bSn, baTs, bkv], writes=[bSn])
                    S.dma("sp", ss_o[b, 2 * p:2 * p + 2].rearrange("h d v -> (h d) v"), Snt[:, p, :], reads=[bSn])
                qmt, bqm = qm[b]
                for dc in range(2):
                    S.op("dve", lambda b=b, dc=dc, qmt=qmt: V.tensor_tensor(out=qmt[:, dc, :], in0=qsT[:, dc, :],
                                                                            in1=oh4[:, 4 * b:4 * b + 4], op=ALU.mult),
                         reads=[bqsT, boh4], writes=[bqm])
            ospb = ((bk[4], bkb[4]), (bk[5], bkb[5]))
            fns = []
            for h in range(4):
                r = slice((h % 2) * 64, (h % 2) * 64 + 64)
                for b in range(4):
                    fns.append(lambda h=h, b=b, r=r: PE.matmul(ospb[h % 2][0][0:4, (h // 2) * 128:(h // 2 + 1) * 128],
                                                               lhsT=qm[b][0][r, h // 2, :],
                                                               rhs=Sn[b][0][r, h // 2, :], start=(b == 0), stop=(b == 3)))
            S.mm(fns, reads=[qm[b][1] for b in range(4)] + [Sn[b][1] for b in range(4)], writes=[bkb[4], bkb[5]])
            osp, bosp = ge[1][0][0:4, :], ge[1][1]
            osp4 = osp.rearrange("p (c r v) -> p c r v", c=2, r=2)
            for par in range(2):
                S.op("act", lambda par=par: A.copy(out=osp4[:, :, par, :], in_=ospb[par][0][0:4, 0:256].rearrange("p (c v) -> p c v", c=2)),
                     reads=[ospb[par][1]], writes=[bosp])
            gsb, bgsb = GS[1][0][0:4, :], GS[1][1]
            S.op("act", lambda: A.copy(out=gsb, in_=hsA[:, 1040:1552]), reads=[bhsA], writes=[bgsb])
            silu_gate(4, gsb, bgsb, ge[0][0], ge[0][1], GS[0][0], GS[0][1])
            rms_mix(4, osp, bosp, sq[0][0], sq[0][1], ssq[0][0], ssq[0][1], GS[0][0], GS[0][1], mixa[0][0], mixa[0][1])
            pt, bpt = pb[0], pbb[0]
            mx = mixa[0][0]
            S.mm([lambda h=h: PE.transpose(pt[:, h * 4:(h + 1) * 4], mx[0:4, h * 128:(h + 1) * 128], ident[0:4, 0:4])
                  for h in range(4)], reads=[mixa[0][1], bid], writes=[bpt])
            S.op("act", lambda: A.copy(out=mixTs[:, 0:4, :], in_=pt[:, 0:16].rearrange("p (c t) -> p c t", c=4)),
                 reads=[bpt], writes=[bmixs])
            S.barrier()
        esx.close()
        esy = ExitStack()
        bk67 = esy.enter_context(nc.psum_tensor("bk67", [128, 1024], F32)).ap()
        for i in (6, 7):
            bk.append(bk67[:, (i - 6) * 512:(i - 5) * 512])
            bkb.append(Buf("bk%d" % i, True))

        with ExitStack() as es:
            logct, blogc = sbt(es, "logct", [128, NOFF * 128])
            S.dma("sp", logct, logc, writes=[blogc])
            stg = [sbt(es, "stg%d" % i, [128, 8, 128]) for i in range(4)]
            bst, bbst = sbt(es, "bst", [128, NOFF * 128])
            NXS = 3
            xst = [sbt(es, "xst%d" % i, [128, 8, 512], BF16) for i in range(NXS)]
            PT = []
            for par in range(1):
                t = {}
                t["Wp"], t["bWp"] = sbt(es, "Wp%d" % par, [128, 8, 4, 128], BF16)
                t["E"], t["bE"] = sbt(es, "E%d" % par, [128, 2, NOFF * 128], BF16)
                t["KT"], _ = sbt(es, "KT%d" % par, [128, NLOC], BF16)
                t["bKT"] = [Buf("KT%d_%d" % (par, i)) for i in range(8)]
                t["QT"], _ = sbt(es, "QT%d" % par, [128, NOWN], BF16)
                t["bQT"] = [Buf("QT%d_%d" % (par, i)) for i in range(4)]
                t["GTs"], _ = sbt(es, "GTs%d" % par, [128, NOWN], BF16)
                t["bGT"] = [Buf("GT%d_%d" % (par, i)) for i in range(4)]
                t["Vaug"], _ = sbt(es, "Vaug%d" % par, [128, NB, 2, 128], BF16)
                t["bVa"] = [Buf("Va%d_%d" % (par, i)) for i in range(8)]
                PT.append(t)
            NSB = 6
            SBK = (0, 1, 2, 3, 6, 7)
            NPB = 3
            pex = [sbt(es, "pex%d" % i, [128, 2, 512], BF16) for i in range(NPB)]
            pmk = [sbt(es, "pmk%d" % i, [128, 2, 512], BF16) for i in range(NPB)]

            def pairview(gi):
                k0 = SBK[gi % NSB]
                base = bk67 if k0 == 6 else bkA[:, k0 * 512:(k0 + 2) * 512]
                return base.rearrange("p (h n) -> p h n", h=2)
            vout = [sbt(es, "vout%d" % i, [128, 4, 128]) for i in range(2)]
            dtmp = [sbt(es, "dtmp%d" % i, [128, 1024]) for i in range(2)]
            Vcp, bVcp = sbt(es, "Vcp", [128, 32, 256], BF16)
            bvscr = Buf("vscr")
            Oacc = [sbt(es, "Oacc%d" % i, [128, NOWN]) for i in range(2)]
            EF4, bEF4 = sbt(es, "EF4", [128, 2, 512], BF16)
            bfs, bbfs = sbt(es, "bfs", [128, 256])
            for par in range(1):
                t = PT[par]
                S.op("pool", lambda t=t: G.memset(t["Vaug"], 1.0), writes=t["bVa"])
                S.op("act", lambda t=t: A.mul(out=t["Vaug"][:, 0:16], in_=t["Vaug"][:, 0:16], mul=cmt[:, 0:1]),
                     reads=t["bVa"][0:4] + [bcm], writes=t["bVa"][0:4])
            grp_n = [0]
            xs_n = [0]
            ipn = [0]

            def ipbank():
                i = (6, 7, 4, 5)[ipn[0] % 4]
                ipn[0] += 1
                return bk[i], bkb[i]

            def xs_load(sb):
                xt_, bxt_ = xst[xs_n[0] % NXS]
                xs_n[0] += 1
                S.dma("sp", xt_, xTd[sb].rearrange("p (c t) -> p c t", c=8), reads=[bxTd], writes=[bxt_])
                return xt_, bxt_

            def setup(hp):
                t = PT[0]
                Wp, bWp, E, bE = t["Wp"], t["bWp"], t["E"], t["bE"]
                KT, bKT, QT, bQT, GTs, bGT, Vaug, bVa = t["KT"], t["bKT"], t["QT"], t["bQT"], t["GTs"], t["bGT"], t["Vaug"], t["bVa"]
                for g in range(4):
                    sg, bsg = stg[g]
                    if g % 2 == 0:
                        S.op("act", lambda g=g, sg=sg: A.copy(out=Wp[:, :, g, :], in_=sg), reads=[bsg], writes=[bWp])
                    else:
                        S.op("dve", lambda g=g, sg=sg: V.tensor_copy(out=Wp[:, :, g, :], in_=sg), reads=[bsg], writes=[bWp])
                tiles = list(pre_x[hp])
                for hh in range(2):
                    S.dma("sp", bst, biasT[2 * hp + hh], writes=[bbst])
                    S.op("dve", lambda: V.tensor_tensor(out=bst, in0=bst, in1=logct, op=ALU.add), reads=[bbst, blogc], writes=[bbst])
                    S.op("act", lambda hh=hh: A.activation(out=E[:, hh, :], in_=bst, func=AF.Exp), reads=[bbst], writes=[bE])
                def do_sb(sb, xt_, bxt_):
                    tok = slice(sb * 512, (sb + 1) * 512)
                    ps, bps = ipbank()
                    S.mm([lambda c=c, ps=ps: PE.matmul(ps, lhsT=Wp[:, c, 1, :], rhs=xt_[:, c, :], start=(c == 0), stop=(c == 7))
                          for c in range(8)], reads=[bWp, bxt_], writes=[bps])
                    S.op("act", lambda ps=ps: A.copy(out=KT[:, tok], in_=ps), reads=[bps], writes=[bKT[sb]])
                    if sb < 4:
                        ps, bps = ipbank()
                        fns = []
                        for j in range(4):
                            for c in range(8):
                                fns.append(lambda c=c, j=j, ps=ps: PE.matmul(
                                    ps[:, j * 128:(j + 1) * 128], lhsT=xt_[:, c, j * 128:(j + 1) * 128], rhs=Wp[:, c, 2, :],
                                    start=(c == 0), stop=(c == 7)))
                        S.mm(fns, reads=[bWp, bxt_], writes=[bps])
                        ps3 = ps.rearrange("p (j n) -> p j n", j=4)
                        S.op("act", lambda ps3=ps3: A.mul(out=Vaug[:, sb * 4:(sb + 1) * 4, 0, 0:64], in_=ps3[:, :, 0:64], mul=cmt[:, 0:1]),
                             reads=[bps, bcm], writes=[bVa[sb]])
                        S.op("dve", lambda ps3=ps3: V.tensor_scalar(out=Vaug[:, sb * 4:(sb + 1) * 4, 1, 64:128], in0=ps3[:, :, 64:128],
                                                                  scalar1=cmt[:, 0:1], scalar2=None, op0=ALU.mult),
                             reads=[bps, bcm], writes=[bVa[sb]])
                    else:
                        so = sb - 4
                        otok = slice(so * 512, (so + 1) * 512)
                        r0 = so * 512
                        vo, bvo = vout[0]
                        ko, bko = vout[1]
                        for g2 in range(2):
                            ps, bps = ipbank()
                            fns = []
                            for jj in range(2):
                                j = 2 * g2 + jj
                                for c in range(8):
                                    fns.append(lambda c=c, j=j, jj=jj, ps=ps: PE.matmul(
                                        ps[:, jj * 256:(jj + 1) * 256], lhsT=xt_[:, c, j * 128:(j + 1) * 128],
                                        rhs=Wp[:, c, 1:3, :].rearrange("p g n -> p (g n)"), start=(c == 0), stop=(c == 7)))
                            S.mm(fns, reads=[bWp, bxt_], writes=[bps])
                            ps4 = ps.rearrange("p (j g n) -> p j g n", j=2, g=2)
                            b0 = sb * 4 + 2 * g2
                            S.op("act", lambda ps4=ps4, b0=b0: A.copy(out=Vaug[:, b0:b0 + 2, 0, 0:64], in_=ps4[:, :, 1, 0:64]),
                                 reads=[bps], writes=[bVa[sb]])
                            S.op("dve", lambda ps4=ps4, b0=b0: V.tensor_copy(out=Vaug[:, b0:b0 + 2, 1, 64:128], in_=ps4[:, :, 1, 64:128]),
                                 reads=[bps], writes=[bVa[sb]])
                            S.op("dve", lambda ps4=ps4, g2=g2: V.tensor_copy(out=vo[:, 2 * g2:2 * g2 + 2, :], in_=ps4[:, :, 1, :]),
                                 reads=[bps], writes=[bvo])
                            S.op("act", lambda ps4=ps4, g2=g2: A.copy(out=ko[:, 2 * g2:2 * g2 + 2, :], in_=ps4[:, :, 0, :]),
                                 reads=[bps], writes=[bko])
                        S.dma("sp", vw_o[r0:r0 + 512, hp * 128:(hp + 1) * 128].rearrange("(j p) n -> p j n", p=128), vo, reads=[bvo])
                        S.dma("sp", kw_o[r0:r0 + 512, hp * 128:(hp + 1) * 128].rearrange("(j p) n -> p j n", p=128), ko, reads=[bko])
                        ps, bps = ipbank()
                        S.mm([lambda c=c, ps=ps: PE.matmul(ps, lhsT=Wp[:, c, 0, :], rhs=xt_[:, c, :], start=(c == 0), stop=(c == 7))
                              for c in range(8)], reads=[bWp, bxt_], writes=[bps])
                        S.op("dve", lambda ps=ps: V.tensor_scalar(out=QT[:, otok], in0=ps, scalar1=0.125, scalar2=None, op0=ALU.mult),
                             reads=[bps], writes=[bQT[so]])
                        ps, bps = ipbank()
                        S.mm([lambda c=c, ps=ps: PE.matmul(ps, lhsT=Wp[:, c, 3, :], rhs=xt_[:, c, :], start=(c == 0), stop=(c == 7))
                              for c in range(8)], reads=[bWp, bxt_], writes=[bps])
                        S.op("act", lambda ps=ps: A.activation(out=GTs[:, otok], in_=ps, func=AF.Silu), reads=[bps], writes=[bGT[so]])
                for sb in range(8):
                    if sb + 2 < 8:
                        tiles.append(xs_load(sb + 2))
                    do_sb(sb, tiles[sb][0], tiles[sb][1])
                for hh in range(2):
                    S.dma("sp", bfs, biasF[2 * hp + hh], writes=[bbfs])
                    S.op("act", lambda hh=hh: A.activation(out=EF4[:, hh, 0:256], in_=bfs, func=AF.Exp), reads=[bbfs], writes=[bEF4])
                    S.op("dve", lambda hh=hh: V.tensor_copy(out=EF4[:, hh, 256:512], in_=EF4[:, hh, 0:256]), reads=[bEF4], writes=[bEF4])
                S.dma("sp", vscr.rearrange("(b p) n -> p b n", p=128), Vaug.rearrange("p b h n -> p b (h n)"), reads=bVa, writes=[bvscr])
                S.dma("sp", Vcp.rearrange("i (f c) n -> i f c n", f=2), vscr.rearrange("(f i c) n -> i f c n", f=2, i=128, c=16),
                      reads=[bvscr], writes=[bVcp])
                ps, bps = ipbank()
                S.mm([lambda c=c, ps=ps: PE.matmul(ps[0:4, :], lhsT=xsT[:, c, :], rhs=Wp[:, c, :, :].rearrange("p g n -> p (g n)"),
                                                   start=(c == 0), stop=(c == 7)) for c in range(8)], reads=[bWp, bxsT], writes=[bps])
                S.op("act", lambda ps=ps: A.copy(out=hsB[:, hp, :], in_=ps[0:4, :]), reads=[bps], writes=[bhsB])
                ps, bps = ipbank()
                S.mm([lambda c=c, ps=ps: PE.matmul(ps[:, 0:4], lhsT=Wp[:, c, 3, :], rhs=xsT[:, c, :], start=(c == 0), stop=(c == 7))
                      for c in range(8)], reads=[bWp, bxsT], writes=[bps])
                S.op("act", lambda ps=ps: A.activation(out=GTss[:, hp, :], in_=ps[:, 0:4], func=AF.Silu), reads=[bps], writes=[bGTss])

            def mainloop(hp):
                t = PT[0]
                E, bE = t["E"], t["bE"]
                KT, bKT, QT, bQT, GTs, bGT, Vaug, bVa = t["KT"], t["bKT"], t["QT"], t["bQT"], t["GTs"], t["bGT"], t["Vaug"], t["bVa"]
                items = []
                for qg in range(4):
                    kbl = list(range(12 + 4 * qg, 20 + 4 * qg))
                    kbl.remove(16 + 4 * qg)
                    kbl = [16 + 4 * qg] + kbl
                    for idx, kb in enumerate(kbl):
                        qa = max(4 * qg, kb - 16)
                        qz = min(4 * qg + 3, kb - 12)
                        for hh in range(2):
                            items.append(dict(qg=qg, hh=hh, kb=kb, qa=qa, n=qz - qa + 1, oa=16 + qa - kb,
                                              first=(idx == 0), last=(idx == len(kbl) - 1), it=hh, gi=grp_n[0]))
                            grp_n[0] += 1

                def emit_qk(w):
                    hh, kb, qa, n, gi = w["hh"], w["kb"], w["qa"], w["n"], w["gi"]
                    r = slice(hh * 64, hh * 64 + 64)
                    sp_, bsp = bk[SBK[gi % NSB]], bkb[SBK[gi % NSB]]
                    S.mm([lambda: PE.matmul(sp_[:, 0:n * 128], lhsT=KT[r, kb * 128:(kb + 1) * 128],
                                            rhs=QT[r, qa * 128:(qa + n) * 128], start=True, stop=True)],
                         reads=[bKT[kb // 4]] + [bQT[q // 4] for q in range(qa, qa + n)], writes=[bsp])

                def emit_rest2(w0, w1):
                    qg, kb, qa, n, gi = w0["qg"], w0["kb"], w0["qa"], w0["n"], w0["gi"]
                    nw = n * 128
                    pv = pairview(gi)
                    bsp0, bsp1 = bkb[SBK[gi % NSB]], bkb[SBK[(gi + 1) % NSB]]
                    px, bpx = pex[(gi // 2) % NPB]
                    pm, bpm = pmk[(gi // 2) % NPB]
                    S.op("act", lambda: A.activation(out=px[:, :, 0:nw], in_=pv[:, :, 0:nw], func=AF.Exp),
                         reads=[bsp0, bsp1], writes=[bpx])
                    esl = E[:, :, w0["oa"] * 128:(w0["oa"] + n) * 128]
                    S.op("dve", lambda: V.tensor_tensor(out=pm[:, :, 0:nw], in0=px[:, :, 0:nw], in1=esl, op=ALU.mult),
                         reads=[bpx, bE], writes=[bpm])
                    c0 = (qa - 4 * qg) * 128
                    for hh in range(2):
                        ot, bot = bk[4 + hh], bkb[4 + hh]
                        S.mm([lambda hh=hh, ot=ot: PE.matmul(ot[:, c0:c0 + nw], lhsT=Vaug[:, kb, hh, :], rhs=pm[:, hh, 0:nw],
                                                             start=w0["first"], stop=w0["last"])],
                             reads=[bpm, bVa[kb // 4]], writes=[bot])
                    if w0["last"]:
                        S.op("act", lambda: A.copy(out=Oacc[0][0][:, qg * 512:(qg + 1) * 512], in_=bk[4]), reads=[bkb[4]], writes=[Oacc[0][1]])
                        S.op("dve", lambda: V.tensor_copy(out=Oacc[1][0][:, qg * 512:(qg + 1) * 512], in_=bk[5]), reads=[bkb[5]], writes=[Oacc[1][1]])

                for i in range(4):
                    emit_qk(items[i])
                for i in range(0, len(items), 2):
                    if i + 4 < len(items):
                        emit_qk(items[i + 4])
                        emit_qk(items[i + 5])
                    emit_rest2(items[i], items[i + 1])

                def far_qk(w):
                    hh, cg, gi = w["hh"], w["cg"], w["gi"]
                    r = slice(hh * 64, hh * 64 + 64)
                    sp_, bsp = bk[SBK[gi % NSB]], bkb[SBK[gi % NSB]]
                    fns = []
                    for k in range(2):
                        cc = 2 * cg + k
                        for f in range(2):
                            fns.append(lambda cc=cc, f=f, k=k: PE.matmul(
                                sp_[:, (2 * k + f) * 128:(2 * k + f + 1) * 128], lhsT=KT[r, 2048 * f + cc:2048 * (f + 1):16],
                                rhs=QT[r, cc:2048:16], start=True, stop=True))
                    S.mm(fns, reads=bKT + bQT, writes=[bsp])

                def far_rest2(w0, w1):
                    cg, gi = w0["cg"], w0["gi"]
                    pv = pairview(gi)
                    bsp0, bsp1 = bkb[SBK[gi % NSB]], bkb[SBK[(gi + 1) % NSB]]
                    px, bpx = pex[(gi // 2) % NPB]
                    pm, bpm = pmk[(gi // 2) % NPB]
                    S.op("act", lambda: A.activation(out=px, in_=pv, func=AF.Exp), reads=[bsp0, bsp1], writes=[bpx])
                    S.op("dve", lambda: V.tensor_tensor(out=pm, in0=px, in1=EF4, op=ALU.mult), reads=[bpx, bEF4], writes=[bpm])
                    for hh in range(2):
                        of, bof = bk[4 + hh], bkb[4 + hh]
                        fns = []
                        for k in range(2):
                            cc = 2 * cg + k
                            slot = (cg % 2) * 2 + k
                            for f in range(2):
                                fns.append(lambda cc=cc, f=f, k=k, slot=slot, hh=hh, of=of: PE.matmul(
                                    of[:, slot * 128:(slot + 1) * 128], lhsT=Vcp[:, f * 16 + cc, hh * 128:(hh + 1) * 128],
                                    rhs=pm[:, hh, (2 * k + f) * 128:(2 * k + f + 1) * 128], start=(f == 0), stop=(f == 1)))
                        S.mm(fns, reads=[bpm, bVcp], writes=[bof])
                        if cg % 2 == 1:
                            oa_, boa = Oacc[hh]
                            c0 = 2 * cg - 2
                            dst = oa_.rearrange("p (j c) -> p c j", c=16)[:, c0:c0 + 4, :]
                            S.op("dve", lambda dst=dst, of=of: V.tensor_tensor(out=dst, in0=of.rearrange("p (c j) -> p c j", c=4), in1=dst, op=ALU.add),
                                 reads=[bof, boa], writes=[boa])

                fitems = []
                for cg in range(8):
                    for hh in range(2):
                        fitems.append(dict(hh=hh, cg=cg, gi=grp_n[0]))
                        grp_n[0] += 1
                for i in range(4):
                    far_qk(fitems[i])
                for i in range(0, len(fitems), 2):
                    if i + 4 < len(fitems):
                        far_qk(fitems[i + 4])
                        far_qk(fitems[i + 5])
                    far_rest2(fitems[i], fitems[i + 1])

                def fin(half):
                    cs = slice(half * 1024, (half + 1) * 1024)
                    dt_, bdt = dtmp[half]
                    o0, bo0 = Oacc[0]
                    o1, bo1 = Oacc[1]
                    S.op("dve", lambda: V.tensor_copy(out=dt_[0:64, :], in_=o0[64:128, cs]), reads=[bo0], writes=[bdt])
                    S.op("dve", lambda: V.tensor_copy(out=dt_[64:128, :], in_=o1[0:64, cs]), reads=[bo1], writes=[bdt])
                    S.op("act", lambda: A.activation(out=dt_, in_=dt_, func=AF.Ln), reads=[bdt], writes=[bdt])
                    S.op("act", lambda: A.activation(out=dt_, in_=dt_, func=AF.Exp, scale=-1.0), reads=[bdt], writes=[bdt])
                    S.op("dve", lambda: V.tensor_tensor(out=dt_[0:64, :], in0=o0[0:64, cs], in1=dt_[0:64, :], op=ALU.mult),
                         reads=[bo0, bdt], writes=[bdt])
                    S.op("dve", lambda: V.tensor_tensor(out=dt_[64:128, :], in0=o1[64:128, cs], in1=dt_[64:128, :], op=ALU.mult),
                         reads=[bo1, bdt], writes=[bdt])
                    S.op("dve", lambda: V.tensor_tensor(out=mixT[:, 4 + hp, cs], in0=dt_, in1=GTs[:, cs], op=ALU.mult),
                         reads=[bdt] + bGT[2 * half:2 * half + 2], writes=[bmix[4 + hp][q] for q in range(8 * half, 8 * half + 8)])

                for half in range(2):
                    fin(half)

            def wdma(hp):
                for g in range(4):
                    col0 = 1552 + 512 * g + 128 * hp
                    S.dma("sp", stg[g][0], w_in[:, col0:col0 + 128].rearrange("(c p) n -> p c n", p=128), writes=[stg[g][1]])

            pre_x = {}
            wdma(0)
            pre_x[0] = [xs_load(0), xs_load(1)]
            for hp in range(4):
                S.replay_rr([S.record(setup, hp)])
                if hp + 1 < 4:
                    wdma(hp + 1)
                    pre_x[hp + 1] = [xs_load(0), xs_load(1)]
                S.replay_rr([S.record(mainloop, hp)])
            S.barrier()

        with ExitStack() as es:
            S.rec = []
            qkv, bqkv = sbt(es, "qkv", [4, 3, 512])
            for t in range(3):
                S.op("dve", lambda t=t: V.tensor_copy(out=qkv[:, t, :].rearrange("p (h n) -> p h n", h=4), in_=hsB[:, :, t * 128:(t + 1) * 128]),
                     reads=[bhsB], writes=[bqkv])
            S.dma("sp", kn_o, qkv[:, 1, :], reads=[bqkv])
            S.dma("sp", vn_o, qkv[:, 2, :], reads=[bqkv])
            bscr = Buf("scr")
            S.dma("sp", scr_q, qkv[:, 0, :], reads=[bqkv], writes=[bscr])
            bdt_, bbd = sbt(es, "bdt", [8, 512])
            S.dma("sp", bdt_, bd_d, writes=[bbd])
            bdect, bbdec = sbt(es, "bdect", [128, 3, 8])
            for s in range(3):
                S.dma("sp", bdect[:, s, :], bdec[s], writes=[bbdec])
            b0t, bb0 = sbt(es, "b0t", [4, 8])
            S.dma("sp", b0t, b0.partition_broadcast(4), writes=[bb0])
            pr0, bpr0 = sbt(es, "pr0", [4, 512])
            l0, bl0 = sbt(es, "l0", [4, 8])
            S.op("dve", lambda: V.tensor_tensor(out=pr0, in0=qkv[:, 0, :], in1=qkv[:, 1, :], op=ALU.mult), reads=[bqkv], writes=[bpr0])
            S.op("dve", lambda: V.tensor_reduce(out=l0, in_=pr0.rearrange("p (h d) -> p h d", h=8), axis=AX.X, op=ALU.add),
                 reads=[bpr0], writes=[bl0])
            S.op("dve", lambda: V.scalar_tensor_tensor(out=l0, in0=l0, scalar=0.125, in1=b0t, op0=ALU.mult, op1=ALU.add),
                 reads=[bl0, bb0], writes=[bl0])
            S.op("act", lambda: A.activation(out=l0, in_=l0, func=AF.Exp), reads=[bl0], writes=[bl0])
            S.op("dve", lambda: V.tensor_scalar(out=l0, in0=l0, scalar1=3.0, scalar2=None, op0=ALU.mult), reads=[bl0], writes=[bl0])
            Kg = [sbt(es, "Kg%d" % i, [128, 512]) for i in range(3)]
            Vg = [sbt(es, "Vg%d" % i, [128, 512]) for i in range(3)]
            qbc, bqbc = sbt(es, "qbc", [128, 512])
            prod, bprod = sbt(es, "prod", [128, 512])
            lg, blg = sbt(es, "lg", [128, 3, 8])
            p0m, bp0m = sbt(es, "p0m", [4, 8])
            rd, brd = sbt(es, "rd", [8, 1])
            om, bom = sbt(es, "om", [8, 512])
            obp, bobp = bk[6], bkb[6]
            for b in range(4):
                S.dma("sp", qbc, scr_q[b:b + 1, :].partition_broadcast(128), reads=[bscr], writes=[bqbc])
                for si, sg_ in enumerate((1, 4, 16)):
                    r0 = 2048 - 128 * sg_
                    S.dma("sp", Kg[si][0], ck[b, r0:2048:sg_, :], writes=[Kg[si][1]])
                    S.dma("sp", Vg[si][0], cv[b, r0:2048:sg_, :], writes=[Vg[si][1]])
                    S.op("dve", lambda si=si: V.tensor_tensor(out=prod, in0=Kg[si][0], in1=qbc, op=ALU.mult),
                         reads=[Kg[si][1], bqbc], writes=[bprod])
                    S.op("dve", lambda si=si: V.tensor_reduce(out=lg[:, si, :], in_=prod.rearrange("p (h d) -> p h d", h=8), axis=AX.X, op=ALU.add),
                         reads=[bprod], writes=[blg])
                S.op("dve", lambda: V.scalar_tensor_tensor(out=lg, in0=lg, scalar=0.125, in1=bdect, op0=ALU.mult, op1=ALU.add),
                     reads=[blg, bbdec], writes=[blg])
                S.op("act", lambda: A.activation(out=lg, in_=lg, func=AF.Exp), reads=[blg], writes=[blg])
                S.op("dve", lambda b=b: V.tensor_scalar(out=p0m, in0=l0, scalar1=identf[0:4, b:b + 1], scalar2=None, op0=ALU.mult),
                     reads=[bl0, bidf], writes=[bp0m])
                ops_, bops_ = bk[7], bkb[7]
                fns = [lambda si=si: PE.matmul(ops_[0:8, :], lhsT=lg[:, si, :], rhs=Vg[si][0], start=(si == 0), stop=False) for si in range(3)]
                fns.append(lambda: PE.matmul(ops_[0:8, :], lhsT=p0m, rhs=qkv[:, 2, :], start=False, stop=True))
                S.mm(fns, reads=[blg, bp0m, bqkv] + [Vg[si][1] for si in range(3)], writes=[bops_])
                dps, bdps = bk[6][:, 32:64], bkb[6]
                fns = [lambda si=si: PE.matmul(dps[0:8, 0:1], lhsT=lg[:, si, :], rhs=ones[:, 0:1], start=(si == 0), stop=False) for si in range(3)]
                fns.append(lambda: PE.matmul(dps[0:8, 0:1], lhsT=p0m, rhs=ones[0:4, 0:1], start=False, stop=True))
                S.mm(fns, reads=[blg, bp0m, bones], writes=[bdps])
                S.op("dve", lambda: V.reciprocal(out=rd, in_=dps[0:8, 0:1]), reads=[bdps], writes=[brd])
                S.op("dve", lambda: V.scalar_tensor_tensor(out=om, in0=ops_[0:8, :], scalar=rd[:, 0:1], in1=bdt_, op0=ALU.mult, op1=ALU.mult),
                     reads=[bops_, brd, bbd], writes=[bom])
                S.mm([lambda ch=ch, b=b: PE.matmul(obp[:, ch * 4 + b:ch * 4 + b + 1], lhsT=om[:, ch * 128:(ch + 1) * 128], rhs=ones[0:8, 0:1],
                                                   start=True, stop=True) for ch in range(4)], reads=[bom, bones], writes=[bobp])
            S.op("dve", lambda: V.tensor_tensor(out=mixTs[:, 4:8, :], in0=obp[:, 0:16].rearrange("p (c t) -> p c t", c=4), in1=GTss, op=ALU.mult),
                 reads=[bobp, bGTss], writes=[bmixs])
            rec3b, S.rec = S.rec, None

            Wo, bWo = sbt(es, "Wo", [128, 8, D], BF16)
            stgO = [sbt(es, "stgO%d" % i, [128, D]) for i in range(2)]
            for c in range(8):
                sg, bsg = stgO[c % 2]
                S.dma("sp", sg, w_out[c * 128:(c + 1) * 128, :], writes=[bsg])
                if c % 2 == 0:
                    S.op("dve", lambda c=c, sg=sg: V.tensor_copy(out=Wo[:, c, :], in_=sg), reads=[bsg], writes=[bWo])
                else:
                    S.op("act", lambda c=c, sg=sg: A.copy(out=Wo[:, c, :], in_=sg), reads=[bsg], writes=[bWo])
            lngt, blng = sbt(es, "lngt", [128, D])
            lnbt, blnb = sbt(es, "lnbt", [128, D])
            S.dma("sp", lngt, lng.partition_broadcast(128), writes=[blng])
            S.dma("sp", lnbt, lnb.partition_broadcast(128), writes=[blnb])
            NXR = 5
            xr = [sbt(es, "xr%d" % i, [128, D]) for i in range(NXR)]
            rr = [sbt(es, "rr%d" % i, [128, D]) for i in range(3)]
            sqrs = [sbt(es, "sqr%d" % i, [128, D]) for i in range(2)]
            stt = [sbt(es, "stt%d" % i, [128, 8]) for i in range(3)]
            aI, baI = sbt(es, "aI", [128, 128])
            S.op("pool", lambda: G.tensor_scalar(out=aI, in0=identf, scalar1=ALPHA, scalar2=None, op0=ALU.mult), reads=[bidf], writes=[baI])

            def x_load(P, it, x_src):
                xt, bxt = xr[it % NXR]
                S.dma("sp", xt[0:P, :], x_src, writes=[bxt])

            def out_block(P, it, lhs_fn, mix_bufs, x_src, y_dst):
                xt, bxt = xr[it % NXR]
                rt, brt = rr[it % 3]
                s8, bs8 = stt[it % 3]
                sqr, bsqr = sqrs[it % 2]
                yps = []
                for half in range(2):
                    yp, byp = bk[2 * (it % 3) + half], bkb[2 * (it % 3) + half]
                    yps.append((yp, byp))
                    fns = [lambda c=c, yp=yp, half=half: PE.matmul(yp[0:P, :], lhsT=lhs_fn(c), rhs=Wo[:, c, half * 512:(half + 1) * 512],
                                                                   start=(c == 0), stop=False) for c in range(8)]
                    fns.append(lambda yp=yp, half=half: PE.matmul(yp[0:P, :], lhsT=aI[0:P, 0:P], rhs=xt[0:P, half * 512:(half + 1) * 512],
                                                                  start=False, stop=True))
                    S.mm(fns, reads=[bWo, bxt, baI] + mix_bufs, writes=[byp])
                for half in range(2):
                    yp, byp = yps[half]
                    S.op("act", lambda yp=yp, half=half: A.activation(out=sqr[0:P, 0:512], in_=yp[0:P, :], func=AF.Copy,
                                                                      accum_out=s8[0:P, half:half + 1]),
                         reads=[byp], writes=[bsqr, bs8])
                    S.op("act", lambda yp=yp, half=half: A.activation(out=sqr[0:P, 512:1024], in_=yp[0:P, :], func=AF.Square,
                                                                      accum_out=s8[0:P, 2 + half:3 + half]),
                         reads=[byp], writes=[bsqr, bs8])
                S.op("dve", lambda: V.tensor_reduce(out=s8[0:P, 4:6], in_=s8[0:P, 0:4].rearrange("p (a b) -> p a b", a=2), axis=AX.X, op=ALU.add),
                     reads=[bs8], writes=[bs8])
                S.op("dve", lambda: V.tensor_scalar(out=s8[0:P, 4:6], in0=s8[0:P, 4:6], scalar1=1.0 / D, scalar2=None, op0=ALU.mult),
                     reads=[bs8], writes=[bs8])
                S.op("dve", lambda: V.tensor_tensor(out=s8[0:P, 6:7], in0=s8[0:P, 4:5], in1=s8[0:P, 4:5], op=ALU.mult), reads=[bs8], writes=[bs8])
                S.op("dve", lambda: V.tensor_tensor(out=s8[0:P, 6:7], in0=s8[0:P, 5:6], in1=s8[0:P, 6:7], op=ALU.subtract), reads=[bs8], writes=[bs8])
                S.op("act", lambda: A.activation(out=s8[0:P, 7:8], in_=s8[0:P, 6:7], func=AF.Ln, bias=LN_EPS, scale=1.0), reads=[bs8], writes=[bs8])
                S.op("act", lambda: A.activation(out=s8[0:P, 7:8], in_=s8[0:P, 7:8], func=AF.Exp, scale=-0.5), reads=[bs8], writes=[bs8])
                S.op("dve", lambda: V.scalar_tensor_tensor(out=s8[0:P, 6:7], in0=s8[0:P, 4:5], scalar=-1.0, in1=s8[0:P, 7:8],
                                                           op0=ALU.mult, op1=ALU.mult), reads=[bs8], writes=[bs8])
                for half in range(2):
                    yp, byp = yps[half]
                    S.op("act", lambda yp=yp, half=half: A.activation(
                        out=rt[0:P, half * 512:(half + 1) * 512], in_=yp[0:P, :], func=AF.Identity,
                        bias=s8[0:P, 6:7], scale=s8[0:P, 7:8]), reads=[byp, bs8], writes=[brt])
                S.op("dve", lambda: V.tensor_tensor(out=rt[0:P, :], in0=rt[0:P, :], in1=lngt[0:P, :], op=ALU.mult), reads=[brt, blng], writes=[brt])
                S.op("pool", lambda: G.tensor_tensor(out=rt[0:P, :], in0=rt[0:P, :], in1=lnbt[0:P, :], op=ALU.add), reads=[brt, blnb], writes=[brt])
                S.dma("sp", y_dst, rt[0:P, :], reads=[brt])

            def xsrc(tb):
                return xloc[NOWN + tb * 128:NOWN + (tb + 1) * 128, :]

            for tb in range(NXR - 1):
                x_load(128, tb, xsrc(tb))
            recs = []
            for tb in range(16):
                recs.append(S.record(out_block, 128, tb, (lambda c, tb=tb: mixT[:, c, tb * 128:(tb + 1) * 128]),
                                     [bmix[c][tb] for c in range(8)], None, y_o[tb * 128:(tb + 1) * 128, :]))
            parts = []
            for q in recs:
                n = len(q)
                parts.append([q[(n * k) // 3:(n * (k + 1)) // 3] for k in range(3)])
            for step in range(16 + 2):
                nb_ = step + NXR - 1
                if nb_ < 16:
                    x_load(128, nb_, xsrc(nb_))
                elif nb_ == 16:
                    x_load(4, 16, xs)
                n3 = len(rec3b)
                ch3 = rec3b[(n3 * step) // 18:(n3 * (step + 1)) // 18]
                S.replay_rr([parts[step - k][k] for k in (2, 1, 0) if 0 <= step - k < 16] + [ch3])
            out_block(4, 16, lambda c: mixTs[:, c, :], [bmixs], xs, ys_o)
            S.barrier()
    except _Stop:
        pass
    S.finish()
    pass
    return nc


def _t5_bucket(dist):
    dist = np.asarray(dist, np.int32)
    d = np.maximum(dist.astype(np.float32), np.float32(1.0))
    lb = np.float32(16) + np.log(d / np.float32(16)) / np.float32(math.log(2048 / 16)) * np.float32(16)
    lb = np.minimum(lb.astype(np.int32), 31)
    return np.where(dist < 16, dist, lb).astype(np.int64)


def _static_tables():
    i = np.arange(128)[:, None]
    j = np.arange(128)[None, :]
    dmat = np.stack([128 * o + j - i for o in range(NOFF)], 0)
    cnt = ((dmat >= 0) & (dmat <= 128)).astype(np.int64) \
        + ((dmat >= 0) & (dmat <= 512) & (dmat % 4 == 0)) + ((dmat >= 0) & (dmat <= 512) & (dmat % 16 == 0))
    valid = cnt > 0
    bucket = _t5_bucket(np.clip(dmat, 0, 2048))
    logc = np.where(valid, np.log(np.maximum(cnt, 1)), 0.0).astype(np.float32)
    return dmat, valid, bucket, logc


_NC = None


def kernel(x_prompt, x_sample, state_gla, cache_k_win, cache_v_win, w_in, w_alpha2, b_alpha,
           gla_norm_g, w_out, ln_g, ln_b, rel_bias):
    global _NC
    f = lambda a: np.ascontiguousarray(np.asarray(a), dtype=np.float32)
    x_prompt, x_sample, state_gla = f(x_prompt), f(x_sample), f(state_gla)
    cache_k_win, cache_v_win = f(cache_k_win), f(cache_v_win)
    w_in, w_alpha2, b_alpha, gla_norm_g = f(w_in)[0], f(w_alpha2)[0], f(b_alpha), f(gla_norm_g)
    w_out, ln_g, ln_b, rel_bias = f(w_out)[0], f(ln_g), f(ln_b), f(rel_bias)

    dmat, valid, bucket, logc = _static_tables()
    gat = rel_bias[bucket]
    gat = np.where(valid[..., None], gat, np.float32(NEG))
    biasT = np.ascontiguousarray(gat.transpose(3, 1, 0, 2).reshape(8, 128, NOFF * 128), dtype=np.float32)
    logc2 = np.ascontiguousarray(logc.transpose(1, 0, 2).reshape(128, NOFF * 128), dtype=np.float32)
    ii = np.arange(128)
    nA = 128 + ii[None, :] - ii[:, None]
    nB = ii[None, :] - ii[:, None]
    nf = np.concatenate([nA, nB], 1)
    vf = (nf >= 33) & (nf <= 128)
    gf = rel_bias[_t5_bucket(np.clip(nf, 0, 128) * 16)]
    biasF = np.ascontiguousarray(np.where(vf[..., None], gf, np.float32(NEG)).transpose(2, 0, 1), dtype=np.float32)
    bdec = np.stack([rel_bias[_t5_bucket((128 - ii) * s)] for s in (1, 4, 16)], 0).astype(np.float32)
    b0 = rel_bias[_t5_bucket(np.array([0]))].astype(np.float32)
    tri = (ii[:, None] <= ii[None, :]).astype(np.float32)
    ident = np.eye(128, dtype=np.float32)
    bd = np.zeros((8, 512), np.float32)
    for h in range(8):
        bd[h, h * 64:(h + 1) * 64] = 1.0
    oh4 = np.zeros((128, 16), np.float32)
    for b in range(4):
        oh4[:, 4 * b + b] = 1.0

    if _NC is None:
        _NC = build()
    in_maps = []
    for c in range(NCORES):
        b, half = c // 2, c % 2
        if half == 0:
            xloc = np.concatenate([np.zeros((NOWN, D), np.float32), x_prompt[b, :NOWN]], 0)
        else:
            xloc = x_prompt[b]
        sl = slice(4 * c, 4 * c + 4)
        in_maps.append({
            "xloc": np.ascontiguousarray(xloc), "xs": np.ascontiguousarray(x_sample[sl, 0]),
            "st": np.ascontiguousarray(state_gla[0, sl]),
            "ck": np.ascontiguousarray(cache_k_win[0, sl].reshape(4, 2048, 512)),
            "cv": np.ascontiguousarray(cache_v_win[0, sl].reshape(4, 2048, 512)),
            "w_in": w_in, "w2": w_alpha2, "ba": b_alpha, "gng": gla_norm_g, "w_out": w_out, "lng": ln_g, "lnb": ln_b,
            "biasT": biasT, "logc": logc2, "bdec": bdec, "b0": b0, "biasF": biasF,
            "cm": np.full((128, 1), float(half), np.float32),
            "tri": tri, "ident": ident, "bd": bd, "oh4": oh4,
        })
    res = run_bass_kernel_spmd(_NC, in_maps, core_ids=list(range(NCORES))).results
    y_p = np.stack([np.concatenate([res[2 * b]["y_o"], res[2 * b + 1]["y_o"]], 0) for b in range(4)], 0)
    y_s = np.concatenate([res[c]["ys_o"] for c in range(NCORES)], 0)[:, None, :]
    s_p = np.stack([res[2 * b + 1]["sp_o"] for b in range(4)], 0)[None]
    s_s = np.concatenate([res[c]["ss_o"] for c in range(NCORES)], 0)[None]
    k_p = np.stack([res[2 * b + 1]["kw_o"].reshape(NOWN, 8, 64) for b in range(4)], 0)[None]
    v_p = np.stack([res[2 * b + 1]["vw_o"].reshape(NOWN, 8, 64) for b in range(4)], 0)[None]
    k_n = np.concatenate([res[c]["kn_o"] for c in range(NCORES)], 0).reshape(32, 1, 8, 64)[None]
    v_n = np.concatenate([res[c]["vn_o"] for c in range(NCORES)], 0).reshape(32, 1, 8, 64)[None]
    return (y_p.astype(np.float32), y_s.astype(np.float32), s_p.astype(np.float32), s_s.astype(np.float32),
            k_p.astype(np.float32), v_p.astype(np.float32), k_n.astype(np.float32), v_n.astype(np.float32))
```
